# Optimizing a Trainium2 kernel written in Bass

```python
import math
import jax
import jax.numpy as jnp
from jax import lax
import numpy as np


D_MODEL = 1024
BATCH = 16
SEQ = 256
DEPTH = 2
DEC_BATCH = 4
DEC_SEQ = 1024
PAST_LEN = 512

F32 = jnp.float32
GRID_W = 64
EPS = 1e-6
S5_WIDTH = 512
S5_GROUP = 16
S5_GROUPS = S5_WIDTH // S5_GROUP
S5_STATE = 64
DA_HEADS = 4
DA_HEAD = 64
DA_VDIM = 2 * DA_HEAD
DA_WIDTH = DA_HEADS * DA_VDIM
Q_BLOCK = 128
ROPE_BASE = 10000.0
DN_HEADS = 4
DN_HEAD = 128
DN_WIDTH = DN_HEADS * DN_HEAD
DN_CONV = 5
DN_CHUNK = 64
N_BRANCH = 3
BRANCH_WIDTH = 512
IN_SIZES = (S5_WIDTH, S5_WIDTH,
            DA_HEADS * 2 * DA_HEAD, DA_HEADS * 2 * DA_HEAD, DA_WIDTH, DA_WIDTH,
            DN_WIDTH, DN_WIDTH, DN_WIDTH, DN_WIDTH, 2 * DN_HEADS, 2 * DN_HEADS,
            N_BRANCH * D_MODEL)
N_IN = sum(IN_SIZES)

kernel_name = 'hybrid_s5_diffattn_deltanet_prefix_flow_step'


def rmsnorm(x, g):
    xf = x.astype(F32)
    y = xf * lax.rsqrt(jnp.mean(xf * xf, axis=-1, keepdims=True) + EPS)
    return (y * g.astype(F32)).astype(x.dtype)


def l2norm(x):
    xf = x.astype(F32)
    return xf * lax.rsqrt(jnp.sum(xf * xf, axis=-1, keepdims=True) + EPS)


def split_in(proj):
    parts, start = [], 0
    for size in IN_SIZES:
        parts.append(proj[..., start:start + size])
        start += size
    return parts


def rope_2d(x, n_tokens):
    rows = n_tokens // GRID_W
    row = jnp.repeat(jnp.arange(rows), GRID_W).astype(F32)
    col = jnp.tile(jnp.arange(GRID_W), rows).astype(F32)
    half = x.shape[-1] // 2
    nf = half // 2
    inv = ROPE_BASE ** (-jnp.arange(nf, dtype=F32) / nf)

    def rot(xa, pos):
        ang = pos[:, None] * inv[None, :]
        cos = jnp.cos(ang)[None, :, None, None, :]
        sin = jnp.sin(ang)[None, :, None, None, :]
        x1, x2 = xa[..., :nf].astype(F32), xa[..., nf:].astype(F32)
        return jnp.concatenate([x1 * cos - x2 * sin, x1 * sin + x2 * cos], axis=-1)

    out = jnp.concatenate([rot(x[..., :half], row), rot(x[..., half:], col)], axis=-1)
    return out.astype(x.dtype)


def s5_scan(u, lam_re, lam_im, log_step, b_re, b_im, h0_re, h0_im):
    lam_re, lam_im = lam_re.astype(F32), lam_im.astype(F32)
    b_re, b_im = b_re.astype(F32), b_im.astype(F32)
    step = jnp.exp(log_step.astype(F32))[:, None]
    mag = jnp.exp(lam_re * step)
    ar, ai = mag * jnp.cos(lam_im * step), mag * jnp.sin(lam_im * step)
    den = lam_re * lam_re + lam_im * lam_im
    fr = ((ar - 1.0) * lam_re + ai * lam_im) / den
    fi = (ai * lam_re - (ar - 1.0) * lam_im) / den
    bbr = fr[..., None] * b_re - fi[..., None] * b_im
    bbi = fr[..., None] * b_im + fi[..., None] * b_re
    bu_re = jnp.einsum('btgi,gpi->btgp', u, bbr)
    bu_im = jnp.einsum('btgi,gpi->btgp', u, bbi)
    h0_re, h0_im = h0_re.astype(F32), h0_im.astype(F32)
    bu_re = bu_re.at[:, 0].add(ar * h0_re - ai * h0_im)
    bu_im = bu_im.at[:, 0].add(ar * h0_im + ai * h0_re)
    a_re = jnp.broadcast_to(ar, bu_re.shape)
    a_im = jnp.broadcast_to(ai, bu_im.shape)

    def combine(e1, e2):
        a1r, a1i, b1r, b1i = e1
        a2r, a2i, b2r, b2i = e2
        return (a2r * a1r - a2i * a1i, a2r * a1i + a2i * a1r,
                a2r * b1r - a2i * b1i + b2r, a2r * b1i + a2i * b1r + b2i)

    _, _, hr, hi = lax.associative_scan(combine, (a_re, a_im, bu_re, bu_im), axis=1)
    return hr, hi


def diff_attention(q, k, v, lam):
    b, s, h, _, d = q.shape
    nq = s // Q_BLOCK
    qb = jnp.moveaxis(q.reshape(b, nq, Q_BLOCK, h, 2, d), 1, 0)
    scale = d ** -0.5

    def one(qc):
        sc = jnp.einsum('bqhmd,bkhmd->bhmqk', qc, k).astype(F32) * scale
        a = jax.nn.softmax(sc, axis=-1)
        a = a[:, :, 0] - lam * a[:, :, 1]
        return jnp.einsum('bhqk,bkhe->bqhe', a.astype(v.dtype), v)

    o = lax.map(one, qb)
    return jnp.moveaxis(o, 0, 1).reshape(b, s, h, -1)


def short_conv(x, w):
    return lax.conv_general_dilated(
        x, w[:, None, :].astype(x.dtype), window_strides=(1,),
        padding=[(DN_CONV // 2, DN_CONV // 2)],
        dimension_numbers=('NWC', 'WIO', 'NWC'), feature_group_count=x.shape[-1])


def gated_delta_rule(q, k, v, beta, g, s0):
    b, t, h, _ = q.shape
    n = t // DN_CHUNK

    def chunk(a):
        a = a.astype(F32).reshape((b, n, DN_CHUNK) + a.shape[2:])
        return jnp.moveaxis(a, 3, 2)

    q, k, v, beta, g = chunk(q), chunk(k), chunk(v), chunk(beta), chunk(g)
    g = jnp.cumsum(g, axis=-1)
    i = jnp.arange(DN_CHUNK)
    incl = i[:, None] >= i[None, :]
    strict = i[:, None] > i[None, :]
    gdiff = g[..., :, None] - g[..., None, :]
    decay = jnp.where(incl, jnp.exp(jnp.where(incl, gdiff, 0.0)), 0.0)
    kb = k * beta[..., None]
    m = jnp.where(strict, jnp.einsum('bnhik,bnhjk->bnhij', kb, k) * decay, 0.0)
    eye = jnp.eye(DN_CHUNK, dtype=F32)
    tinv = lax.linalg.triangular_solve(eye + m, jnp.broadcast_to(eye, m.shape), left_side=True, lower=True)
    u = tinv @ (v * beta[..., None])
    w = tinv @ (kb * jnp.exp(g)[..., None])
    qk = jnp.einsum('bnhik,bnhjk->bnhij', q, k) * decay

    def step(s, xs):
        q_c, k_c, u_c, w_c, g_c, qk_c = xs
        v_new = u_c - w_c @ s
        o_c = (q_c * jnp.exp(g_c)[..., None]) @ s + qk_c @ v_new
        g_last = g_c[..., -1:]
        s = s * jnp.exp(g_last)[..., None] + jnp.einsum(
            'bhck,bhcv->bhkv', k_c * jnp.exp(g_last - g_c)[..., None], v_new)
        return s, o_c

    xs = tuple(jnp.moveaxis(a, 1, 0) for a in (q, k, u, w, g, qk))
    s, o = lax.scan(step, s0.astype(F32), xs)
    o = jnp.moveaxis(jnp.moveaxis(o, 0, 1), 2, 3).reshape(b, t, h, -1)
    return o, s


def mixer(h, lp, lam_init, ctx):
    b, t, _ = h.shape
    dt = h.dtype
    (u_a, z_a, q_b, k_b, v_b, z_b, q_c, k_c, v_c, z_c,
     beta_c, alpha_c, gates) = split_in(h @ lp['w_in'])
    if ctx is None:
        s5_h0_re = jnp.zeros((b, 2, S5_GROUPS, S5_STATE), F32)
        s5_h0_im = jnp.zeros((b, 2, S5_GROUPS, S5_STATE), F32)
        dn_s0 = jnp.zeros((b, 2, DN_HEADS, DN_HEAD, DN_HEAD), F32)
    else:
        k_ctx, v_ctx, s5_h0_re, s5_h0_im, dn_s0 = ctx

    u = u_a.astype(F32).reshape(b, t, S5_GROUPS, S5_GROUP)
    y_a = lp['s5_d'].astype(F32) * u_a.astype(F32)
    s5_fin_re, s5_fin_im = [], []
    for d in range(2):
        ud = u if d == 0 else u[:, ::-1]
        hr, hi = s5_scan(ud, lp['s5_lam_re'][d], lp['s5_lam_im'][d], lp['s5_log_step'][d],
                         lp['s5_b_re'][d], lp['s5_b_im'][d], s5_h0_re[:, d], s5_h0_im[:, d])
        yd = (jnp.einsum('btgp,gip->btgi', hr, lp['s5_c_re'][d].astype(F32))
              - jnp.einsum('btgp,gip->btgi', hi, lp['s5_c_im'][d].astype(F32)))
        if d == 1:
            yd = yd[:, ::-1]
        y_a = y_a + yd.reshape(b, t, S5_WIDTH)
        s5_fin_re.append(hr[:, -1])
        s5_fin_im.append(hi[:, -1])
    y_a = jax.nn.gelu(y_a)
    y_a = y_a * jax.nn.sigmoid(y_a @ lp['s5_w_glu'].astype(F32))
    out_a = (y_a * jax.nn.silu(z_a.astype(F32))).astype(dt)

    q = q_b.reshape(b, t, DA_HEADS, 2, DA_HEAD)
    k = k_b.reshape(b, t, DA_HEADS, 2, DA_HEAD)
    v = v_b.reshape(b, t, DA_HEADS, DA_VDIM)
    if ctx is None:
        k_all, v_all = k, v
    else:
        q = rope_2d(q, t)
        k_all = jnp.concatenate([rope_2d(k, t), k_ctx.astype(dt)], axis=1)
        v_all = jnp.concatenate([v, v_ctx.astype(dt)], axis=1)
    lp_lam = lp['da_lam'].astype(F32)
    lam = (jnp.exp(jnp.sum(lp_lam[0] * lp_lam[1])) - jnp.exp(jnp.sum(lp_lam[2] * lp_lam[3])) + lam_init)
    o_b = diff_attention(q, k_all, v_all, lam)
    o_b = rmsnorm(o_b, lp['da_norm_g']).astype(F32) * (1.0 - lam_init)
    out_b = (o_b.reshape(b, t, DA_WIDTH) * jax.nn.silu(z_b.astype(F32))).astype(dt)

    qkv = jax.nn.silu(short_conv(jnp.concatenate([q_c, k_c, v_c], axis=-1), lp['dn_conv']))
    qd = l2norm(qkv[..., :DN_WIDTH].reshape(b, t, DN_HEADS, DN_HEAD)) * (DN_HEAD ** -0.5)
    kd = l2norm(qkv[..., DN_WIDTH:2 * DN_WIDTH].reshape(b, t, DN_HEADS, DN_HEAD))
    vd = qkv[..., 2 * DN_WIDTH:].astype(F32).reshape(b, t, DN_HEADS, DN_HEAD)
    beta = jax.nn.sigmoid(beta_c.astype(F32)).reshape(b, t, 2, DN_HEADS)
    g = -jnp.exp(lp['dn_a_log'].astype(F32)) * jax.nn.softplus(
        alpha_c.astype(F32).reshape(b, t, 2, DN_HEADS) + lp['dn_dt_bias'].astype(F32))
    o_f, s_f = gated_delta_rule(qd, kd, vd, beta[:, :, 0], g[:, :, 0], dn_s0[:, 0])
    o_r, s_r = gated_delta_rule(qd[:, ::-1], kd[:, ::-1], vd[:, ::-1], beta[:, ::-1, 1], g[:, ::-1, 1], dn_s0[:, 1])
    o_c = rmsnorm(o_f + o_r[:, ::-1], lp['dn_norm_g'])
    out_c = (o_c.reshape(b, t, DN_WIDTH) * jax.nn.silu(z_c.astype(F32))).astype(dt)

    branches = jnp.stack([out_a, out_b, out_c], axis=2)
    pr = jnp.einsum('btnw,nwd->btnd', branches, lp['w_branch'])
    gt = jax.nn.sigmoid(gates.reshape(b, t, N_BRANCH, D_MODEL))
    y = jnp.sum(gt * pr, axis=2) @ lp['w_out']
    if ctx is None:
        return y, (k, v, jnp.stack(s5_fin_re, axis=1), jnp.stack(s5_fin_im, axis=1),
                   jnp.stack([s_f, s_r], axis=1))
    return y, None


def layer(x, cond, params, l, ctx):
    lp = {name: arr[l] for name, arr in params.items()}
    mod = jax.nn.silu(cond.astype(F32)) @ lp['w_ada'].astype(F32) + lp['b_ada'].astype(F32)
    shift, scale, gate = jnp.split(mod, 3, axis=-1)
    hn = (rmsnorm(x, lp['norm_g']).astype(F32) * (1.0 + scale[:, None]) + shift[:, None]).astype(x.dtype)
    y, st = mixer(hn, lp, 0.8 - 0.6 * math.exp(-0.3 * l), ctx)
    return x + gate[:, None].astype(x.dtype) * y.astype(x.dtype), st


def setup_inputs(seed: int = 0) -> dict:
    key = jax.random.key(seed)
    k = jax.random.split(key, 32)

    def nrm(i, shape, scale=1.0):
        return scale * jax.random.normal(k[i], shape, F32)

    def uni(i, shape, lo, hi):
        return jax.random.uniform(k[i], shape, F32, lo, hi)

    dt0 = jnp.exp(uni(26, (DEPTH, 2, DN_HEADS), math.log(1e-3), math.log(1e-1)))
    return {
        'x_prompt': nrm(0, (BATCH, SEQ, D_MODEL)),
        'x_sample': nrm(1, (DEC_BATCH, DEC_SEQ, D_MODEL)),
        'cache_k': nrm(2, (DEC_BATCH, DEPTH, PAST_LEN, DA_HEADS, 2, DA_HEAD)),
        'cache_v': nrm(3, (DEC_BATCH, DEPTH, PAST_LEN, DA_HEADS, DA_VDIM)),
        'state_s5_re': nrm(4, (DEC_BATCH, DEPTH, 2, S5_GROUPS, S5_STATE), 0.1),
        'state_s5_im': nrm(5, (DEC_BATCH, DEPTH, 2, S5_GROUPS, S5_STATE), 0.1),
        'state_dn': nrm(6, (DEC_BATCH, DEPTH, 2, DN_HEADS, DN_HEAD, DN_HEAD), 0.1),
        'c': nrm(7, (DEC_BATCH, D_MODEL)),
        'c_ctx': nrm(8, (D_MODEL,)),
        'norm_g': 1.0 + nrm(9, (DEPTH, D_MODEL), 0.01),
        'w_ada': nrm(10, (DEPTH, D_MODEL, 3 * D_MODEL), 0.5 * D_MODEL ** -0.5),
        'b_ada': nrm(11, (DEPTH, 3 * D_MODEL), 0.01),
        'w_in': nrm(12, (DEPTH, D_MODEL, N_IN), D_MODEL ** -0.5),
        's5_lam_re': -0.5 + nrm(13, (DEPTH, 2, S5_GROUPS, S5_STATE), 0.01),
        's5_lam_im': math.pi * jnp.arange(S5_STATE, dtype=F32) + nrm(14, (DEPTH, 2, S5_GROUPS, S5_STATE), 0.01),
        's5_log_step': uni(15, (DEPTH, 2, S5_GROUPS), math.log(1e-3), math.log(1e-1)),
        's5_b_re': nrm(16, (DEPTH, 2, S5_GROUPS, S5_STATE, S5_GROUP), 0.7 * S5_GROUP ** -0.5),
        's5_b_im': nrm(17, (DEPTH, 2, S5_GROUPS, S5_STATE, S5_GROUP), 0.7 * S5_GROUP ** -0.5),
        's5_c_re': nrm(18, (DEPTH, 2, S5_GROUPS, S5_GROUP, S5_STATE), 0.7 * S5_STATE ** -0.5),
        's5_c_im': nrm(19, (DEPTH, 2, S5_GROUPS, S5_GROUP, S5_STATE), 0.7 * S5_STATE ** -0.5),
        's5_d': nrm(20, (DEPTH, S5_WIDTH)),
        's5_w_glu': nrm(21, (DEPTH, S5_WIDTH, S5_WIDTH), S5_WIDTH ** -0.5),
        'da_lam': nrm(22, (DEPTH, 4, DA_HEAD), 0.1),
        'da_norm_g': 1.0 + nrm(23, (DEPTH, DA_VDIM), 0.01),
        'dn_conv': nrm(24, (DEPTH, DN_CONV, 3 * DN_WIDTH), DN_CONV ** -0.5),
        'dn_a_log': jnp.log(uni(25, (DEPTH, 2, DN_HEADS), 1.0, 16.0)),
        'dn_dt_bias': dt0 + jnp.log(-jnp.expm1(-dt0)),
        'dn_norm_g': 1.0 + nrm(27, (DEPTH, DN_HEAD), 0.01),
        'w_branch': nrm(28, (DEPTH, N_BRANCH, BRANCH_WIDTH, D_MODEL), BRANCH_WIDTH ** -0.5),
        'w_out': nrm(29, (DEPTH, D_MODEL, D_MODEL), D_MODEL ** -0.5),
        'final_norm_g': 1.0 + nrm(30, (D_MODEL,), 0.01),
    }


def reference(x_prompt, x_sample, cache_k, cache_v, state_s5_re, state_s5_im, state_dn, c, c_ctx,
              norm_g, w_ada, b_ada, w_in, s5_lam_re, s5_lam_im, s5_log_step, s5_b_re, s5_b_im,
              s5_c_re, s5_c_im, s5_d, s5_w_glu, da_lam, da_norm_g, dn_conv, dn_a_log, dn_dt_bias,
              dn_norm_g, w_branch, w_out, final_norm_g):
    params = {
        'norm_g': norm_g, 'w_ada': w_ada, 'b_ada': b_ada, 'w_in': w_in,
        's5_lam_re': s5_lam_re, 's5_lam_im': s5_lam_im, 's5_log_step': s5_log_step,
        's5_b_re': s5_b_re, 's5_b_im': s5_b_im, 's5_c_re': s5_c_re, 's5_c_im': s5_c_im,
        's5_d': s5_d, 's5_w_glu': s5_w_glu, 'da_lam': da_lam, 'da_norm_g': da_norm_g,
        'dn_conv': dn_conv, 'dn_a_log': dn_a_log, 'dn_dt_bias': dn_dt_bias, 'dn_norm_g': dn_norm_g,
        'w_branch': w_branch, 'w_out': w_out,
    }
    x = x_prompt
    states = []
    for l in range(DEPTH):
        x, st = layer(x, c_ctx[None, :], params, l, None)
        states.append(st)
    y_prompt = rmsnorm(x, final_norm_g)
    new_cache_k = jnp.stack([s[0] for s in states], axis=1)
    new_cache_v = jnp.stack([s[1] for s in states], axis=1)
    new_state_s5_re = jnp.stack([s[2] for s in states], axis=1)
    new_state_s5_im = jnp.stack([s[3] for s in states], axis=1)
    new_state_dn = jnp.stack([s[4] for s in states], axis=1)

    x = x_sample
    for l in range(DEPTH):
        ctx = (cache_k[:, l], cache_v[:, l], state_s5_re[:, l], state_s5_im[:, l], state_dn[:, l])
        x, _ = layer(x, c, params, l, ctx)
    y_sample = rmsnorm(x, final_norm_g)
    return (y_prompt, y_sample, new_cache_k, new_cache_v, new_state_s5_re, new_state_s5_im, new_state_dn)
```

```python
import numpy as np
import concourse.bass as bass
import concourse.mybir as mybir
from concourse.bass_utils import run_bass_kernel_spmd

F32 = mybir.dt.float32
BF16 = mybir.dt.bfloat16
ALU = mybir.AluOpType
AF = mybir.ActivationFunctionType


class Buf:
    def __init__(self, name):
        self.name = name
        self.writer = None
        self.readers = {}


class Eng:
    def __init__(self, fw, name, eng, sem):
        self.fw, self.name, self.eng, self.sem = fw, name, eng, sem
        self.count = 0
        self.known = {}


class FW:
    NDMA = 8

    def __init__(self, nc, stack):
        self.nc = nc
        self.sems = {}
        self.engs = {}
        for name, eng in (("pe", nc.tensor), ("dve", nc.vector), ("act", nc.scalar),
                          ("pool", nc.gpsimd), ("sp", nc.sync)):
            sem = stack.enter_context(nc.semaphore("s_" + name))
            self.sems[name] = sem
            self.engs[name] = Eng(self, name, eng, sem)
        self.dma_sems = {}
        self.dma_cnt = {}
        self.dma_idx = {}
        for q in ("sp", "act", "pool"):
            self.dma_sems[q] = []
            for i in range(self.NDMA):
                key = "d_%s%d" % (q, i)
                sem = stack.enter_context(nc.semaphore(key))
                self.sems[key] = sem
                self.dma_sems[q].append(key)
                self.dma_cnt[key] = 0
            self.dma_idx[q] = 0
        self.n_inst = 0

    def _deps(self, ename, reads, writes):
        deps = {}

        def add(k, v):
            if v > deps.get(k, 0):
                deps[k] = v

        for b in reads:
            if b.writer is not None:
                add(*b.writer)
        for b in writes:
            if b.writer is not None:
                k, v = b.writer
                if not (k == "pe" and ename == "pe"):
                    add(k, v)
            for k, v in b.readers.items():
                add(k, v)
        return deps

    def _emit_waits(self, e, deps):
        for k, v in deps.items():
            if e.known.get(k, 0) < v:
                e.eng.wait_ge(self.sems[k], v)
                e.known[k] = v

    def _record(self, key, val, reads, writes):
        for b in reads:
            if b.readers.get(key, 0) < val:
                b.readers[key] = val
        for b in writes:
            b.writer = (key, val)
            b.readers = {}

    def op(self, ename, fn, reads=(), writes=()):
        e = self.engs[ename]
        deps = self._deps(ename, reads, writes)
        if ename == "pe":
            deps.pop("pe", None)
        self._emit_waits(e, deps)
        ins = fn(e.eng)
        e.count += 1
        ins.then_inc(e.sem, 1)
        e.known[ename] = max(e.known.get(ename, 0), 0)
        self._record(ename, e.count, reads, writes)
        self.n_inst += 1
        return ins

    def dma(self, q, out, in_, reads=(), writes=(), **kw):
        e = self.engs[q]
        ring = self.dma_sems[q]
        key = ring[self.dma_idx[q] % self.NDMA]
        self.dma_idx[q] += 1
        deps = self._deps("dma", reads, writes)
        if self.dma_cnt[key] > 0:
            deps[key] = max(deps.get(key, 0), self.dma_cnt[key])
        self._emit_waits(e, deps)
        e.eng.dma_start(out=out, in_=in_, **kw).then_inc(self.sems[key], 16)
        self.dma_cnt[key] += 16
        self._record(key, self.dma_cnt[key], reads, writes)
        self.n_inst += 1

    def finish(self):
        e = self.engs["sp"]
        for key, cnt in self.dma_cnt.items():
            if cnt > 0 and e.known.get(key, 0) < cnt:
                e.eng.wait_ge(self.sems[key], cnt)
                e.known[key] = cnt
        for name in ("pe", "dve", "act", "pool"):
            c = self.engs[name].count
            if c > 0:
                e.eng.wait_ge(self.sems[name], c)

    def barrier(self):
        snap = {n: self.engs[n].count for n in ("pe", "dve", "act", "pool")}
        dsnap = dict(self.dma_cnt)
        for n in ("pe", "dve", "act", "pool", "sp"):
            e = self.engs[n]
            for k, v in list(snap.items()) + list(dsnap.items()):
                if v > 0 and k != n and e.known.get(k, 0) < v:
                    e.eng.wait_ge(self.sems[k], v)
                    e.known[k] = v


class TV:
    def __init__(self, t, name):
        self.t = t
        self.b = Buf(name)


class T:
    def __init__(self, t, name, nsub=0):
        self.t = t
        self.b = None if nsub else Buf(name)
        self.k = [TV(t, "%s.%d" % (name, i)) for i in range(nsub)]


NT = 1024
DM = 1024
NIN = 8208
EPS = 1e-6
OFF = dict(u_a=0, z_a=512, q_b=1024, k_b=1536, v_b=2048, z_b=2560, q_c=3072, k_c=3584,
           v_c=4096, z_c=4608, beta=5120, alpha=5128, gates=5136)
LAM_INIT = [0.8 - 0.6 * float(np.exp(-0.3 * l)) for l in range(2)]
S5_BF16 = True

IN_SPECS = [
    ("xT", [128, 8, 1024]), ("cond", [128, 8]), ("flag", [128, 1]),
    ("ropec", [128, 1024]), ("ropes", [128, 1024]), ("maskb", [128, 48]),
    ("kctx", [128, 2, 4, 512]), ("vctx", [128, 2, 4, 512]),
    ("s5h0r", [128, 2, 2, 16]), ("s5h0i", [128, 2, 2, 16]), ("dns0", [128, 2, 2, 4, 128]),
    ("w_ada", [2, 1024, 3072]), ("b_ada", [2, 128, 24]), ("norm_g", [2, 128, 8]), ("fng", [128, 8]),
    ("w_in", [2, 1024, NIN]), ("w_qkp", [2, 1024, 1024]),
    ("lamre_c", [2, 128, 2, 4, 64]), ("lamim_c", [2, 128, 2, 4, 64]), ("lstep_c", [2, 128, 2, 4, 64]),
    ("bre_c", [2, 128, 2, 4, 64]), ("bim_c", [2, 128, 2, 4, 64]),
    ("lamre_s", [2, 128, 2, 16]), ("lamim_s", [2, 128, 2, 16]), ("lstep_s", [2, 128, 2, 16]),
    ("cre_s", [2, 128, 2, 16, 16]), ("cim_s", [2, 128, 2, 16, 16]),
    ("s5d", [2, 128, 4]), ("wglu", [2, 512, 512]), ("dalam", [2, 128, 256]), ("dang", [2, 128, 128]),
    ("dnconv", [2, 128, 12, 5]), ("dnalog", [2, 128, 8]), ("dndt", [2, 128, 8]), ("dnng", [2, 128, 128]),
    ("w_branch", [2, 3, 512, 1024]), ("w_out", [2, 1024, 1024]),
    ("ident", [128, 128]), ("maskB", [128, 8]), ("cm", [128, 3, 128]),
]
OUT_SPECS = [
    ("yT", [128, 8, 1024]), ("newk", [128, 2, 4, 1024]), ("newv", [128, 2, 4, 1024]),
    ("ns5r", [128, 2, 2, 16, 4]), ("ns5i", [128, 2, 2, 16, 4]), ("ndn", [128, 2, 2, 4, 4, 128]),
]


def _f(a):
    return np.ascontiguousarray(np.asarray(a, dtype=np.float32))


def _rope_tables():
    t = np.arange(1024)
    row = (t // 64).astype(np.float32)
    col = (t % 64).astype(np.float32)
    nf = 16
    inv = (np.float32(10000.0) ** (-np.arange(nf, dtype=np.float32) / np.float32(nf))).astype(np.float32)
    cosT = np.zeros((128, 1024), np.float32)
    sinT = np.zeros((128, 1024), np.float32)
    for p in range(128):
        d = p % 64
        pos = row if d < 32 else col
        j = d % 16
        ang = (pos * inv[j]).astype(np.float32)
        first = (d % 32) < 16
        cosT[p] = np.cos(ang)
        sinT[p] = -np.sin(ang) if first else np.sin(ang)
    return cosT, sinT


def _const_masks():
    i = np.arange(128)
    same = (i[:, None] // 64) == (i[None, :] // 64)
    cm = np.zeros((128, 3, 128), np.float32)
    cm[:, 0, :] = same & (i[:, None] <= i[None, :])
    cm[:, 1, :] = same & (i[:, None] >= i[None, :])
    cm[:, 2, :] = 1.0
    return cm


def prep_inputs(inp):
    g = {k: np.asarray(v) for k, v in inp.items()}
    sh = {}
    sh["w_ada"] = _f(g["w_ada"])
    sh["b_ada"] = _f(g["b_ada"].reshape(2, 24, 128).transpose(0, 2, 1))
    sh["norm_g"] = _f(g["norm_g"].reshape(2, 8, 128).transpose(0, 2, 1))
    sh["fng"] = _f(g["final_norm_g"].reshape(8, 128).T)
    sh["w_in"] = _f(g["w_in"])
    d = np.arange(64)
    perm = np.where((d % 32) < 16, d + 16, d - 16)
    cols = []
    for base in (OFF["q_b"], OFF["k_b"]):
        for hm in range(8):
            cols.append(base + hm * 64 + perm)
    cols = np.concatenate(cols)
    sh["w_qkp"] = _f(g["w_in"][:, :, cols])
    def chmaj(a):
        a = a.reshape(2, 2, 4, 8, 64)
        a = np.broadcast_to(a[:, :, :, :, None, :], (2, 2, 4, 8, 16, 64))
        return _f(a.transpose(0, 3, 4, 1, 2, 5).reshape(2, 128, 2, 4, 64))
    sh["lamre_c"] = chmaj(g["s5_lam_re"])
    sh["lamim_c"] = chmaj(g["s5_lam_im"])
    sh["lstep_c"] = chmaj(np.broadcast_to(g["s5_log_step"][..., None], (2, 2, 32, 64)))
    def bmaj(a):
        a = a.reshape(2, 2, 4, 8, 64, 16)
        return _f(a.transpose(0, 3, 5, 1, 2, 4).reshape(2, 128, 2, 4, 64))
    sh["bre_c"] = bmaj(g["s5_b_re"])
    sh["bim_c"] = bmaj(g["s5_b_im"])
    def stmaj(a):
        a = a.reshape(2, 2, 16, 2, 64)
        return _f(a.transpose(0, 3, 4, 1, 2).reshape(2, 128, 2, 16))
    sh["lamre_s"] = stmaj(g["s5_lam_re"])
    sh["lamim_s"] = stmaj(g["s5_lam_im"])
    sh["lstep_s"] = stmaj(np.broadcast_to(g["s5_log_step"][..., None], (2, 2, 32, 64)))
    def cmaj(a):
        a = a.reshape(2, 2, 16, 2, 16, 64)
        return _f(a.transpose(0, 3, 5, 1, 2, 4).reshape(2, 128, 2, 16, 16))
    sh["cre_s"] = cmaj(g["s5_c_re"])
    sh["cim_s"] = cmaj(g["s5_c_im"])
    sh["s5d"] = _f(g["s5_d"].reshape(2, 4, 128).transpose(0, 2, 1))
    sh["wglu"] = _f(g["s5_w_glu"])
    sh["dalam"] = _f(np.broadcast_to(g["da_lam"].reshape(2, 1, 256), (2, 128, 256)))
    sh["dang"] = _f(np.broadcast_to(g["da_norm_g"].reshape(2, 1, 128), (2, 128, 128)))
    sh["dnconv"] = _f(g["dn_conv"].reshape(2, 5, 12, 128).transpose(0, 3, 2, 1))
    sh["dnalog"] = _f(np.broadcast_to(g["dn_a_log"].reshape(2, 1, 8), (2, 128, 8)))
    sh["dndt"] = _f(np.broadcast_to(g["dn_dt_bias"].reshape(2, 1, 8), (2, 128, 8)))
    sh["dnng"] = _f(np.broadcast_to(g["dn_norm_g"].reshape(2, 1, 128), (2, 128, 128)))
    sh["w_branch"] = _f(g["w_branch"])
    sh["w_out"] = _f(g["w_out"])
    sh["ident"] = np.eye(128, dtype=np.float32)
    p = np.arange(128)
    mB = np.zeros((128, 8), np.float32)
    for st in range(4):
        for gl in range(2):
            mB[:, st * 2 + gl] = ((p // 32) == st) & (((p // 16) % 2) == gl)
    sh["maskB"] = mB
    sh["cm"] = _const_masks()
    cosT, sinT = _rope_tables()
    maps = []
    for c in range(8):
        m = dict(sh)
        if c < 4:
            X = g["x_prompt"][4 * c:4 * c + 4].reshape(1024, 1024)
            cond = g["c_ctx"]
            m["flag"] = np.zeros((128, 1), np.float32)
            m["ropec"] = np.ones((128, 1024), np.float32)
            m["ropes"] = np.zeros((128, 1024), np.float32)
            mb = np.full((12, 4), -30000.0, np.float32)
            for kt in range(8):
                mb[kt, kt // 2] = 0.0
            m["kctx"] = np.zeros((128, 2, 4, 512), np.float32)
            m["vctx"] = np.zeros((128, 2, 4, 512), np.float32)
            m["s5h0r"] = np.zeros((128, 2, 2, 16), np.float32)
            m["s5h0i"] = np.zeros((128, 2, 2, 16), np.float32)
            m["dns0"] = np.zeros((128, 2, 2, 4, 128), np.float32)
        else:
            b = c - 4
            X = g["x_sample"][b]
            cond = g["c"][b]
            m["flag"] = np.ones((128, 1), np.float32)
            m["ropec"] = cosT
            m["ropes"] = sinT
            mb = np.zeros((12, 4), np.float32)
            ck = g["cache_k"][b]
            m["kctx"] = _f(ck.transpose(3, 4, 0, 2, 1).reshape(128, 2, 4, 512))
            cv = g["cache_v"][b]
            m["vctx"] = _f(cv.reshape(2, 4, 128, 512).transpose(2, 0, 1, 3))
            def st5(a):
                a = a.reshape(2, 2, 16, 2, 64)
                return _f(a.transpose(3, 4, 0, 1, 2).reshape(128, 2, 2, 16))
            m["s5h0r"] = st5(g["state_s5_re"][b])
            m["s5h0i"] = st5(g["state_s5_im"][b])
            m["dns0"] = _f(g["state_dn"][b].transpose(3, 0, 1, 2, 4))
        m["maskb"] = _f(np.broadcast_to(mb.reshape(1, 48), (128, 48)))
        m["xT"] = _f(X.T.reshape(8, 128, 1024).transpose(1, 0, 2))
        m["cond"] = _f(cond.reshape(8, 128).T)
        maps.append(m)
    return maps


def assemble(results):
    y_prompt = np.zeros((16, 256, 1024), np.float32)
    y_sample = np.zeros((4, 1024, 1024), np.float32)
    nk = np.zeros((16, 2, 256, 4, 2, 64), np.float32)
    nv = np.zeros((16, 2, 256, 4, 128), np.float32)
    s5r = np.zeros((16, 2, 2, 32, 64), np.float32)
    s5i = np.zeros((16, 2, 2, 32, 64), np.float32)
    sdn = np.zeros((16, 2, 2, 4, 128, 128), np.float32)
    for c in range(8):
        r = results[c]
        Y = np.asarray(r["yT"]).transpose(1, 0, 2).reshape(1024, 1024).T
        if c < 4:
            y_prompt[4 * c:4 * c + 4] = Y.reshape(4, 256, 1024)
            k = np.asarray(r["newk"]).reshape(2, 64, 2, 4, 4, 256)
            nk[4 * c:4 * c + 4] = k.transpose(4, 2, 5, 3, 0, 1)
            v = np.asarray(r["newv"]).reshape(128, 2, 4, 4, 256)
            nv[4 * c:4 * c + 4] = v.transpose(3, 1, 4, 2, 0)
            for nm, dst in (("ns5r", s5r), ("ns5i", s5i)):
                a = np.asarray(r[nm]).reshape(2, 64, 2, 2, 16, 4)
                dst[4 * c:4 * c + 4] = a.transpose(5, 2, 3, 4, 0, 1).reshape(4, 2, 2, 32, 64)
            a = np.asarray(r["ndn"])
            sdn[4 * c:4 * c + 4] = a.transpose(4, 1, 2, 3, 0, 5)
        else:
            y_sample[c - 4] = Y
    return (y_prompt, y_sample, nk, nv, s5r, s5i, sdn)


def build(dbg=(), stop_after=None, nlayers=2, skip=()):
    from contextlib import ExitStack
    nc = bass.Bass("TRN2", target_bir_lowering=False)
    D = {n: nc.dram_tensor(n, list(s), F32, kind="ExternalInput").ap() for n, s in IN_SPECS}
    O = {n: nc.dram_tensor(n, list(s), F32, kind="ExternalOutput").ap() for n, s in OUT_SPECS}
    DBG = {}
    top = ExitStack()
    fw = FW(nc, top)

    uniq = [0]
    halt = [False]

    def sbt(stk, name, shape, dt=F32, nsub=0):
        uniq[0] += 1
        return T(stk.enter_context(nc.sbuf_tensor("t%d_%s" % (uniq[0], name), list(shape), dt)), name, nsub)

    def OP(eng, meth, *a, R=(), W=(), **kw):
        return fw.op(eng, lambda e: getattr(e, meth)(*a, **kw),
                     reads=[x.b for x in R], writes=[x.b for x in W])

    def DMA(q, out, in_, R=(), W=(), **kw):
        fw.dma(q, out, in_, reads=[x.b for x in R], writes=[x.b for x in W], **kw)

    def tap(name, src_ap, shape, R):
        if name in dbg:
            DBG[name] = nc.dram_tensor("dbg_" + name, list(shape), F32, kind="ExternalOutput").ap()
            q = "sp" if src_ap.tensor.dtype == F32 else "pool"
            DMA(q, DBG[name], src_ap, R=R)

    def MM(outT, out_ap, lhsT_T, lhsT_ap, rhs_T, rhs_ap, start=True, stop=True):
        return fw.op("pe", lambda e: e.matmul(out_ap, lhsT=lhsT_ap, rhs=rhs_ap, start=start, stop=stop),
                     reads=[lhsT_T.b, rhs_T.b], writes=[outT.b])

    def v3(ap, s=4):
        return ap.rearrange("p (s t) -> p s t", s=s)

    def col3(ap_p_q, n):
        q = ap_p_q.shape[1]
        return ap_p_q.rearrange("p (q o) -> p q o", o=1).broadcast_to([128, q, n])

    P = [T(top.enter_context(nc.psum_tensor("P%d" % i, [128, 1024], F32)), "P%d" % i) for i in range(4)]
    xT = sbt(top, "xT", [128, 8, 1024], nsub=8)
    hnT = sbt(top, "hnT", [128, 8, 1024], BF16, nsub=8)
    merged = sbt(top, "merged", [128, 8, 1024], nsub=8)
    WB = [sbt(top, "wb%d" % i, [128, 8, 512], BF16) for i in range(2)]
    wb_i = [0]
    ident = sbt(top, "ident", [128, 128])
    cm = sbt(top, "cm", [128, 3, 128])
    onesb = sbt(top, "onesb", [128, 128], BF16)
    maskB = sbt(top, "maskB", [128, 8])
    flag = sbt(top, "flag", [128, 1])
    modt = sbt(top, "modt", [128, 48])
    gmod = sbt(top, "gmod", [128, 16])
    rstd = sbt(top, "rstd", [128, 1024])
    SCA = [sbt(top, "sca%d" % i, [128, 1024]) for i in range(3)]
    SCB = [sbt(top, "scb%d" % i, [128, 1024], BF16) for i in range(2)]
    sci = [0, 0]

    def scA():
        sci[0] += 1
        return SCA[sci[0] % 3]

    def scB():
        sci[1] += 1
        return SCB[sci[1] % 2]

    for kt in range(8):
        DMA("sp", xT.t[:, kt, :], D["xT"][:, kt, :], W=[xT.k[kt]])
    DMA("sp", ident.t[:], D["ident"], W=[ident])
    DMA("sp", cm.t[:], D["cm"], W=[cm])
    DMA("sp", maskB.t[:], D["maskB"], W=[maskB])
    DMA("sp", flag.t[:], D["flag"], W=[flag])
    OP("pool", "memset", onesb.t[:], 1.0, W=[onesb])

    def wb_next():
        buf = WB[wb_i[0] % 2]
        wb_i[0] += 1
        return buf

    def load_into(buf, coff, dram_ap, ktiles, ncols):
        DMA("pool", buf.t[:, 0:ktiles, coff:coff + ncols], dram_ap.rearrange("(kt p) n -> p kt n", p=128), W=[buf])

    def load_w(dram_ap, ktiles, ncols):
        buf = wb_next()
        load_into(buf, 0, dram_ap, ktiles, ncols)
        return buf

    def project(wbuf, ktiles, c0, rhs_fn, pT):
        for half in range(2):
            for kt in range(ktiles):
                rT, rap = rhs_fn(kt, half)
                MM(pT, pT.t[:, half * 512:(half + 1) * 512], wbuf, wbuf.t[:, kt, c0:c0 + 128],
                   rT, rap, start=(kt == 0), stop=(kt == ktiles - 1))

    def hn_rhs(kt, half):
        return hnT.k[kt], hnT.t[:, kt, half * 512:(half + 1) * 512]

    def k_rhs(tt):
        return lambda kt, half: (tt.k[kt], tt.t[:, kt, half * 512:(half + 1) * 512])

    with ExitStack() as s0:
        cond = sbt(s0, "cond", [128, 8])
        scond = sbt(s0, "scond", [128, 8])
        bada = sbt(s0, "bada", [128, 48])
        ng = sbt(s0, "ng", [128, 16])
        wst = [sbt(s0, "wst%d" % i, [128, 8, 512]) for i in range(2)]
        DMA("sp", cond.t[:], D["cond"], W=[cond])
        for l in range(2):
            DMA("sp", bada.t[:, l * 24:(l + 1) * 24], D["b_ada"][l], W=[bada])
            DMA("sp", ng.t[:, l * 8:(l + 1) * 8], D["norm_g"][l], W=[ng])
        OP("act", "activation", scond.t[:], cond.t[:], AF.Silu, R=[cond], W=[scond])
        i = 0
        for l in range(2):
            wv = D["w_ada"][l].rearrange("(kt p) n -> p kt n", p=128)
            for nb in range(6):
                w = wst[i % 2]
                i += 1
                DMA("sp" if i % 2 == 1 else "act", w.t[:], wv[:, :, nb * 512:(nb + 1) * 512], W=[w])
                for m in range(4):
                    j = l * 24 + nb * 4 + m
                    for kt in range(8):
                        MM(P[0], P[0].t[:, j:j + 1], w, w.t[:, kt, m * 128:(m + 1) * 128],
                           scond, scond.t[:, kt:kt + 1], start=(kt == 0), stop=(kt == 7))
        OP("dve", "tensor_tensor", modt.t[:], P[0].t[:, 0:48], bada.t[:], ALU.add, R=[P[0], bada], W=[modt])
        for l in range(2):
            OP("dve", "scalar_tensor_tensor", gmod.t[:, l * 8:(l + 1) * 8], modt.t[:, l * 24 + 8:l * 24 + 16], 1.0,
               ng.t[:, l * 8:(l + 1) * 8], ALU.add, ALU.mult, R=[modt, ng], W=[gmod])
        tap("modt", modt.t[:], [128, 48], [modt])
        fw.barrier()

    def rms_stats():
        for kt in range(8):
            sb_ = scB()
            OP("act", "activation", sb_.t[:], xT.t[:, kt, :], AF.Square, R=[xT.k[kt]], W=[sb_])
            for half in range(2):
                MM(P[0], P[0].t[:, half * 512:(half + 1) * 512], onesb, onesb.t[:], sb_,
                   sb_.t[:, half * 512:(half + 1) * 512], start=(kt == 0), stop=(kt == 7))
        OP("dve", "tensor_scalar", rstd.t[:], P[0].t[:], 1.0 / DM, EPS, ALU.mult, ALU.add, R=[P[0]], W=[rstd])
        OP("dve", "reciprocal", rstd.t[:], rstd.t[:], R=[rstd], W=[rstd])
        OP("act", "activation", rstd.t[:], rstd.t[:], AF.Sqrt, R=[rstd], W=[rstd])

    def norm_modulate(l):
        rms_stats()
        for kt in range(8):
            sa = scA()
            OP("dve", "scalar_tensor_tensor", sa.t[:], xT.t[:, kt, :], gmod.t[:, l * 8 + kt:l * 8 + kt + 1],
               rstd.t[:], ALU.mult, ALU.mult, R=[xT.k[kt], gmod, rstd], W=[sa])
            OP("act", "activation", hnT.t[:, kt, :], sa.t[:], AF.Identity,
               bias=modt.t[:, l * 24 + kt:l * 24 + kt + 1], R=[sa, modt], W=[hnT.k[kt]])

    def merge_branch(l, n, outTs):
        for blk in range(2):
            wbr = load_w(D["w_branch"][l, n][:, blk * 512:(blk + 1) * 512], 4, 512)
            g0 = OFF["gates"] + n * 1024 + blk * 512
            wg = load_w(D["w_in"][l][:, g0:g0 + 512], 8, 512)
            for m in range(4):
                dt_ = blk * 4 + m
                pr, gp = P[(dt_ % 2) * 2], P[(dt_ % 2) * 2 + 1]
                project(wbr, 4, m * 128, k_rhs(outTs), pr)
                project(wg, 8, m * 128, hn_rhs, gp)
                sa = scA()
                OP("act", "activation", sa.t[:], gp.t[:], AF.Sigmoid, R=[gp], W=[sa])
                if n == 0:
                    OP("dve", "tensor_tensor", merged.t[:, dt_, :], sa.t[:], pr.t[:], ALU.mult,
                       R=[sa, pr], W=[merged.k[dt_]])
                else:
                    OP("dve", "tensor_tensor", sa.t[:], sa.t[:], pr.t[:], ALU.mult, R=[sa, pr], W=[sa])
                    OP("pool", "tensor_tensor", merged.t[:, dt_, :], merged.t[:, dt_, :], sa.t[:], ALU.add,
                       R=[sa, merged.k[dt_]], W=[merged.k[dt_]])

    def out_proj_residual(l):
        for kt in range(8):
            OP("act", "activation", hnT.t[:, kt, :], merged.t[:, kt, :], AF.Copy, R=[merged.k[kt]], W=[hnT.k[kt]])
        for blk in range(2):
            wo = load_w(D["w_out"][l][:, blk * 512:(blk + 1) * 512], 8, 512)
            for m in range(4):
                dt_ = blk * 4 + m
                pp = P[2 + m % 2]
                project(wo, 8, m * 128, hn_rhs, pp)
                OP("dve", "scalar_tensor_tensor", xT.t[:, dt_, :], pp.t[:],
                   modt.t[:, l * 24 + 16 + dt_:l * 24 + 17 + dt_],
                   xT.t[:, dt_, :], ALU.mult, ALU.add, R=[pp, modt, xT.k[dt_]], W=[xT.k[dt_]])

    def TT(eng, oT, o, aT, a, bT, b, op):
        OP(eng, "tensor_tensor", o, a, b, op, R=[aT, bT], W=[oT])

    def TS(eng, oT, o, aT, a, s1, s2, op0, op1=None, RS=()):
        if op1 is None:
            OP(eng, "tensor_scalar", o, a, s1, None, op0, R=[aT] + list(RS), W=[oT])
        else:
            OP(eng, "tensor_scalar", o, a, s1, s2, op0, op1, R=[aT] + list(RS), W=[oT])

    def STT(eng, oT, o, aT, a, sc, bT, b, op0, op1, RS=()):
        OP(eng, "scalar_tensor_tensor", o, a, sc, b, op0, op1, R=[aT, bT] + list(RS), W=[oT])

    def ACT(oT, o, aT, a, func, RS=(), **kw):
        OP("act", "activation", o, a, func, R=[aT] + list(RS), W=[oT], **kw)

    def cos_sin(stk, eng, th, n, tag):
        c = [sbt(stk, "cs_c%d%s" % (i, tag), [128, n]) for i in range(2)]
        s = [sbt(stk, "cs_s%d%s" % (i, tag), [128, n]) for i in range(2)]
        t = sbt(stk, "cs_t" + tag, [128, n])
        hp = sbt(stk, "cs_hp" + tag, [128, 1])
        OP(eng, "memset", hp.t[:], float(np.pi / 2), W=[hp])
        ACT(s[0], s[0].t[:], th, th.t[:], AF.Sin, scale=1.0 / 16)
        ACT(c[0], c[0].t[:], th, th.t[:], AF.Sin, RS=[hp], scale=1.0 / 16, bias=hp.t[:])
        for it in range(4):
            a, b = it % 2, (it + 1) % 2
            TT(eng, t, t.t[:], s[a], s[a].t[:], s[a], s[a].t[:], ALU.mult)
            STT(eng, s[b], s[b].t[:], s[a], s[a].t[:], 2.0, c[a], c[a].t[:], ALU.mult, ALU.mult)
            TS(eng, c[b], c[b].t[:], t, t.t[:], -2.0, 1.0, ALU.mult, ALU.add)
        return c[0], s[0]

    def branch_a(l):
        with ExitStack() as sa_:
            uT = sbt(sa_, "uT", [128, 4, 1024], BF16, nsub=4)
            ya = sbt(sa_, "ya", [128, 4, 1024], BF16, nsub=4)
            bbc = sbt(sa_, "bbc", [128, 2, 512])
            rho = sbt(sa_, "rho", [128, 32])
            g0r = sbt(sa_, "g0r", [128, 32])
            g0i = sbt(sa_, "g0i", [128, 32])
            cs = sbt(sa_, "cs", [128, 2, 512])
            ns5 = sbt(sa_, "ns5", [128, 2, 128])
            s5d = sbt(sa_, "s5d", [128, 4])
            cth0 = None
            DMA("sp", s5d.t[:], D["s5d"][l], W=[s5d])
            DMA("sp", cs.t[:, 0, :], D["cre_s"][l].rearrange("p d t i -> p (d t i)"), W=[cs])
            DMA("sp", cs.t[:, 1, :], D["cim_s"][l].rearrange("p d t i -> p (d t i)"), W=[cs])
            OP("pool", "tensor_scalar", cs.t[:, 1, :], cs.t[:, 1, :], -1.0, None, ALU.mult, R=[cs], W=[cs])

            wu = load_w(D["w_in"][l][:, OFF["u_a"]:OFF["u_a"] + 512], 8, 512)
            for m in range(4):
                pp = P[m % 2]
                project(wu, 8, m * 128, hn_rhs, pp)
                ACT(uT.k[m], uT.t[:, m, :], pp, pp.t[:], AF.Copy)

            cthp = sbt(sa_, "cthp", [128, 32]); sthp = sbt(sa_, "sthp", [128, 32])
            sd = ExitStack()
            lr = sbt(sd, "lr_s", [128, 32]); li = sbt(sd, "li_s", [128, 32]); ls = sbt(sd, "ls_s", [128, 32])
            h0r = sbt(sd, "h0r", [128, 32]); h0i = sbt(sd, "h0i", [128, 32])
            th = sbt(sd, "th_s", [128, 32]); tmp = sbt(sd, "tmp_s", [128, 32])
            for tns, nm in ((lr, "lamre_s"), (li, "lamim_s"), (ls, "lstep_s")):
                DMA("sp", tns.t[:], D[nm][l].rearrange("p d t -> p (d t)"), W=[tns])
            DMA("sp", h0r.t[:], D["s5h0r"][:, l].rearrange("p d t -> p (d t)"), W=[h0r])
            DMA("sp", h0i.t[:], D["s5h0i"][:, l].rearrange("p d t -> p (d t)"), W=[h0i])
            ACT(ls, ls.t[:], ls, ls.t[:], AF.Exp)
            TT("dve", tmp, tmp.t[:], lr, lr.t[:], ls, ls.t[:], ALU.mult)
            ACT(rho, rho.t[:], tmp, tmp.t[:], AF.Exp)
            TT("dve", th, th.t[:], li, li.t[:], ls, ls.t[:], ALU.mult)
            tap("li%d" % l, li.t[:], [128, 32], [li])
            tap("th%d" % l, th.t[:], [128, 32], [th])
            c_, s_ = cos_sin(sd, "dve", th, 32, "s")
            OP("dve", "tensor_copy", cthp.t[:], c_.t[:], R=[c_], W=[cthp])
            OP("dve", "tensor_copy", sthp.t[:], s_.t[:], R=[s_], W=[sthp])
            TT("dve", tmp, tmp.t[:], s_, s_.t[:], h0i, h0i.t[:], ALU.mult)
            TT("dve", g0r, g0r.t[:], c_, c_.t[:], h0r, h0r.t[:], ALU.mult)
            TT("dve", g0r, g0r.t[:], g0r, g0r.t[:], tmp, tmp.t[:], ALU.subtract)
            TT("dve", tmp, tmp.t[:], s_, s_.t[:], h0r, h0r.t[:], ALU.mult)
            TT("dve", g0i, g0i.t[:], c_, c_.t[:], h0i, h0i.t[:], ALU.mult)
            TT("dve", g0i, g0i.t[:], g0i, g0i.t[:], tmp, tmp.t[:], ALU.add)

            names = ["lrc", "lic", "lsc", "brc", "bic", "thc", "mag", "ar1", "ai", "den", "t1", "t2", "fr", "fi"]
            A = {n_: sbt(sd, n_ + "_c", [128, 512]) for n_ in names}
            for tns, nm in ((A["lrc"], "lamre_c"), (A["lic"], "lamim_c"), (A["lsc"], "lstep_c"),
                            (A["brc"], "bre_c"), (A["bic"], "bim_c")):
                DMA("sp", tns.t[:], D[nm][l].rearrange("p d c q -> p (d c q)"), W=[tns])

            def e2(o, a, b, op, eng="dve"):
                TT(eng, A[o], A[o].t[:], A[a], A[a].t[:], A[b], A[b].t[:], op)
            ACT(A["lsc"], A["lsc"].t[:], A["lsc"], A["lsc"].t[:], AF.Exp)
            e2("t1", "lrc", "lsc", ALU.mult)
            ACT(A["mag"], A["mag"].t[:], A["t1"], A["t1"].t[:], AF.Exp)
            e2("thc", "lic", "lsc", ALU.mult)
            cc, ss = cos_sin(sd, "dve", A["thc"], 512, "c")
            TT("dve", A["ai"], A["ai"].t[:], A["mag"], A["mag"].t[:], ss, ss.t[:], ALU.mult)
            TT("dve", A["ar1"], A["ar1"].t[:], A["mag"], A["mag"].t[:], cc, cc.t[:], ALU.mult)
            TS("dve", A["ar1"], A["ar1"].t[:], A["ar1"], A["ar1"].t[:], -1.0, None, ALU.add)
            e2("den", "lrc", "lrc", ALU.mult)
            e2("t1", "lic", "lic", ALU.mult)
            e2("den", "den", "t1", ALU.add)
            OP("dve", "reciprocal", A["den"].t[:], A["den"].t[:], R=[A["den"]], W=[A["den"]])
            e2("t1", "ar1", "lrc", ALU.mult)
            e2("t2", "ai", "lic", ALU.mult)
            e2("fr", "t1", "t2", ALU.add)
            e2("fr", "fr", "den", ALU.mult)
            e2("t1", "ai", "lrc", ALU.mult)
            e2("t2", "ar1", "lic", ALU.mult)
            e2("fi", "t1", "t2", ALU.subtract)
            e2("fi", "fi", "den", ALU.mult)
            e2("t1", "fr", "brc", ALU.mult)
            e2("t2", "fi", "bic", ALU.mult)
            TT("dve", bbc, bbc.t[:, 0, :], A["t1"], A["t1"].t[:], A["t2"], A["t2"].t[:], ALU.subtract)
            e2("t1", "fr", "bic", ALU.mult)
            e2("t2", "fi", "brc", ALU.mult)
            TT("dve", bbc, bbc.t[:, 1, :], A["t1"], A["t1"].t[:], A["t2"], A["t2"].t[:], ALU.add)
            fw.barrier()
            sd.close()
            tap("bbc%d" % l, bbc.t[:], [128, 2, 512], [bbc])
            tap("rho%d" % l, rho.t[:], [128, 32], [rho])
            tap("cth%d" % l, cthp.t[:], [128, 32], [cthp])
            tap("sth%d" % l, sthp.t[:], [128, 32], [sthp])

            Tr32 = sbt(sa_, "Tr32", [128, 8, 256]); Ti32 = sbt(sa_, "Ti32", [128, 8, 256])
            SETS = []
            for i_ in range(2):
                SETS.append(dict(
                    Bbf=sbt(sa_, "Bbf", [128, 2, 4, 2, 128], BF16), CTp=sbt(sa_, "CTp", [128, 2, 4, 2, 128], BF16),
                    Tr=sbt(sa_, "Trb", [128, 8, 256], BF16 if S5_BF16 else F32),
                    Ti=sbt(sa_, "Tib", [128, 8, 256], BF16 if S5_BF16 else F32),
                    Cr=sbt(sa_, "Cr", [128, 9, 8]), Ci=sbt(sa_, "Ci", [128, 9, 8]), ctmp=sbt(sa_, "ctmp", [128, 3, 8]),
                    Kr=sbt(sa_, "Kr", [128, 8]), Ki=sbt(sa_, "Ki", [128, 8]), nKi=sbt(sa_, "nKi", [128, 8]),
                    c255=sbt(sa_, "c255", [128, 8]), s255=sbt(sa_, "s255", [128, 8]), ns255=sbt(sa_, "ns255", [128, 8])))
            HB = [sbt(sa_, "hb%d" % i, [128, 1024], BF16) for i in range(4)]
            carry = sbt(sa_, "carry", [128, 2, 2, 4])
            bbc4 = bbc.t[:].rearrange("p r (d c q) -> p r d c q", d=2, c=4)
            cs4 = cs.t[:].rearrange("p r (d t i) -> p r d t i", d=2, t=16)
            cth3 = cthp.t[:].rearrange("p (d t) -> p d t", d=2)
            sth3 = sthp.t[:].rearrange("p (d t) -> p d t", d=2)
            rho3 = rho.t[:].rearrange("p (d t) -> p d t", d=2)
            ns54 = ns5.t[:].rearrange("p r (d t s) -> p r d t s", d=2, t=16)
            TE = "pool"
            pipe_ = [0]

            def bfv(i):
                if not S5_BF16:
                    return merged.t[:, i, :]
                return merged.t[:, i, :].bitcast(BF16)[:, 0:1024]

            def unpack(ct):
                S_ = SETS[ct % 2]
                return tuple(S_[n_] for n_ in ("Bbf", "CTp", "Tr", "Ti", "Cr", "Ci", "ctmp", "Kr", "Ki", "nKi",
                                               "c255", "s255", "ns255"))

            def build_ct(ct):
                Bbf, CTp, Trb_, Tib_, Cr, Ci, ctmp, Kr, Ki, nKi, c255, s255, ns255 = unpack(ct)
                Tr, Ti = Tr32, Ti32
                for st_ in range(4):
                    for gl in range(2):
                        for ri in range(2):
                            OP(TE, "tensor_scalar", Bbf.t[:, :, st_, ri, gl * 64:(gl + 1) * 64], bbc4[:, ri, :, ct, :],
                               maskB.t[:, st_ * 2 + gl:st_ * 2 + gl + 1], None, ALU.mult, R=[bbc, maskB], W=[Bbf])
                OP(TE, "memset", CTp.t[:], 0.0, W=[CTp])
                for st_ in range(4):
                    for gl in range(2):
                        for ri in range(2):
                            OP(TE, "tensor_copy",
                               CTp.t[gl * 64:(gl + 1) * 64, :, st_, ri, 32 * st_ + gl * 16:32 * st_ + gl * 16 + 16],
                               cs4[gl * 64:(gl + 1) * 64, ri, :, ct * 4 + st_, :], R=[cs], W=[CTp])
                OP(TE, "tensor_copy", Cr.t[:, 0, :].rearrange("p (d s) -> p d s", d=2), cth3[:, :, ct * 4:(ct + 1) * 4],
                   R=[cthp], W=[Cr])
                OP(TE, "tensor_copy", Ci.t[:, 0, :].rearrange("p (d s) -> p d s", d=2), sth3[:, :, ct * 4:(ct + 1) * 4],
                   R=[sthp], W=[Ci])
                OP(TE, "memset", Tr.t[:, :, 0:1], 1.0, W=[Tr])
                OP(TE, "memset", Ti.t[:, :, 0:1], 0.0, W=[Ti])
                tA, tB = SCA[0], SCA[1]
                for k in range(8):
                    n = 1 << k
                    crb = col3(Cr.t[:, k, :], n)
                    cib = col3(Ci.t[:, k, :], n)
                    a3 = tA.t[:, 0:8 * n].rearrange("p (q n) -> p q n", q=8)
                    b3 = tB.t[:, 0:8 * n].rearrange("p (q n) -> p q n", q=8)
                    OP(TE, "tensor_tensor", a3, Tr.t[:, :, 0:n], crb, ALU.mult, R=[Tr, Cr], W=[tA])
                    OP(TE, "tensor_tensor", b3, Ti.t[:, :, 0:n], cib, ALU.mult, R=[Ti, Ci], W=[tB])
                    OP(TE, "tensor_tensor", Tr.t[:, :, n:2 * n], a3, b3, ALU.subtract, R=[tA, tB], W=[Tr])
                    OP(TE, "tensor_tensor", a3, Tr.t[:, :, 0:n], cib, ALU.mult, R=[Tr, Ci], W=[tA])
                    OP(TE, "tensor_tensor", b3, Ti.t[:, :, 0:n], crb, ALU.mult, R=[Ti, Cr], W=[tB])
                    OP(TE, "tensor_tensor", Ti.t[:, :, n:2 * n], a3, b3, ALU.add, R=[tA, tB], W=[Ti])
                    OP(TE, "tensor_tensor", ctmp.t[:, 0, :], Cr.t[:, k, :], Cr.t[:, k, :], ALU.mult, R=[Cr], W=[ctmp])
                    OP(TE, "tensor_tensor", ctmp.t[:, 1, :], Ci.t[:, k, :], Ci.t[:, k, :], ALU.mult, R=[Ci], W=[ctmp])
                    OP(TE, "tensor_tensor", Cr.t[:, k + 1, :], ctmp.t[:, 0, :], ctmp.t[:, 1, :], ALU.subtract,
                       R=[ctmp], W=[Cr])
                    OP(TE, "tensor_tensor", ctmp.t[:, 2, :], Cr.t[:, k, :], Ci.t[:, k, :], ALU.mult, R=[Cr, Ci], W=[ctmp])
                    OP(TE, "tensor_scalar", Ci.t[:, k + 1, :], ctmp.t[:, 2, :], 2.0, None, ALU.mult, R=[ctmp], W=[Ci])
                OP(TE, "tensor_scalar", Kr.t[:], Cr.t[:, 8, :], flag.t[:, 0:1], None, ALU.mult, R=[Cr, flag], W=[Kr])
                OP(TE, "tensor_scalar", Ki.t[:], Ci.t[:, 8, :], flag.t[:, 0:1], None, ALU.mult, R=[Ci, flag], W=[Ki])
                OP(TE, "tensor_scalar", nKi.t[:], Ki.t[:], -1.0, None, ALU.mult, R=[Ki], W=[nKi])
                OP(TE, "tensor_copy", c255.t[:], Tr.t[:, :, 255], R=[Tr], W=[c255])
                OP(TE, "tensor_copy", s255.t[:], Ti.t[:, :, 255], R=[Ti], W=[s255])
                OP(TE, "tensor_scalar", ns255.t[:], s255.t[:], -1.0, None, ALU.mult, R=[s255], W=[ns255])
                OP(TE, "tensor_copy", Trb_.t[:], Tr.t[:], R=[Tr], W=[Trb_])
                OP(TE, "tensor_copy", Tib_.t[:], Ti.t[:], R=[Ti], W=[Tib_])
                if l == 0 and ct == 0:
                    tap("Tr", Tr.t[:], [128, 8, 256], [Tr])
                    tap("Ti", Ti.t[:], [128, 8, 256], [Ti])

            def scan_ct(ct):
                Bbf, CTp, Tr, Ti, Cr, Ci, ctmp, Kr, Ki, nKi, c255, s255, ns255 = unpack(ct)
                pipe = pipe_[0]
                first = True

                def emit_bu(dr, st_, pp_):
                    for ri in range(2):
                        for half in range(2):
                            MM(P[ri], P[ri].t[:, half * 512:(half + 1) * 512], Bbf, Bbf.t[:, dr, st_, ri, :],
                               uT.k[ct], uT.t[:, ct, half * 512:(half + 1) * 512])
                    for ri in range(2):
                        src = v3(P[ri].t[:])
                        if dr == 1:
                            src = src[:, :, ::-1]
                        ACT(merged.k[pp_ * 4 + ri], v3(bfv(pp_ * 4 + ri)), P[ri], src, AF.Copy)

                pairs = [(dr, st_) for dr in range(2) for st_ in range(4)]
                emit_bu(pairs[0][0], pairs[0][1], pipe)
                for ip_, (dr, st_) in enumerate(pairs):
                    if True:
                        q = dr * 4 + st_
                        tp = ct * 4 + st_
                        E = "dve"
                        wk = [merged.k[pipe * 4 + i] for i in range(4)]
                        wa = [bfv(pipe * 4 + i) for i in range(4)]
                        hr, hi = HB[pipe * 2], HB[pipe * 2 + 1]
                        pipe = (pipe + 1) % 2
                        BUr, BUi, TA, TB_ = wk
                        bur, bui, ta, tb = wa
                        if ip_ + 1 < len(pairs):
                            emit_bu(pairs[ip_ + 1][0], pairs[ip_ + 1][1], pipe)
                        cosb = Tr.t[:, q:q + 1, :].broadcast_to([128, 4, 256])
                        sinb = Ti.t[:, q:q + 1, :].broadcast_to([128, 4, 256])
                        OP(E, "tensor_tensor", v3(ta), v3(bur), cosb, ALU.mult, R=[BUr, Tr], W=[TA])
                        OP(E, "tensor_tensor", v3(tb), v3(bui), sinb, ALU.mult, R=[BUi, Ti], W=[TB_])
                        OP(E, "tensor_tensor", ta, ta, tb, ALU.add, R=[TA, TB_], W=[TA])
                        OP(E, "tensor_tensor", v3(tb), v3(bui), cosb, ALU.mult, R=[BUi, Tr], W=[TB_])
                        OP(E, "tensor_tensor", v3(bur), v3(bur), sinb, ALU.mult, R=[BUr, Ti], W=[BUr])
                        OP(E, "tensor_tensor", tb, tb, bur, ALU.subtract, R=[TB_, BUr], W=[TB_])
                        order = [0, 1, 2, 3] if dr == 0 else [3, 2, 1, 0]
                        rcol = rho3[:, dr, tp:tp + 1]
                        rb = rcol.broadcast_to([128, 256])
                        for idx, sg in enumerate(order):
                            sl = slice(sg * 256, (sg + 1) * 256)
                            if idx == 0:
                                ir = g0r.t[:, dr * 16 + tp:dr * 16 + tp + 1]
                                ii = g0i.t[:, dr * 16 + tp:dr * 16 + tp + 1]
                                RI = [g0r, g0i]
                            else:
                                ir = carry.t[:, pipe, 0, idx:idx + 1]
                                ii = carry.t[:, pipe, 1, idx:idx + 1]
                                RI = [carry]
                            OP(E, "tensor_tensor_scan", bur[:, sl], rb, ta[:, sl], ir, ALU.mult, ALU.add,
                               R=[rho, TA] + RI, W=[BUr])
                            OP(E, "tensor_tensor_scan", bui[:, sl], rb, tb[:, sl], ii, ALU.mult, ALU.add,
                               R=[rho, TB_] + RI, W=[BUi])
                            if idx < 3:
                                last = sg * 256 + 255
                                glr = bur[:, last:last + 1]
                                gli = bui[:, last:last + 1]
                                nr = carry.t[:, pipe, 0, idx + 1:idx + 2]
                                ni = carry.t[:, pipe, 1, idx + 1:idx + 2]
                                OP(E, "tensor_scalar", nr, glr, Kr.t[:, q:q + 1], None, ALU.mult, R=[BUr, Kr], W=[carry])
                                OP(E, "scalar_tensor_tensor", nr, gli, nKi.t[:, q:q + 1], nr, ALU.mult, ALU.add,
                                   R=[BUi, nKi, carry], W=[carry])
                                OP(E, "tensor_scalar", ni, gli, Kr.t[:, q:q + 1], None, ALU.mult, R=[BUi, Kr], W=[carry])
                                OP(E, "scalar_tensor_tensor", ni, glr, Ki.t[:, q:q + 1], ni, ALU.mult, ALU.add,
                                   R=[BUr, Ki, carry], W=[carry])
                        g255r = v3(bur)[:, :, 255]
                        g255i = v3(bui)[:, :, 255]
                        fr_ = ns54[:, 0, dr, tp, :]
                        fi_ = ns54[:, 1, dr, tp, :]
                        OP(E, "tensor_scalar", fr_, g255r, c255.t[:, q:q + 1], None, ALU.mult, R=[BUr, c255], W=[ns5])
                        OP(E, "scalar_tensor_tensor", fr_, g255i, ns255.t[:, q:q + 1], fr_, ALU.mult, ALU.add,
                           R=[BUi, ns255, ns5], W=[ns5])
                        OP(E, "tensor_scalar", fi_, g255i, c255.t[:, q:q + 1], None, ALU.mult, R=[BUi, c255], W=[ns5])
                        OP(E, "scalar_tensor_tensor", fi_, g255r, s255.t[:, q:q + 1], fi_, ALU.mult, ALU.add,
                           R=[BUr, s255, ns5], W=[ns5])
                        ohr = v3(hr.t[:]) if dr == 0 else v3(hr.t[:])[:, :, ::-1]
                        ohi = v3(hi.t[:]) if dr == 0 else v3(hi.t[:])[:, :, ::-1]
                        OP(E, "tensor_tensor", v3(ta), v3(bur), cosb, ALU.mult, R=[BUr, Tr], W=[TA])
                        OP(E, "tensor_tensor", v3(tb), v3(bui), sinb, ALU.mult, R=[BUi, Ti], W=[TB_])
                        OP(E, "tensor_tensor", ohr, v3(ta), v3(tb), ALU.subtract, R=[TA, TB_], W=[hr])
                        OP(E, "tensor_tensor", v3(ta), v3(bur), sinb, ALU.mult, R=[BUr, Ti], W=[TA])
                        OP(E, "tensor_tensor", v3(tb), v3(bui), cosb, ALU.mult, R=[BUi, Tr], W=[TB_])
                        OP(E, "tensor_tensor", ohi, v3(ta), v3(tb), ALU.add, R=[TA, TB_], W=[hi])
                        if l == 0 and ct == 0 and st_ == 0:
                            tap("hr%d" % dr, hr.t[:], [128, 1024], [hr])
                        for ri, hh in ((0, hr), (1, hi)):
                            lastmm = (dr == 1 and st_ == 3 and ri == 1)
                            for half in range(2):
                                MM(P[3], P[3].t[:, half * 512:(half + 1) * 512], CTp, CTp.t[:, dr, st_, ri, :],
                                   hh, hh.t[:, half * 512:(half + 1) * 512], start=(first and ri == 0), stop=lastmm)
                        first = False
                sa = scA()
                STT("dve", sa, sa.t[:], uT.k[ct], uT.t[:, ct, :], s5d.t[:, ct:ct + 1], P[3], P[3].t[:],
                    ALU.mult, ALU.add, RS=[s5d])
                if ct == 0:
                    tap("ypre%d" % l, sa.t[:], [128, 1024], [sa])
                ACT(ya.k[ct], ya.t[:, ct, :], sa, sa.t[:], AF.Gelu)
                pipe_[0] = pipe

            build_ct(0)
            for ct in range(4):
                if ct + 1 < 4:
                    build_ct(ct + 1)
                scan_ct(ct)
            DMA("sp", O["ns5r"][:, l].rearrange("p d t s -> p (d t s)"), ns5.t[:, 0, :], R=[ns5])
            DMA("sp", O["ns5i"][:, l].rearrange("p d t s -> p (d t s)"), ns5.t[:, 1, :], R=[ns5])
            wgl = load_w(D["wglu"][l], 4, 512)
            wz = load_w(D["w_in"][l][:, OFF["z_a"]:OFF["z_a"] + 512], 8, 512)
            for m in range(4):
                pg_, pz_ = (P[0], P[1]) if m % 2 == 0 else (P[2], P[3])
                project(wgl, 4, m * 128, k_rhs(ya), pg_)
                project(wz, 8, m * 128, hn_rhs, pz_)
                g_ = scB()
                ACT(g_, g_.t[:], pg_, pg_.t[:], AF.Sigmoid)
                z_ = scB()
                ACT(z_, z_.t[:], pz_, pz_.t[:], AF.Silu)
                OP("dve", "tensor_tensor", g_.t[:], g_.t[:], ya.t[:, m, :], ALU.mult, R=[g_, ya.k[m]], W=[g_])
                OP("dve", "tensor_tensor", uT.t[:, m, :], g_.t[:], z_.t[:], ALU.mult, R=[g_, z_], W=[uT.k[m]])
            tap("outa%d" % l, uT.t[:, 0, :], [128, 1024], [uT.k[0]])
            merge_branch(l, 0, uT)
            fw.barrier()

    AX = mybir.AxisListType.X

    def branch_b(l):
        with ExitStack() as sb_:
            kT = sbt(sb_, "kTall", [128, 4, 1536], BF16, nsub=4)
            qT = sbt(sb_, "qTall", [128, 4, 1024], BF16, nsub=4)
            Vt = sbt(sb_, "Vt", [128, 12, 4, 129], BF16, nsub=12)
            szb = sbt(sb_, "szb", [128, 4, 1024], BF16, nsub=4)
            outb = sbt(sb_, "outb", [128, 4, 1024], BF16, nsub=4)
            ropec = sbt(sb_, "ropec", [128, 1024]); ropes = sbt(sb_, "ropes", [128, 1024])
            maskb = sbt(sb_, "maskb", [128, 48])
            dalam = sbt(sb_, "dalam", [128, 256]); gn = sbt(sb_, "gn", [128, 128])
            lamv = sbt(sb_, "lamv", [128, 4]); lt = sbt(sb_, "lt", [128, 128])
            EX = [sbt(sb_, "ex%d" % i, [128, 512], BF16) for i in range(3)]
            ACC = [sbt(sb_, "accS%d" % i, [128, 8, 160]) for i in range(2)]
            OB4 = [sbt(sb_, "ob4%d" % i, [128, 4, 128]) for i in range(2)]
            SS4 = [sbt(sb_, "ss4%d" % i, [128, 16]) for i in range(2)]
            lt4 = sbt(sb_, "lt4", [128, 4, 128])
            DMA("sp", ropec.t[:], D["ropec"], W=[ropec])
            DMA("sp", ropes.t[:], D["ropes"], W=[ropes])
            DMA("sp", maskb.t[:], D["maskb"], W=[maskb])
            DMA("sp", dalam.t[:], D["dalam"][l], W=[dalam])
            DMA("sp", gn.t[:], D["dang"][l], W=[gn])
            TT("dve", lt, lt.t[:, 0:64], dalam, dalam.t[:, 0:64], dalam, dalam.t[:, 64:128], ALU.mult)
            TT("dve", lt, lt.t[:, 64:128], dalam, dalam.t[:, 128:192], dalam, dalam.t[:, 192:256], ALU.mult)
            OP("dve", "tensor_reduce", lamv.t[:, 0:1], lt.t[:, 0:64], AX, ALU.add, R=[lt], W=[lamv])
            OP("dve", "tensor_reduce", lamv.t[:, 1:2], lt.t[:, 64:128], AX, ALU.add, R=[lt], W=[lamv])
            ACT(lamv, lamv.t[:, 0:2], lamv, lamv.t[:, 0:2], AF.Exp)
            TT("dve", lamv, lamv.t[:, 2:3], lamv, lamv.t[:, 1:2], lamv, lamv.t[:, 0:1], ALU.subtract)
            TS("dve", lamv, lamv.t[:, 2:3], lamv, lamv.t[:, 2:3], -LAM_INIT[l], None, ALU.add)
            OP("pool", "memset", Vt.t[:, :, :, 128:129], 1.0, W=Vt.k)
            lnc = sbt(sb_, "lnc", [128, 1])
            OP("pool", "memset", lnc.t[:], float(np.log(1.0 - LAM_INIT[l])), W=[lnc])

            for which, dst, c0 in (("q", qT, 0), ("k", kT, 512)):
                wa_ = load_w(D["w_in"][l][:, OFF["q_b"] + c0:OFF["q_b"] + c0 + 512], 8, 512)
                wp_ = load_w(D["w_qkp"][l][:, c0:c0 + 512], 8, 512)
                for h in range(4):
                    pa_, pb_ = (P[0], P[1]) if h % 2 == 0 else (P[2], P[3])
                    project(wa_, 8, h * 128, hn_rhs, pa_)
                    project(wp_, 8, h * 128, hn_rhs, pb_)
                    s1, s2 = scA(), scA()
                    TT("dve", s1, s1.t[:], pa_, pa_.t[:], ropec, ropec.t[:], ALU.mult)
                    TT("dve", s2, s2.t[:], pb_, pb_.t[:], ropes, ropes.t[:], ALU.mult)
                    if which == "q":
                        TT("dve", dst.k[h], dst.t[:, h, :], s1, s1.t[:], s2, s2.t[:], ALU.add)
                    else:
                        TT("dve", s1, s1.t[:], s1, s1.t[:], s2, s2.t[:], ALU.add)
                        DMA("sp", O["newk"][:, l, h, :], s1.t[:], R=[s1])
                        ACT(dst.k[h], dst.t[:, h, 0:1024], s1, s1.t[:], AF.Copy)
                        DMA("pool", dst.t[:, h, 1024:1536], D["kctx"][:, l, h, :], W=[dst.k[h]])
                        if h == 0:
                            tap("krot%d" % l, s1.t[:], [128, 1024], [s1])
            wv_ = load_w(D["w_in"][l][:, OFF["v_b"]:OFF["v_b"] + 512], 8, 512)
            wz_ = load_w(D["w_in"][l][:, OFF["z_b"]:OFF["z_b"] + 512], 8, 512)
            for h in range(4):
                project(wv_, 8, h * 128, hn_rhs, P[2])
                s1 = scA()
                ACT(s1, s1.t[:], P[2], P[2].t[:], AF.Copy)
                DMA("sp", O["newv"][:, l, h, :], s1.t[:], R=[s1])
                for tt in range(8):
                    OP("pe", "transpose", P[3].t[:, tt * 128:(tt + 1) * 128], s1.t[:, tt * 128:(tt + 1) * 128], ident.t[:],
                       R=[s1, ident], W=[P[3]])
                OP("act", "activation", Vt.t[:, 0:8, h, 0:128], P[3].t[:].rearrange("p (a e) -> p a e", a=8), AF.Copy,
                   R=[P[3]], W=Vt.k[0:8])
                DMA("pool", Vt.t[:, 8:12, h, 0:128], D["vctx"][:, l, :, h * 128:(h + 1) * 128], W=Vt.k[8:12])
                project(wz_, 8, h * 128, hn_rhs, P[h % 2])
                ACT(szb.k[h], szb.t[:, h, :], P[h % 2], P[h % 2].t[:], AF.Silu)
            fw.barrier()
            PH = [TV(P[i // 2].t, "PH%d" % i) for i in range(4)]
            PT3 = TV(P[3].t, "PT3")
            blk_ = 0
            sci_ = 0
            exi = 0
            scale = 64 ** -0.5
            for h in range(4):
                for qb in range(2):
                    def acc(qt, m):
                        s = qt * 2 + m
                        bank, pos = s // 3, s % 3
                        pt = P[2] if bank < 2 else P[3]
                        off = (bank % 2) * 512 + pos * 160
                        return pt, pt.t[:, off:off + 129]
                    tiles = [(m, kt) for m in range(2) for kt in range(12)]

                    def emit_score(m, kt):
                        nonlocal sci_
                        ph = PH[sci_ % 4]
                        sc_ap = ph.t[:, (sci_ % 2) * 512:(sci_ % 2) * 512 + 512]
                        sci_ += 1
                        MM(ph, sc_ap, kT.k[h], kT.t[m * 64:(m + 1) * 64, h, kt * 128:(kt + 1) * 128],
                           qT.k[h], qT.t[m * 64:(m + 1) * 64, h, qb * 512:(qb + 1) * 512])
                        return ph, sc_ap

                    nxt = emit_score(*tiles[0])
                    for it_, (m, kt) in enumerate(tiles):
                        ph, sc_ap = nxt
                        if it_ + 1 < len(tiles):
                            nxt = emit_score(*tiles[it_ + 1])
                        ex = EX[exi % 3]
                        exi += 1
                        for sgl in range(2):
                            seg = qb * 2 + sgl
                            OP("act", "activation", ex.t[:, sgl * 256:(sgl + 1) * 256],
                               sc_ap[:, sgl * 256:(sgl + 1) * 256], AF.Exp,
                               bias=maskb.t[:, kt * 4 + seg:kt * 4 + seg + 1], scale=scale,
                               R=[ph, maskb], W=[ex])
                        for qt in range(4):
                            pt, aap = acc(qt, m)
                            first_in_bank = (qt * 2 + m) in (0, 4, 6, 1, 3, 7)
                            fw.op("pe", lambda e, aap=aap, ex=ex, qt=qt, kt=kt, fb=first_in_bank: e.matmul(
                                aap, lhsT=ex.t[:, qt * 128:(qt + 1) * 128], rhs=Vt.t[:, kt, h, :],
                                start=(kt == 0 and fb), stop=(kt == 11), skip_group_check=True),
                                reads=[ex.b, Vt.k[kt].b], writes=[pt.b])
                    accS = ACC[blk_ % 2]
                    o4 = OB4[blk_ % 2]
                    ss = SS4[blk_ % 2]
                    blk_ += 1
                    OP("dve", "tensor_copy", accS.t[:, 0:3, :], P[2].t[:, 0:480].rearrange("p (a e) -> p a e", a=3),
                       R=[P[2]], W=[accS])
                    OP("dve", "tensor_copy", accS.t[:, 3:6, :], P[2].t[:, 512:992].rearrange("p (a e) -> p a e", a=3),
                       R=[P[2]], W=[accS])
                    OP("dve", "tensor_copy", accS.t[:, 6:8, :], P[3].t[:, 0:320].rearrange("p (a e) -> p a e", a=2),
                       R=[P[3]], W=[accS])
                    OP("dve", "reciprocal", ss.t[:, 0:8], accS.t[:, :, 128], R=[accS], W=[ss])
                    TS("dve", ss, ss.t[:, 1:8:2], ss, ss.t[:, 1:8:2], lamv.t[:, 2:3], None, ALU.mult, RS=[lamv])
                    TT("dve", o4, o4.t[:], accS, accS.t[:, 0:8:2, 0:128], ss, col3(ss.t[:, 0:8:2], 128), ALU.mult)
                    TT("dve", lt4, lt4.t[:], accS, accS.t[:, 1:8:2, 0:128], ss, col3(ss.t[:, 1:8:2], 128), ALU.mult)
                    TT("dve", o4, o4.t[:], o4, o4.t[:], lt4, lt4.t[:], ALU.add)
                    TT("dve", lt4, lt4.t[:], o4, o4.t[:], o4, o4.t[:], ALU.mult)
                    OP("dve", "tensor_reduce", ss.t[:, 8:12], lt4.t[:], AX, ALU.add, R=[lt4], W=[ss])
                    TS("dve", ss, ss.t[:, 8:12], ss, ss.t[:, 8:12], 1.0 / 128, EPS, ALU.mult, ALU.add)
                    ACT(ss, ss.t[:, 8:12], ss, ss.t[:, 8:12], AF.Ln)
                    ACT(ss, ss.t[:, 12:16], ss, ss.t[:, 8:12], AF.Exp, RS=[lnc], scale=-0.5, bias=lnc.t[:, 0:1])
                    TT("dve", o4, o4.t[:], o4, o4.t[:], ss, col3(ss.t[:, 12:16], 128), ALU.mult)
                    TT("dve", o4, o4.t[:], o4, o4.t[:], gn,
                       gn.t[:].rearrange("p (o e) -> p o e", o=1).broadcast_to([128, 4, 128]), ALU.mult)
                    for qt in range(4):
                        OP("pe", "transpose", P[3].t[:, 512 + qt * 128:512 + (qt + 1) * 128], o4.t[:, qt, :], ident.t[:],
                           R=[o4, ident], W=[PT3])
                    TT("dve", outb.k[h], outb.t[:, h, qb * 512:(qb + 1) * 512], PT3, P[3].t[:, 512:1024],
                       szb.k[h], szb.t[:, h, qb * 512:(qb + 1) * 512], ALU.mult)
            tap("outb%d" % l, outb.t[:, 0, :], [128, 1024], [outb.k[0]])
            fw.barrier()
            merge_branch(l, 1, outb)
            fw.barrier()

    def branch_c(l):
        with ExitStack() as sc_:
            outc = sbt(sc_, "outc", [128, 4, 1024], BF16, nsub=4)
            cw = sbt(sc_, "cw", [128, 60])
            beta_t = sbt(sc_, "beta_t", [128, 8, 8]); g_t = sbt(sc_, "g_t", [128, 8, 8])
            alog = sbt(sc_, "alog", [128, 8]); dtb = sbt(sc_, "dtb", [128, 8]); gnc = sbt(sc_, "gnc", [128, 128])
            xp = sbt(sc_, "xp", [128, 4, 260])
            qf = sbt(sc_, "qf", [128, 1024]); kf = sbt(sc_, "kf", [128, 1024])
            k_tok = sbt(sc_, "k_tok", [128, 1024]); v_tok = sbt(sc_, "v_tok", [128, 1024])
            o_acc = sbt(sc_, "o_acc", [128, 1024])
            PTm = sbt(sc_, "PTm", [128, 1024]); RT = sbt(sc_, "RTm", [128, 1024])
            Dm = [sbt(sc_, "Dm%d" % d, [128, 1024]) for d in range(2)]
            DT = [sbt(sc_, "DT%d" % d, [128, 1024]) for d in range(2)]
            U = [sbt(sc_, "U%d" % d, [128, 1024]) for d in range(2)]
            WT = [sbt(sc_, "WT%d" % d, [128, 1024]) for d in range(2)]
            sml = [{n_: sbt(sc_, "%s%d" % (n_, d), [128, 8]) for n_ in ("gcol", "egc", "nbeta", "bge", "kds")}
                   for d in range(2)]
            egl = [sbt(sc_, "egl%d" % d, [128, 16]) for d in range(2)]
            S = [[sbt(sc_, "S%d%d" % (d, i), [128, 128]) for i in range(2)] for d in range(2)]
            VN = [[sbt(sc_, "VN%d%d" % (d, i), [128, 128]) for i in range(1)] * 2 for d in range(2)]
            otmp = [[sbt(sc_, "otmp%d%d" % (d, i), [128, 128]) for i in range(1)] * 2 for d in range(2)]
            rs8 = sbt(sc_, "rs8", [128, 8])
            gcr, Pm = SCA[0], rstd
            X1, X2 = SCA[1], SCA[2]
            epsc = sbt(sc_, "epsc", [128, 3])
            OP("pool", "memset", epsc.t[:, 0:1], EPS, W=[epsc])
            OP("pool", "memset", epsc.t[:, 1:2], 0.0, W=[epsc])
            OP("pool", "memset", epsc.t[:, 2:3], float(np.log(128 ** -0.5)), W=[epsc])
            DMA("sp", cw.t[:], D["dnconv"][l].rearrange("p j k -> p (j k)"), W=[cw])
            DMA("sp", alog.t[:], D["dnalog"][l], W=[alog])
            DMA("sp", dtb.t[:], D["dndt"][l], W=[dtb])
            DMA("sp", gnc.t[:], D["dnng"][l], W=[gnc])

            def r3(ap):
                return ap.rearrange("p (a e) -> p a e", a=8)

            def bc_tt(ap128):
                return ap128.rearrange("p (o e) -> p o e", o=1).broadcast_to([128, 8, 128])

            wba = load_w(D["w_in"][l][:, OFF["beta"]:OFF["beta"] + 16], 8, 16)
            for tt in range(8):
                for kt in range(8):
                    MM(P[0], P[0].t[:, tt * 16:(tt + 1) * 16], hnT.k[kt], hnT.t[:, kt, tt * 128:(tt + 1) * 128],
                       wba, wba.t[:, kt, 0:16], start=(kt == 0), stop=(kt == 7))
            pb = P[0].t[:, 0:128].rearrange("p (a c) -> p a c", a=8)
            ACT(beta_t, beta_t.t[:], P[0], pb[:, :, 0:8], AF.Sigmoid)
            dtbB = dtb.t[:].rearrange("p (o c) -> p o c", o=1).broadcast_to([128, 8, 8])
            TT("dve", g_t, g_t.t[:], P[0], pb[:, :, 8:16], dtb, dtbB, ALU.add)
            ACT(g_t, g_t.t[:], g_t, g_t.t[:], AF.Exp)
            ACT(g_t, g_t.t[:], g_t, g_t.t[:], AF.Ln, bias=1.0)
            ACT(alog, alog.t[:], alog, alog.t[:], AF.Exp)
            TS("dve", alog, alog.t[:], alog, alog.t[:], -1.0, None, ALU.mult)
            alB = alog.t[:].rearrange("p (o c) -> p o c", o=1).broadcast_to([128, 8, 8])
            TT("dve", g_t, g_t.t[:], g_t, g_t.t[:], alog, alB, ALU.mult)
            tap("g_t%d" % l, g_t.t[:], [128, 8, 8], [g_t])
            tap("beta_t%d" % l, beta_t.t[:], [128, 8, 8], [beta_t])
            if stop_after == "c0":
                halt[0] = True
                return


            QF = [qf, sbt(sc_, "qf1", [128, 1024])]

            def conv_phase(h):
                th = []
                wh = wb_next()
                for i_, nm_ in enumerate(("q_c", "k_c", "v_c", "z_c")):
                    c0_ = OFF[nm_] + h * 128
                    load_into(wh, i_ * 128, D["w_in"][l][:, c0_:c0_ + 128], 8, 128)
                qdst = QF[h % 2]
                tmp, l2t = X1, gcr

                def conv_tile(m, j):
                    pp = P[2]
                    a_ = X2
                    th.append(lambda: project(wh, 8, m * 128, hn_rhs, pp))
                    th.append(lambda: ACT(xp, xp.t[:, :, 2:258], pp, v3(pp.t[:]), AF.Copy))

                    def halo():
                        OP("dve", "memset", xp.t[:, 0, 0:2], 0.0, W=[xp])
                        OP("dve", "memset", xp.t[:, 3, 258:260], 0.0, W=[xp])
                        OP("dve", "tensor_scalar", xp.t[:, 1:4, 0:2], xp.t[:, 0:3, 256:258], flag.t[:, 0:1], None,
                           ALU.mult, R=[xp, flag], W=[xp])
                        OP("dve", "tensor_scalar", xp.t[:, 0:3, 258:260], xp.t[:, 1:4, 2:4], flag.t[:, 0:1], None,
                           ALU.mult, R=[xp, flag], W=[xp])
                    th.append(halo)
                    th.append(lambda: TS("dve", a_, v3(a_.t[:]), xp, xp.t[:, :, 0:256], cw.t[:, j * 5:j * 5 + 1], None,
                                         ALU.mult, RS=[cw]))
                    for k in range(1, 5):
                        th.append(lambda k=k: STT("dve", a_, v3(a_.t[:]), xp, xp.t[:, :, k:k + 256],
                                                  cw.t[:, j * 5 + k:j * 5 + k + 1], a_, v3(a_.t[:]), ALU.mult, ALU.add,
                                                  RS=[cw]))
                    th.append(lambda: ACT(tmp, tmp.t[:], a_, a_.t[:], AF.Silu))

                def l2n(dstT, scl):
                    def f1():
                        sb_ = scB()
                        ACT(sb_, sb_.t[:], tmp, tmp.t[:], AF.Square)
                        for half in range(2):
                            MM(P[3], P[3].t[:, half * 512:(half + 1) * 512], onesb, onesb.t[:], sb_,
                               sb_.t[:, half * 512:(half + 1) * 512])
                    th.append(f1)
                    th.append(lambda: ACT(l2t, l2t.t[:], P[3], P[3].t[:], AF.Ln, RS=[epsc], bias=epsc.t[:, 0:1]))
                    th.append(lambda: ACT(l2t, l2t.t[:], l2t, l2t.t[:], AF.Exp, RS=[epsc], scale=-0.5,
                                          bias=epsc.t[:, 1:2] if scl == 1.0 else epsc.t[:, 2:3]))
                    th.append(lambda: TT("dve", dstT, dstT.t[:], tmp, tmp.t[:], l2t, l2t.t[:], ALU.mult))

                conv_tile(0, h)
                l2n(qdst, 128 ** -0.5)
                conv_tile(1, 4 + h)
                l2n(kf, 1.0)

                def ktr():
                    for tt in range(8):
                        OP("pe", "transpose", P[3].t[:, tt * 128:(tt + 1) * 128], kf.t[:, tt * 128:(tt + 1) * 128],
                           ident.t[:], R=[kf, ident], W=[P[3]])
                    ACT(k_tok, k_tok.t[:], P[3], P[3].t[:], AF.Copy)
                th.append(ktr)
                conv_tile(2, 8 + h)

                def vtr():
                    for tt in range(8):
                        OP("pe", "transpose", P[3].t[:, tt * 128:(tt + 1) * 128], tmp.t[:, tt * 128:(tt + 1) * 128],
                           ident.t[:], R=[tmp, ident], W=[P[3]])
                    ACT(v_tok, v_tok.t[:], P[3], P[3].t[:], AF.Copy)
                th.append(vtr)
                return th, wh

            pending, wh_next = conv_phase(0)
            for h in range(4):
                for f_ in pending:
                    f_()
                pending = []
                wh = wh_next
                qf = QF[h % 2]
                if h == 0:
                    tap("qf%d" % l, qf.t[:], [128, 1024], [qf])
                    tap("kf%d" % l, kf.t[:], [128, 1024], [kf])
                    tap("vtok%d" % l, v_tok.t[:], [128, 1024], [v_tok])

                for dr in range(2):
                    cb = dr * 4 + h
                    sm_ = sml[dr]
                    gcol, egc, nbeta, bge, kds = (sm_[n_] for n_ in ("gcol", "egc", "nbeta", "bge", "kds"))
                    TRI = cm.t[:, dr, :]
                    MASK = cm.t[:, 1 - dr, :]
                    for tt in range(8):
                        OP("act", "activation", X1.t[:, tt * 128:(tt + 1) * 128], cm.t[:, 2, :], AF.Copy,
                           scale=g_t.t[:, tt, cb:cb + 1], R=[cm, g_t], W=[X1])
                    for tt in range(8):
                        MM(P[0], P[0].t[:, tt * 128:(tt + 1) * 128], X1, X1.t[:, tt * 128:(tt + 1) * 128], cm, TRI)
                        MM(P[1], P[1].t[:, tt:tt + 1], cm, TRI, g_t, g_t.t[:, tt, cb:cb + 1])
                        MM(P[2], P[2].t[:, tt * 128:(tt + 1) * 128], kf, kf.t[:, tt * 128:(tt + 1) * 128],
                           kf, kf.t[:, tt * 128:(tt + 1) * 128])
                    ACT(gcr, gcr.t[:], P[0], P[0].t[:], AF.Copy)
                    OP("dve", "tensor_copy", gcol.t[:], P[1].t[:, 0:8], R=[P[1]], W=[gcol])
                    ACT(egc, egc.t[:], gcol, gcol.t[:], AF.Exp)
                    TS("dve", nbeta, nbeta.t[:], beta_t, beta_t.t[:, :, cb], -1.0, None, ALU.mult)
                    TT("dve", bge, bge.t[:], beta_t, beta_t.t[:, :, cb], egc, egc.t[:], ALU.mult)
                    if stop_after == "c2a0":
                        halt[0] = True
                        return
                    D_ = Dm[dr]
                    TT("dve", D_, r3(D_.t[:]), gcol, col3(gcol.t[:], 128), gcr, r3(gcr.t[:]), ALU.subtract)
                    TS("dve", D_, D_.t[:], D_, D_.t[:], 0.0, None, ALU.min)
                    ACT(D_, D_.t[:], D_, D_.t[:], AF.Exp)
                    TT("dve", D_, r3(D_.t[:]), D_, r3(D_.t[:]), cm, bc_tt(MASK), ALU.mult)
                    TT("dve", X2, r3(X2.t[:]), D_, r3(D_.t[:]), ident, bc_tt(ident.t[:]), ALU.subtract)
                    TT("dve", Pm, r3(Pm.t[:]), P[2], r3(P[2].t[:]), nbeta, col3(nbeta.t[:], 128), ALU.mult)
                    TT("dve", Pm, Pm.t[:], Pm, Pm.t[:], X2, X2.t[:], ALU.mult)
                    if stop_after == "c2a1":
                        halt[0] = True
                        return
                    for tt in range(8):
                        OP("pe", "transpose", P[3].t[:, tt * 128:(tt + 1) * 128], Pm.t[:, tt * 128:(tt + 1) * 128],
                           ident.t[:], R=[Pm, ident], W=[P[3]])
                    ACT(PTm, PTm.t[:], P[3], P[3].t[:], AF.Copy)
                    if stop_after == "c2a2":
                        halt[0] = True
                        return
                    TT("dve", RT, r3(RT.t[:]), PTm, r3(PTm.t[:]), ident, bc_tt(ident.t[:]), ALU.add)
                    if stop_after == "c2a3":
                        halt[0] = True
                        return
                    for tt in range(8):
                        OP("pe", "transpose", P[0].t[:, tt * 128:(tt + 1) * 128], D_.t[:, tt * 128:(tt + 1) * 128],
                           ident.t[:], R=[D_, ident], W=[P[0]])
                    ACT(DT[dr], DT[dr].t[:], P[0], P[0].t[:], AF.Copy)
                    if stop_after == "c2a":
                        halt[0] = True
                        return
                    for k in range(1, 6):
                        for tt in range(8):
                            sl = slice(tt * 128, (tt + 1) * 128)
                            MM(P[0], P[0].t[:, sl], PTm, PTm.t[:, sl], Pm, Pm.t[:, sl])
                            if k < 5:
                                MM(P[1], P[1].t[:, sl], Pm, Pm.t[:, sl], PTm, PTm.t[:, sl])
                        ACT(Pm, Pm.t[:], P[0], P[0].t[:], AF.Copy)
                        if k < 5:
                            OP("dve", "tensor_copy", PTm.t[:], P[1].t[:], R=[P[1]], W=[PTm])
                        for tt in range(8):
                            sl = slice(tt * 128, (tt + 1) * 128)
                            MM(P[2], P[2].t[:, sl], Pm, Pm.t[:, sl], RT, RT.t[:, sl])
                        TT("dve", RT, RT.t[:], RT, RT.t[:], P[2], P[2].t[:], ALU.add)
                    if stop_after == "c2b":
                        halt[0] = True
                        return
                    TT("dve", X1, r3(X1.t[:]), v_tok, r3(v_tok.t[:]), beta_t, col3(beta_t.t[:, :, cb], 128), ALU.mult)
                    TT("dve", X2, r3(X2.t[:]), k_tok, r3(k_tok.t[:]), bge, col3(bge.t[:], 128), ALU.mult)
                    for tt in range(8):
                        sl = slice(tt * 128, (tt + 1) * 128)
                        MM(P[0], P[0].t[:, sl], RT, RT.t[:, sl], X1, X1.t[:, sl])
                        MM(P[1], P[1].t[:, sl], X2, X2.t[:, sl], RT, RT.t[:, sl])
                        MM(P[2], P[2].t[:, sl], kf, kf.t[:, sl], qf, qf.t[:, sl])
                    ACT(U[dr], U[dr].t[:], P[0], P[0].t[:], AF.Copy)
                    ACT(WT[dr], WT[dr].t[:], P[1], P[1].t[:], AF.Copy)
                    TT("dve", DT[dr], DT[dr].t[:], P[2], P[2].t[:], DT[dr], DT[dr].t[:], ALU.mult)
                    g3 = r3(gcr.t[:])
                    for hp in (0, 64):
                        lc = hp + 63 if dr == 0 else hp
                        TT("dve", kds, kds.t[hp:hp + 64, :], gcr, g3[hp:hp + 64, :, lc], gcol, gcol.t[hp:hp + 64, :],
                           ALU.subtract)
                    ACT(kds, kds.t[:], kds, kds.t[:], AF.Exp)
                    lc0 = 63 if dr == 0 else 0
                    ACT(egl[dr], egl[dr].t[:].rearrange("p (a c) -> p a c", a=8), gcr, g3[:, :, lc0::64], AF.Exp)
                    TT("dve", D_, r3(D_.t[:]), k_tok, r3(k_tok.t[:]), kds, col3(kds.t[:], 128), ALU.mult)
                    if h == 0 and dr == 0:
                        tap("U%d" % l, U[0].t[:], [128, 1024], [U[0]])
                        tap("RT%d" % l, RT.t[:], [128, 1024], [RT])
                if stop_after == "c2":
                    halt[0] = True
                    return
                OP("pool", "memset", o_acc.t[:], 0.0, W=[o_acc])
                if h + 1 < 4:
                    pending, wh_next = conv_phase(h + 1)
                per_step = (len(pending) + 15) // 16
                cur = [0, 0]
                for dr in range(2):
                    DMA("sp", S[dr][0].t[:], D["dns0"][:, l, dr, h, :], W=[S[dr][0]])
                for step in range(16):
                    for dr in range(2):
                        c = step if dr == 0 else 15 - step
                        tt, hp, ci = c // 2, (c % 2) * 64, c % 2
                        sm_ = sml[dr]
                        egc = sm_["egc"]
                        Sc = S[dr][cur[dr]]
                        Sn = S[dr][1 - cur[dr]]
                        boundary = (dr == 0 and c % 4 == 0 and c > 0) or (dr == 1 and c % 4 == 3 and c < 15)
                        if boundary:
                            seq = c // 4 - 1 if dr == 0 else c // 4 + 1
                            DMA("sp", O["ndn"][:, l, dr, h, seq, :], Sc.t[:], R=[Sc])
                            TS("dve", Sn, Sn.t[:], Sc, Sc.t[:], flag.t[:, 0:1], None, ALU.mult, RS=[flag])
                            cur[dr] = 1 - cur[dr]
                            Sc, Sn = Sn, Sc
                        pd = P[dr]
                        pc = (step % 2) * 512
                        vn = VN[dr][step % 2]
                        ot = otmp[dr][step % 2]
                        MM(pd, pd.t[hp:hp + 64, pc:pc + 128], WT[dr], WT[dr].t[:, tt * 128 + hp:tt * 128 + hp + 64], Sc, Sc.t[:])
                        MM(pd, pd.t[hp:hp + 64, pc + 128:pc + 256], qf, qf.t[:, c * 64:(c + 1) * 64], Sc, Sc.t[:])
                        TT("dve", vn, vn.t[hp:hp + 64, :], U[dr], U[dr].t[hp:hp + 64, tt * 128:(tt + 1) * 128],
                           pd, pd.t[hp:hp + 64, pc:pc + 128], ALU.subtract)
                        MM(pd, pd.t[hp:hp + 64, pc + 256:pc + 384],
                           DT[dr], DT[dr].t[hp:hp + 64, tt * 128 + hp:tt * 128 + hp + 64], vn, vn.t[hp:hp + 64, :])
                        MM(pd, pd.t[:, pc + 384:pc + 512], Dm[dr], Dm[dr].t[hp:hp + 64, tt * 128:(tt + 1) * 128],
                           vn, vn.t[hp:hp + 64, :])
                        TS("dve", ot, ot.t[hp:hp + 64, :], pd, pd.t[hp:hp + 64, pc + 128:pc + 256],
                           egc.t[hp:hp + 64, tt:tt + 1], None, ALU.mult, RS=[egc])
                        TT("dve", ot, ot.t[hp:hp + 64, :], ot, ot.t[hp:hp + 64, :], pd, pd.t[hp:hp + 64, pc + 256:pc + 384],
                           ALU.add)
                        TT("pool", o_acc, o_acc.t[hp:hp + 64, tt * 128:(tt + 1) * 128],
                           o_acc, o_acc.t[hp:hp + 64, tt * 128:(tt + 1) * 128], ot, ot.t[hp:hp + 64, :], ALU.add)
                        STT("dve", Sn, Sn.t[:], Sc, Sc.t[:], egl[dr].t[:, tt * 2 + ci:tt * 2 + ci + 1],
                            pd, pd.t[:, pc + 384:pc + 512], ALU.mult, ALU.add, RS=[egl[dr]])
                        cur[dr] = 1 - cur[dr]
                    for _ in range(per_step):
                        if pending:
                            pending.pop(0)()
                for dr in range(2):
                    seq = 3 if dr == 0 else 0
                    Sc = S[dr][cur[dr]]
                    DMA("sp", O["ndn"][:, l, dr, h, seq, :], Sc.t[:], R=[Sc])
                if h == 0:
                    tap("oacc%d" % l, o_acc.t[:], [128, 1024], [o_acc])
                if stop_after == "c3":
                    halt[0] = True
                    return
                TT("dve", X1, X1.t[:], o_acc, o_acc.t[:], o_acc, o_acc.t[:], ALU.mult)
                OP("dve", "tensor_reduce", rs8.t[:], r3(X1.t[:]), AX, ALU.add, R=[X1], W=[rs8])
                TS("dve", rs8, rs8.t[:], rs8, rs8.t[:], 1.0 / 128, EPS, ALU.mult, ALU.add)
                OP("dve", "reciprocal", rs8.t[:], rs8.t[:], R=[rs8], W=[rs8])
                ACT(rs8, rs8.t[:], rs8, rs8.t[:], AF.Sqrt)
                TT("dve", X1, r3(X1.t[:]), o_acc, r3(o_acc.t[:]), rs8, col3(rs8.t[:], 128), ALU.mult)
                TT("dve", X1, r3(X1.t[:]), X1, r3(X1.t[:]), gnc, bc_tt(gnc.t[:]), ALU.mult)
                for tt in range(8):
                    OP("pe", "transpose", P[2].t[:, tt * 128:(tt + 1) * 128], X1.t[:, tt * 128:(tt + 1) * 128], ident.t[:],
                       R=[X1, ident], W=[P[2]])
                project(wh, 8, 3 * 128, hn_rhs, P[3])
                z_ = scB()
                ACT(z_, z_.t[:], P[3], P[3].t[:], AF.Silu)
                TT("dve", outc.k[h], outc.t[:, h, :], P[2], P[2].t[:], z_, z_.t[:], ALU.mult)
            tap("outc%d" % l, outc.t[:, 0, :], [128, 1024], [outc.k[0]])
            merge_branch(l, 2, outc)
            fw.barrier()

    for l in range(nlayers):
        norm_modulate(l)
        if stop_after == "norm":
            break
        if "a" not in skip:
            branch_a(l)
        if stop_after == "a":
            break
        if "b" not in skip:
            branch_b(l)
        if stop_after == "b":
            break
        branch_c(l)
        if stop_after == "c" or halt[0]:
            break
        out_proj_residual(l)
    rms_stats()
    fng = sbt(top, "fng", [128, 8])
    DMA("sp", fng.t[:], D["fng"], W=[fng])
    for kt in range(8):
        sa = scA()
        OP("dve", "scalar_tensor_tensor", sa.t[:], xT.t[:, kt, :], fng.t[:, kt:kt + 1], rstd.t[:],
           ALU.mult, ALU.mult, R=[xT.k[kt], fng, rstd], W=[sa])
        DMA("sp", O["yT"][:, kt, :], sa.t[:], R=[sa])
    fw.finish()
    n_inst = fw.n_inst
    top.close()
    return nc, list(DBG.keys()), n_inst


_CACHE = {}


def kernel(**inputs):
    maps = prep_inputs(inputs)
    if "nc" not in _CACHE:
        _CACHE["nc"] = build()[0]
    nc = _CACHE["nc"]
    res = run_bass_kernel_spmd(nc, maps, core_ids=list(range(8)))
    return assemble(res.results)
```

```python
import numpy as np
import concourse.bass as bass
import concourse.mybir as mybir
from concourse.bass_utils import run_bass_kernel_spmd

F32 = mybir.dt.float32
BF16 = mybir.dt.bfloat16
ALU = mybir.AluOpType
AF = mybir.ActivationFunctionType


class Buf:
    def __init__(self, name):
        self.name = name
        self.writer = None
        self.readers = {}


class Eng:
    def __init__(self, fw, name, eng, sem):
        self.fw, self.name, self.eng, self.sem = fw, name, eng, sem
        self.count = 0
        self.known = {}


class FW:
    NDMA = 8

    def __init__(self, nc, stack):
        self.nc = nc
        self.sems = {}
        self.engs = {}
        for name, eng in (("pe", nc.tensor), ("dve", nc.vector), ("act", nc.scalar),
                          ("pool", nc.gpsimd), ("sp", nc.sync)):
            sem = stack.enter_context(nc.semaphore("s_" + name))
            self.sems[name] = sem
            self.engs[name] = Eng(self, name, eng, sem)
        self.dma_sems = {}
        self.dma_cnt = {}
        self.dma_idx = {}
        for q in ("sp", "act", "pool"):
            self.dma_sems[q] = []
            for i in range(self.NDMA):
                key = "d_%s%d" % (q, i)
                sem = stack.enter_context(nc.semaphore(key))
                self.sems[key] = sem
                self.dma_sems[q].append(key)
                self.dma_cnt[key] = 0
            self.dma_idx[q] = 0
        self.n_inst = 0

    def _deps(self, ename, reads, writes):
        deps = {}

        def add(k, v):
            if v > deps.get(k, 0):
                deps[k] = v

        for b in reads:
            if b.writer is not None:
                add(*b.writer)
        for b in writes:
            if b.writer is not None:
                k, v = b.writer
                if not (k == "pe" and ename == "pe"):
                    add(k, v)
            for k, v in b.readers.items():
                add(k, v)
        return deps

    def _emit_waits(self, e, deps):
        for k, v in deps.items():
            if e.known.get(k, 0) < v:
                e.eng.wait_ge(self.sems[k], v)
                e.known[k] = v

    def _record(self, key, val, reads, writes):
        for b in reads:
            if b.readers.get(key, 0) < val:
                b.readers[key] = val
        for b in writes:
            b.writer = (key, val)
            b.readers = {}

    def op(self, ename, fn, reads=(), writes=()):
        e = self.engs[ename]
        deps = self._deps(ename, reads, writes)
        if ename == "pe":
            deps.pop("pe", None)
        self._emit_waits(e, deps)
        ins = fn(e.eng)
        e.count += 1
        ins.then_inc(e.sem, 1)
        e.known[ename] = max(e.known.get(ename, 0), 0)
        self._record(ename, e.count, reads, writes)
        self.n_inst += 1
        return ins

    def dma(self, q, out, in_, reads=(), writes=(), **kw):
        e = self.engs[q]
        ring = self.dma_sems[q]
        key = ring[self.dma_idx[q] % self.NDMA]
        self.dma_idx[q] += 1
        deps = self._deps("dma", reads, writes)
        if self.dma_cnt[key] > 0:
            deps[key] = max(deps.get(key, 0), self.dma_cnt[key])
        self._emit_waits(e, deps)
        e.eng.dma_start(out=out, in_=in_, **kw).then_inc(self.sems[key], 16)
        self.dma_cnt[key] += 16
        self._record(key, self.dma_cnt[key], reads, writes)
        self.n_inst += 1

    def finish(self):
        e = self.engs["sp"]
        for key, cnt in self.dma_cnt.items():
            if cnt > 0 and e.known.get(key, 0) < cnt:
                e.eng.wait_ge(self.sems[key], cnt)
                e.known[key] = cnt
        for name in ("pe", "dve", "act", "pool"):
            c = self.engs[name].count
            if c > 0:
                e.eng.wait_ge(self.sems[name], c)

    def barrier(self):
        snap = {n: self.engs[n].count for n in ("pe", "dve", "act", "pool")}
        dsnap = dict(self.dma_cnt)
        for n in ("pe", "dve", "act", "pool", "sp"):
            e = self.engs[n]
            for k, v in list(snap.items()) + list(dsnap.items()):
                if v > 0 and k != n and e.known.get(k, 0) < v:
                    e.eng.wait_ge(self.sems[k], v)
                    e.known[k] = v


class TV:
    def __init__(self, t, name):
        self.t = t
        self.b = Buf(name)


class T:
    def __init__(self, t, name, nsub=0):
        self.t = t
        self.b = None if nsub else Buf(name)
        self.k = [TV(t, "%s.%d" % (name, i)) for i in range(nsub)]


NT = 1024
DM = 1024
NIN = 8208
EPS = 1e-6
OFF = dict(u_a=0, z_a=512, q_b=1024, k_b=1536, v_b=2048, z_b=2560, q_c=3072, k_c=3584,
           v_c=4096, z_c=4608, beta=5120, alpha=5128, gates=5136)
LAM_INIT = [0.8 - 0.6 * float(np.exp(-0.3 * l)) for l in range(2)]
S5_BF16 = True

IN_SPECS = [
    ("xT", [128, 8, 1024]), ("cond", [128, 8]), ("flag", [128, 1]),
    ("ropec", [128, 1024]), ("ropes", [128, 1024]), ("maskb", [128, 48]),
    ("kctx", [128, 2, 4, 512]), ("vctx", [128, 2, 4, 512]),
    ("s5h0r", [128, 2, 2, 16]), ("s5h0i", [128, 2, 2, 16]), ("dns0", [128, 2, 2, 4, 128]),
    ("w_ada", [2, 1024, 3072]), ("b_ada", [2, 128, 24]), ("norm_g", [2, 128, 8]), ("fng", [128, 8]),
    ("w_in", [2, 1024, NIN]), ("w_qkp", [2, 1024, 1024]),
    ("lamre_c", [2, 128, 2, 4, 64]), ("lamim_c", [2, 128, 2, 4, 64]), ("lstep_c", [2, 128, 2, 4, 64]),
    ("bre_c", [2, 128, 2, 4, 64]), ("bim_c", [2, 128, 2, 4, 64]),
    ("lamre_s", [2, 128, 2, 16]), ("lamim_s", [2, 128, 2, 16]), ("lstep_s", [2, 128, 2, 16]),
    ("cre_s", [2, 128, 2, 16, 16]), ("cim_s", [2, 128, 2, 16, 16]),
    ("s5d", [2, 128, 4]), ("wglu", [2, 512, 512]), ("dalam", [2, 128, 256]), ("dang", [2, 128, 128]),
    ("dnconv", [2, 128, 12, 5]), ("dnalog", [2, 128, 8]), ("dndt", [2, 128, 8]), ("dnng", [2, 128, 128]),
    ("w_branch", [2, 3, 512, 1024]), ("w_out", [2, 1024, 1024]),
    ("ident", [128, 128]), ("maskB", [128, 8]), ("cm", [128, 3, 128]),
]
OUT_SPECS = [
    ("yT", [128, 8, 1024]), ("newk", [128, 2, 4, 1024]), ("newv", [128, 2, 4, 1024]),
    ("ns5r", [128, 2, 2, 16, 4]), ("ns5i", [128, 2, 2, 16, 4]), ("ndn", [128, 2, 2, 4, 4, 128]),
]


def _f(a):
    return np.ascontiguousarray(np.asarray(a, dtype=np.float32))


def _rope_tables():
    t = np.arange(1024)
    row = (t // 64).astype(np.float32)
    col = (t % 64).astype(np.float32)
    nf = 16
    inv = (np.float32(10000.0) ** (-np.arange(nf, dtype=np.float32) / np.float32(nf))).astype(np.float32)
    cosT = np.zeros((128, 1024), np.float32)
    sinT = np.zeros((128, 1024), np.float32)
    for p in range(128):
        d = p % 64
        pos = row if d < 32 else col
        j = d % 16
        ang = (pos * inv[j]).astype(np.float32)
        first = (d % 32) < 16
        cosT[p] = np.cos(ang)
        sinT[p] = -np.sin(ang) if first else np.sin(ang)
    return cosT, sinT


def _const_masks():
    i = np.arange(128)
    same = (i[:, None] // 64) == (i[None, :] // 64)
    cm = np.zeros((128, 3, 128), np.float32)
    cm[:, 0, :] = same & (i[:, None] <= i[None, :])
    cm[:, 1, :] = same & (i[:, None] >= i[None, :])
    cm[:, 2, :] = 1.0
    return cm


def prep_inputs(inp):
    g = {k: np.asarray(v) for k, v in inp.items()}
    sh = {}
    sh["w_ada"] = _f(g["w_ada"])
    sh["b_ada"] = _f(g["b_ada"].reshape(2, 24, 128).transpose(0, 2, 1))
    sh["norm_g"] = _f(g["norm_g"].reshape(2, 8, 128).transpose(0, 2, 1))
    sh["fng"] = _f(g["final_norm_g"].reshape(8, 128).T)
    sh["w_in"] = _f(g["w_in"])
    d = np.arange(64)
    perm = np.where((d % 32) < 16, d + 16, d - 16)
    cols = []
    for base in (OFF["q_b"], OFF["k_b"]):
        for hm in range(8):
            cols.append(base + hm * 64 + perm)
    cols = np.concatenate(cols)
    sh["w_qkp"] = _f(g["w_in"][:, :, cols])
    def chmaj(a):
        a = a.reshape(2, 2, 4, 8, 64)
        a = np.broadcast_to(a[:, :, :, :, None, :], (2, 2, 4, 8, 16, 64))
        return _f(a.transpose(0, 3, 4, 1, 2, 5).reshape(2, 128, 2, 4, 64))
    sh["lamre_c"] = chmaj(g["s5_lam_re"])
    sh["lamim_c"] = chmaj(g["s5_lam_im"])
    sh["lstep_c"] = chmaj(np.broadcast_to(g["s5_log_step"][..., None], (2, 2, 32, 64)))
    def bmaj(a):
        a = a.reshape(2, 2, 4, 8, 64, 16)
        return _f(a.transpose(0, 3, 5, 1, 2, 4).reshape(2, 128, 2, 4, 64))
    sh["bre_c"] = bmaj(g["s5_b_re"])
    sh["bim_c"] = bmaj(g["s5_b_im"])
    def stmaj(a):
        a = a.reshape(2, 2, 16, 2, 64)
        return _f(a.transpose(0, 3, 4, 1, 2).reshape(2, 128, 2, 16))
    sh["lamre_s"] = stmaj(g["s5_lam_re"])
    sh["lamim_s"] = stmaj(g["s5_lam_im"])
    sh["lstep_s"] = stmaj(np.broadcast_to(g["s5_log_step"][..., None], (2, 2, 32, 64)))
    def cmaj(a):
        a = a.reshape(2, 2, 16, 2, 16, 64)
        return _f(a.transpose(0, 3, 5, 1, 2, 4).reshape(2, 128, 2, 16, 16))
    sh["cre_s"] = cmaj(g["s5_c_re"])
    sh["cim_s"] = cmaj(g["s5_c_im"])
    sh["s5d"] = _f(g["s5_d"].reshape(2, 4, 128).transpose(0, 2, 1))
    sh["wglu"] = _f(g["s5_w_glu"])
    sh["dalam"] = _f(np.broadcast_to(g["da_lam"].reshape(2, 1, 256), (2, 128, 256)))
    sh["dang"] = _f(np.broadcast_to(g["da_norm_g"].reshape(2, 1, 128), (2, 128, 128)))
    sh["dnconv"] = _f(g["dn_conv"].reshape(2, 5, 12, 128).transpose(0, 3, 2, 1))
    sh["dnalog"] = _f(np.broadcast_to(g["dn_a_log"].reshape(2, 1, 8), (2, 128, 8)))
    sh["dndt"] = _f(np.broadcast_to(g["dn_dt_bias"].reshape(2, 1, 8), (2, 128, 8)))
    sh["dnng"] = _f(np.broadcast_to(g["dn_norm_g"].reshape(2, 1, 128), (2, 128, 128)))
    sh["w_branch"] = _f(g["w_branch"])
    sh["w_out"] = _f(g["w_out"])
    sh["ident"] = np.eye(128, dtype=np.float32)
    p = np.arange(128)
    mB = np.zeros((128, 8), np.float32)
    for st in range(4):
        for gl in range(2):
            mB[:, st * 2 + gl] = ((p // 32) == st) & (((p // 16) % 2) == gl)
    sh["maskB"] = mB
    sh["cm"] = _const_masks()
    cosT, sinT = _rope_tables()
    maps = []
    for c in range(8):
        m = dict(sh)
        if c < 4:
            X = g["x_prompt"][4 * c:4 * c + 4].reshape(1024, 1024)
            cond = g["c_ctx"]
            m["flag"] = np.zeros((128, 1), np.float32)
            m["ropec"] = np.ones((128, 1024), np.float32)
            m["ropes"] = np.zeros((128, 1024), np.float32)
            mb = np.full((12, 4), -30000.0, np.float32)
            for kt in range(8):
                mb[kt, kt // 2] = 0.0
            m["kctx"] = np.zeros((128, 2, 4, 512), np.float32)
            m["vctx"] = np.zeros((128, 2, 4, 512), np.float32)
            m["s5h0r"] = np.zeros((128, 2, 2, 16), np.float32)
            m["s5h0i"] = np.zeros((128, 2, 2, 16), np.float32)
            m["dns0"] = np.zeros((128, 2, 2, 4, 128), np.float32)
        else:
            b = c - 4
            X = g["x_sample"][b]
            cond = g["c"][b]
            m["flag"] = np.ones((128, 1), np.float32)
            m["ropec"] = cosT
            m["ropes"] = sinT
            mb = np.zeros((12, 4), np.float32)
            ck = g["cache_k"][b]
            m["kctx"] = _f(ck.transpose(3, 4, 0, 2, 1).reshape(128, 2, 4, 512))
            cv = g["cache_v"][b]
            m["vctx"] = _f(cv.reshape(2, 4, 128, 512).transpose(2, 0, 1, 3))
            def st5(a):
                a = a.reshape(2, 2, 16, 2, 64)
                return _f(a.transpose(3, 4, 0, 1, 2).reshape(128, 2, 2, 16))
            m["s5h0r"] = st5(g["state_s5_re"][b])
            m["s5h0i"] = st5(g["state_s5_im"][b])
            m["dns0"] = _f(g["state_dn"][b].transpose(3, 0, 1, 2, 4))
        m["maskb"] = _f(np.broadcast_to(mb.reshape(1, 48), (128, 48)))
        m["xT"] = _f(X.T.reshape(8, 128, 1024).transpose(1, 0, 2))
        m["cond"] = _f(cond.reshape(8, 128).T)
        maps.append(m)
    return maps


def assemble(results):
    y_prompt = np.zeros((16, 256, 1024), np.float32)
    y_sample = np.zeros((4, 1024, 1024), np.float32)
    nk = np.zeros((16, 2, 256, 4, 2, 64), np.float32)
    nv = np.zeros((16, 2, 256, 4, 128), np.float32)
    s5r = np.zeros((16, 2, 2, 32, 64), np.float32)
    s5i = np.zeros((16, 2, 2, 32, 64), np.float32)
    sdn = np.zeros((16, 2, 2, 4, 128, 128), np.float32)
    for c in range(8):
        r = results[c]
        Y = np.asarray(r["yT"]).transpose(1, 0, 2).reshape(1024, 1024).T
        if c < 4:
            y_prompt[4 * c:4 * c + 4] = Y.reshape(4, 256, 1024)
            k = np.asarray(r["newk"]).reshape(2, 64, 2, 4, 4, 256)
            nk[4 * c:4 * c + 4] = k.transpose(4, 2, 5, 3, 0, 1)
            v = np.asarray(r["newv"]).reshape(128, 2, 4, 4, 256)
            nv[4 * c:4 * c + 4] = v.transpose(3, 1, 4, 2, 0)
            for nm, dst in (("ns5r", s5r), ("ns5i", s5i)):
                a = np.asarray(r[nm]).reshape(2, 64, 2, 2, 16, 4)
                dst[4 * c:4 * c + 4] = a.transpose(5, 2, 3, 4, 0, 1).reshape(4, 2, 2, 32, 64)
            a = np.asarray(r["ndn"])
            sdn[4 * c:4 * c + 4] = a.transpose(4, 1, 2, 3, 0, 5)
        else:
            y_sample[c - 4] = Y
    return (y_prompt, y_sample, nk, nv, s5r, s5i, sdn)


def build(dbg=(), stop_after=None, nlayers=2, skip=()):
    from contextlib import ExitStack
    nc = bass.Bass("TRN2", target_bir_lowering=False)
    D = {n: nc.dram_tensor(n, list(s), F32, kind="ExternalInput").ap() for n, s in IN_SPECS}
    O = {n: nc.dram_tensor(n, list(s), F32, kind="ExternalOutput").ap() for n, s in OUT_SPECS}
    DBG = {}
    top = ExitStack()
    fw = FW(nc, top)

    uniq = [0]
    halt = [False]

    def sbt(stk, name, shape, dt=F32, nsub=0):
        uniq[0] += 1
        return T(stk.enter_context(nc.sbuf_tensor("t%d_%s" % (uniq[0], name), list(shape), dt)), name, nsub)

    def OP(eng, meth, *a, R=(), W=(), **kw):
        return fw.op(eng, lambda e: getattr(e, meth)(*a, **kw),
                     reads=[x.b for x in R], writes=[x.b for x in W])

    def DMA(q, out, in_, R=(), W=(), **kw):
        fw.dma(q, out, in_, reads=[x.b for x in R], writes=[x.b for x in W], **kw)

    def tap(name, src_ap, shape, R):
        if name in dbg:
            DBG[name] = nc.dram_tensor("dbg_" + name, list(shape), F32, kind="ExternalOutput").ap()
            q = "sp" if src_ap.tensor.dtype == F32 else "pool"
            DMA(q, DBG[name], src_ap, R=R)

    def MM(outT, out_ap, lhsT_T, lhsT_ap, rhs_T, rhs_ap, start=True, stop=True):
        return fw.op("pe", lambda e: e.matmul(out_ap, lhsT=lhsT_ap, rhs=rhs_ap, start=start, stop=stop),
                     reads=[lhsT_T.b, rhs_T.b], writes=[outT.b])

    def v3(ap, s=4):
        return ap.rearrange("p (s t) -> p s t", s=s)

    def col3(ap_p_q, n):
        q = ap_p_q.shape[1]
        return ap_p_q.rearrange("p (q o) -> p q o", o=1).broadcast_to([128, q, n])

    P = [T(top.enter_context(nc.psum_tensor("P%d" % i, [128, 1024], F32)), "P%d" % i) for i in range(4)]
    xT = sbt(top, "xT", [128, 8, 1024], nsub=8)
    hnT = sbt(top, "hnT", [128, 8, 1024], BF16, nsub=8)
    merged = sbt(top, "merged", [128, 8, 1024], nsub=8)
    WB = [sbt(top, "wb%d" % i, [128, 8, 512], BF16) for i in range(2)]
    wb_i = [0]
    ident = sbt(top, "ident", [128, 128])
    cm = sbt(top, "cm", [128, 3, 128])
    onesb = sbt(top, "onesb", [128, 128], BF16)
    maskB = sbt(top, "maskB", [128, 8])
    flag = sbt(top, "flag", [128, 1])
    modt = sbt(top, "modt", [128, 48])
    gmod = sbt(top, "gmod", [128, 16])
    rstd = sbt(top, "rstd", [128, 1024])
    SCA = [sbt(top, "sca%d" % i, [128, 1024]) for i in range(3)]
    SCB = [sbt(top, "scb%d" % i, [128, 1024], BF16) for i in range(2)]
    sci = [0, 0]

    def scA():
        sci[0] += 1
        return SCA[sci[0] % 3]

    def scB():
        sci[1] += 1
        return SCB[sci[1] % 2]

    for kt in range(8):
        DMA("sp", xT.t[:, kt, :], D["xT"][:, kt, :], W=[xT.k[kt]])
    DMA("sp", ident.t[:], D["ident"], W=[ident])
    DMA("sp", cm.t[:], D["cm"], W=[cm])
    DMA("sp", maskB.t[:], D["maskB"], W=[maskB])
    DMA("sp", flag.t[:], D["flag"], W=[flag])
    OP("pool", "memset", onesb.t[:], 1.0, W=[onesb])

    def wb_next():
        buf = WB[wb_i[0] % 2]
        wb_i[0] += 1
        return buf

    def load_into(buf, coff, dram_ap, ktiles, ncols):
        DMA("pool", buf.t[:, 0:ktiles, coff:coff + ncols], dram_ap.rearrange("(kt p) n -> p kt n", p=128), W=[buf])

    def load_w(dram_ap, ktiles, ncols):
        buf = wb_next()
        load_into(buf, 0, dram_ap, ktiles, ncols)
        return buf

    def project(wbuf, ktiles, c0, rhs_fn, pT):
        for half in range(2):
            for kt in range(ktiles):
                rT, rap = rhs_fn(kt, half)
                MM(pT, pT.t[:, half * 512:(half + 1) * 512], wbuf, wbuf.t[:, kt, c0:c0 + 128],
                   rT, rap, start=(kt == 0), stop=(kt == ktiles - 1))

    def hn_rhs(kt, half):
        return hnT.k[kt], hnT.t[:, kt, half * 512:(half + 1) * 512]

    def k_rhs(tt):
        return lambda kt, half: (tt.k[kt], tt.t[:, kt, half * 512:(half + 1) * 512])

    with ExitStack() as s0:
        cond = sbt(s0, "cond", [128, 8])
        scond = sbt(s0, "scond", [128, 8])
        bada = sbt(s0, "bada", [128, 48])
        ng = sbt(s0, "ng", [128, 16])
        wst = [sbt(s0, "wst%d" % i, [128, 8, 512]) for i in range(2)]
        DMA("sp", cond.t[:], D["cond"], W=[cond])
        for l in range(2):
            DMA("sp", bada.t[:, l * 24:(l + 1) * 24], D["b_ada"][l], W=[bada])
            DMA("sp", ng.t[:, l * 8:(l + 1) * 8], D["norm_g"][l], W=[ng])
        OP("act", "activation", scond.t[:], cond.t[:], AF.Silu, R=[cond], W=[scond])
        i = 0
        for l in range(2):
            wv = D["w_ada"][l].rearrange("(kt p) n -> p kt n", p=128)
            for nb in range(6):
                w = wst[i % 2]
                i += 1
                DMA("sp" if i % 2 == 1 else "act", w.t[:], wv[:, :, nb * 512:(nb + 1) * 512], W=[w])
                for m in range(4):
                    j = l * 24 + nb * 4 + m
                    for kt in range(8):
                        MM(P[0], P[0].t[:, j:j + 1], w, w.t[:, kt, m * 128:(m + 1) * 128],
                           scond, scond.t[:, kt:kt + 1], start=(kt == 0), stop=(kt == 7))
        OP("dve", "tensor_tensor", modt.t[:], P[0].t[:, 0:48], bada.t[:], ALU.add, R=[P[0], bada], W=[modt])
        for l in range(2):
            OP("dve", "scalar_tensor_tensor", gmod.t[:, l * 8:(l + 1) * 8], modt.t[:, l * 24 + 8:l * 24 + 16], 1.0,
               ng.t[:, l * 8:(l + 1) * 8], ALU.add, ALU.mult, R=[modt, ng], W=[gmod])
        tap("modt", modt.t[:], [128, 48], [modt])
        fw.barrier()

    def rms_stats():
        for kt in range(8):
            sb_ = scB()
            OP("act", "activation", sb_.t[:], xT.t[:, kt, :], AF.Square, R=[xT.k[kt]], W=[sb_])
            for half in range(2):
                MM(P[0], P[0].t[:, half * 512:(half + 1) * 512], onesb, onesb.t[:], sb_,
                   sb_.t[:, half * 512:(half + 1) * 512], start=(kt == 0), stop=(kt == 7))
        OP("dve", "tensor_scalar", rstd.t[:], P[0].t[:], 1.0 / DM, EPS, ALU.mult, ALU.add, R=[P[0]], W=[rstd])
        OP("dve", "reciprocal", rstd.t[:], rstd.t[:], R=[rstd], W=[rstd])
        OP("act", "activation", rstd.t[:], rstd.t[:], AF.Sqrt, R=[rstd], W=[rstd])

    def norm_modulate(l):
        rms_stats()
        for kt in range(8):
            sa = scA()
            OP("dve", "scalar_tensor_tensor", sa.t[:], xT.t[:, kt, :], gmod.t[:, l * 8 + kt:l * 8 + kt + 1],
               rstd.t[:], ALU.mult, ALU.mult, R=[xT.k[kt], gmod, rstd], W=[sa])
            OP("act", "activation", hnT.t[:, kt, :], sa.t[:], AF.Identity,
               bias=modt.t[:, l * 24 + kt:l * 24 + kt + 1], R=[sa, modt], W=[hnT.k[kt]])

    def merge_branch(l, n, outTs):
        for blk in range(2):
            wbr = load_w(D["w_branch"][l, n][:, blk * 512:(blk + 1) * 512], 4, 512)
            g0 = OFF["gates"] + n * 1024 + blk * 512
            wg = load_w(D["w_in"][l][:, g0:g0 + 512], 8, 512)
            for m in range(4):
                dt_ = blk * 4 + m
                pr, gp = P[(dt_ % 2) * 2], P[(dt_ % 2) * 2 + 1]
                project(wbr, 4, m * 128, k_rhs(outTs), pr)
                project(wg, 8, m * 128, hn_rhs, gp)
                sa = scA()
                OP("act", "activation", sa.t[:], gp.t[:], AF.Sigmoid, R=[gp], W=[sa])
                if n == 0:
                    OP("dve", "tensor_tensor", merged.t[:, dt_, :], sa.t[:], pr.t[:], ALU.mult,
                       R=[sa, pr], W=[merged.k[dt_]])
                else:
                    OP("dve", "tensor_tensor", sa.t[:], sa.t[:], pr.t[:], ALU.mult, R=[sa, pr], W=[sa])
                    OP("pool", "tensor_tensor", merged.t[:, dt_, :], merged.t[:, dt_, :], sa.t[:], ALU.add,
                       R=[sa, merged.k[dt_]], W=[merged.k[dt_]])

    def out_proj_residual(l):
        for kt in range(8):
            OP("act", "activation", hnT.t[:, kt, :], merged.t[:, kt, :], AF.Copy, R=[merged.k[kt]], W=[hnT.k[kt]])
        for blk in range(2):
            wo = load_w(D["w_out"][l][:, blk * 512:(blk + 1) * 512], 8, 512)
            for m in range(4):
                dt_ = blk * 4 + m
                pp = P[2 + m % 2]
                project(wo, 8, m * 128, hn_rhs, pp)
                OP("dve", "scalar_tensor_tensor", xT.t[:, dt_, :], pp.t[:],
                   modt.t[:, l * 24 + 16 + dt_:l * 24 + 17 + dt_],
                   xT.t[:, dt_, :], ALU.mult, ALU.add, R=[pp, modt, xT.k[dt_]], W=[xT.k[dt_]])

    def TT(eng, oT, o, aT, a, bT, b, op):
        OP(eng, "tensor_tensor", o, a, b, op, R=[aT, bT], W=[oT])

    def TS(eng, oT, o, aT, a, s1, s2, op0, op1=None, RS=()):
        if op1 is None:
            OP(eng, "tensor_scalar", o, a, s1, None, op0, R=[aT] + list(RS), W=[oT])
        else:
            OP(eng, "tensor_scalar", o, a, s1, s2, op0, op1, R=[aT] + list(RS), W=[oT])

    def STT(eng, oT, o, aT, a, sc, bT, b, op0, op1, RS=()):
        OP(eng, "scalar_tensor_tensor", o, a, sc, b, op0, op1, R=[aT, bT] + list(RS), W=[oT])

    def ACT(oT, o, aT, a, func, RS=(), **kw):
        OP("act", "activation", o, a, func, R=[aT] + list(RS), W=[oT], **kw)

    def cos_sin(stk, eng, th, n, tag):
        c = [sbt(stk, "cs_c%d%s" % (i, tag), [128, n]) for i in range(2)]
        s = [sbt(stk, "cs_s%d%s" % (i, tag), [128, n]) for i in range(2)]
        t = sbt(stk, "cs_t" + tag, [128, n])
        hp = sbt(stk, "cs_hp" + tag, [128, 1])
        OP(eng, "memset", hp.t[:], float(np.pi / 2), W=[hp])
        ACT(s[0], s[0].t[:], th, th.t[:], AF.Sin, scale=1.0 / 16)
        ACT(c[0], c[0].t[:], th, th.t[:], AF.Sin, RS=[hp], scale=1.0 / 16, bias=hp.t[:])
        for it in range(4):
            a, b = it % 2, (it + 1) % 2
            TT(eng, t, t.t[:], s[a], s[a].t[:], s[a], s[a].t[:], ALU.mult)
            STT(eng, s[b], s[b].t[:], s[a], s[a].t[:], 2.0, c[a], c[a].t[:], ALU.mult, ALU.mult)
            TS(eng, c[b], c[b].t[:], t, t.t[:], -2.0, 1.0, ALU.mult, ALU.add)
        return c[0], s[0]

    def branch_a(l):
        with ExitStack() as sa_:
            uT = sbt(sa_, "uT", [128, 4, 1024], BF16, nsub=4)
            ya = sbt(sa_, "ya", [128, 4, 1024], BF16, nsub=4)
            bbc = sbt(sa_, "bbc", [128, 2, 512])
            rho = sbt(sa_, "rho", [128, 32])
            g0r = sbt(sa_, "g0r", [128, 32])
            g0i = sbt(sa_, "g0i", [128, 32])
            cs = sbt(sa_, "cs", [128, 2, 512])
            ns5 = sbt(sa_, "ns5", [128, 2, 128])
            s5d = sbt(sa_, "s5d", [128, 4])
            cth0 = None
            DMA("sp", s5d.t[:], D["s5d"][l], W=[s5d])
            DMA("sp", cs.t[:, 0, :], D["cre_s"][l].rearrange("p d t i -> p (d t i)"), W=[cs])
            DMA("sp", cs.t[:, 1, :], D["cim_s"][l].rearrange("p d t i -> p (d t i)"), W=[cs])
            OP("pool", "tensor_scalar", cs.t[:, 1, :], cs.t[:, 1, :], -1.0, None, ALU.mult, R=[cs], W=[cs])

            wu = load_w(D["w_in"][l][:, OFF["u_a"]:OFF["u_a"] + 512], 8, 512)
            for m in range(4):
                pp = P[m % 2]
                project(wu, 8, m * 128, hn_rhs, pp)
                ACT(uT.k[m], uT.t[:, m, :], pp, pp.t[:], AF.Copy)

            cthp = sbt(sa_, "cthp", [128, 32]); sthp = sbt(sa_, "sthp", [128, 32])
            sd = ExitStack()
            lr = sbt(sd, "lr_s", [128, 32]); li = sbt(sd, "li_s", [128, 32]); ls = sbt(sd, "ls_s", [128, 32])
            h0r = sbt(sd, "h0r", [128, 32]); h0i = sbt(sd, "h0i", [128, 32])
            th = sbt(sd, "th_s", [128, 32]); tmp = sbt(sd, "tmp_s", [128, 32])
            for tns, nm in ((lr, "lamre_s"), (li, "lamim_s"), (ls, "lstep_s")):
                DMA("sp", tns.t[:], D[nm][l].rearrange("p d t -> p (d t)"), W=[tns])
            DMA("sp", h0r.t[:], D["s5h0r"][:, l].rearrange("p d t -> p (d t)"), W=[h0r])
            DMA("sp", h0i.t[:], D["s5h0i"][:, l].rearrange("p d t -> p (d t)"), W=[h0i])
            ACT(ls, ls.t[:], ls, ls.t[:], AF.Exp)
            TT("dve", tmp, tmp.t[:], lr, lr.t[:], ls, ls.t[:], ALU.mult)
            ACT(rho, rho.t[:], tmp, tmp.t[:], AF.Exp)
            TT("dve", th, th.t[:], li, li.t[:], ls, ls.t[:], ALU.mult)
            tap("li%d" % l, li.t[:], [128, 32], [li])
            tap("th%d" % l, th.t[:], [128, 32], [th])
            c_, s_ = cos_sin(sd, "dve", th, 32, "s")
            OP("dve", "tensor_copy", cthp.t[:], c_.t[:], R=[c_], W=[cthp])
            OP("dve", "tensor_copy", sthp.t[:], s_.t[:], R=[s_], W=[sthp])
            TT("dve", tmp, tmp.t[:], s_, s_.t[:], h0i, h0i.t[:], ALU.mult)
            TT("dve", g0r, g0r.t[:], c_, c_.t[:], h0r, h0r.t[:], ALU.mult)
            TT("dve", g0r, g0r.t[:], g0r, g0r.t[:], tmp, tmp.t[:], ALU.subtract)
            TT("dve", tmp, tmp.t[:], s_, s_.t[:], h0r, h0r.t[:], ALU.mult)
            TT("dve", g0i, g0i.t[:], c_, c_.t[:], h0i, h0i.t[:], ALU.mult)
            TT("dve", g0i, g0i.t[:], g0i, g0i.t[:], tmp, tmp.t[:], ALU.add)

            names = ["lrc", "lic", "lsc", "brc", "bic", "thc", "mag", "ar1", "ai", "den", "t1", "t2", "fr", "fi"]
            A = {n_: sbt(sd, n_ + "_c", [128, 512]) for n_ in names}
            for tns, nm in ((A["lrc"], "lamre_c"), (A["lic"], "lamim_c"), (A["lsc"], "lstep_c"),
                            (A["brc"], "bre_c"), (A["bic"], "bim_c")):
                DMA("sp", tns.t[:], D[nm][l].rearrange("p d c q -> p (d c q)"), W=[tns])

            def e2(o, a, b, op, eng="dve"):
                TT(eng, A[o], A[o].t[:], A[a], A[a].t[:], A[b], A[b].t[:], op)
            ACT(A["lsc"], A["lsc"].t[:], A["lsc"], A["lsc"].t[:], AF.Exp)
            e2("t1", "lrc", "lsc", ALU.mult)
            ACT(A["mag"], A["mag"].t[:], A["t1"], A["t1"].t[:], AF.Exp)
            e2("thc", "lic", "lsc", ALU.mult)
            cc, ss = cos_sin(sd, "dve", A["thc"], 512, "c")
            TT("dve", A["ai"], A["ai"].t[:], A["mag"], A["mag"].t[:], ss, ss.t[:], ALU.mult)
            TT("dve", A["ar1"], A["ar1"].t[:], A["mag"], A["mag"].t[:], cc, cc.t[:], ALU.mult)
            TS("dve", A["ar1"], A["ar1"].t[:], A["ar1"], A["ar1"].t[:], -1.0, None, ALU.add)
            e2("den", "lrc", "lrc", ALU.mult)
            e2("t1", "lic", "lic", ALU.mult)
            e2("den", "den", "t1", ALU.add)
            OP("dve", "reciprocal", A["den"].t[:], A["den"].t[:], R=[A["den"]], W=[A["den"]])
            e2("t1", "ar1", "lrc", ALU.mult)
            e2("t2", "ai", "lic", ALU.mult)
            e2("fr", "t1", "t2", ALU.add)
            e2("fr", "fr", "den", ALU.mult)
            e2("t1", "ai", "lrc", ALU.mult)
            e2("t2", "ar1", "lic", ALU.mult)
            e2("fi", "t1", "t2", ALU.subtract)
            e2("fi", "fi", "den", ALU.mult)
            e2("t1", "fr", "brc", ALU.mult)
            e2("t2", "fi", "bic", ALU.mult)
            TT("dve", bbc, bbc.t[:, 0, :], A["t1"], A["t1"].t[:], A["t2"], A["t2"].t[:], ALU.subtract)
            e2("t1", "fr", "bic", ALU.mult)
            e2("t2", "fi", "brc", ALU.mult)
            TT("dve", bbc, bbc.t[:, 1, :], A["t1"], A["t1"].t[:], A["t2"], A["t2"].t[:], ALU.add)
            fw.barrier()
            sd.close()
            tap("bbc%d" % l, bbc.t[:], [128, 2, 512], [bbc])
            tap("rho%d" % l, rho.t[:], [128, 32], [rho])
            tap("cth%d" % l, cthp.t[:], [128, 32], [cthp])
            tap("sth%d" % l, sthp.t[:], [128, 32], [sthp])

            Tr32 = sbt(sa_, "Tr32", [128, 8, 256]); Ti32 = sbt(sa_, "Ti32", [128, 8, 256])
            SETS = []
            for i_ in range(2):
                SETS.append(dict(
                    Bbf=sbt(sa_, "Bbf", [128, 2, 4, 2, 128], BF16), CTp=sbt(sa_, "CTp", [128, 2, 4, 2, 128], BF16),
                    Tr=sbt(sa_, "Trb", [128, 8, 256], BF16 if S5_BF16 else F32),
                    Ti=sbt(sa_, "Tib", [128, 8, 256], BF16 if S5_BF16 else F32),
                    Cr=sbt(sa_, "Cr", [128, 9, 8]), Ci=sbt(sa_, "Ci", [128, 9, 8]), ctmp=sbt(sa_, "ctmp", [128, 3, 8]),
                    Kr=sbt(sa_, "Kr", [128, 8]), Ki=sbt(sa_, "Ki", [128, 8]), nKi=sbt(sa_, "nKi", [128, 8]),
                    c255=sbt(sa_, "c255", [128, 8]), s255=sbt(sa_, "s255", [128, 8]), ns255=sbt(sa_, "ns255", [128, 8])))
            HB = [sbt(sa_, "hb%d" % i, [128, 1024], BF16) for i in range(4)]
            carry = sbt(sa_, "carry", [128, 2, 2, 4])
            bbc4 = bbc.t[:].rearrange("p r (d c q) -> p r d c q", d=2, c=4)
            cs4 = cs.t[:].rearrange("p r (d t i) -> p r d t i", d=2, t=16)
            cth3 = cthp.t[:].rearrange("p (d t) -> p d t", d=2)
            sth3 = sthp.t[:].rearrange("p (d t) -> p d t", d=2)
            rho3 = rho.t[:].rearrange("p (d t) -> p d t", d=2)
            ns54 = ns5.t[:].rearrange("p r (d t s) -> p r d t s", d=2, t=16)
            TE = "pool"
            pipe_ = [0]

            def bfv(i):
                if not S5_BF16:
                    return merged.t[:, i, :]
                return merged.t[:, i, :].bitcast(BF16)[:, 0:1024]

            def unpack(ct):
                S_ = SETS[ct % 2]
                return tuple(S_[n_] for n_ in ("Bbf", "CTp", "Tr", "Ti", "Cr", "Ci", "ctmp", "Kr", "Ki", "nKi",
                                               "c255", "s255", "ns255"))

            def build_ct(ct):
                Bbf, CTp, Trb_, Tib_, Cr, Ci, ctmp, Kr, Ki, nKi, c255, s255, ns255 = unpack(ct)
                Tr, Ti = Tr32, Ti32
                for st_ in range(4):
                    for gl in range(2):
                        for ri in range(2):
                            OP(TE, "tensor_scalar", Bbf.t[:, :, st_, ri, gl * 64:(gl + 1) * 64], bbc4[:, ri, :, ct, :],
                               maskB.t[:, st_ * 2 + gl:st_ * 2 + gl + 1], None, ALU.mult, R=[bbc, maskB], W=[Bbf])
                OP(TE, "memset", CTp.t[:], 0.0, W=[CTp])
                for st_ in range(4):
                    for gl in range(2):
                        for ri in range(2):
                            OP(TE, "tensor_copy",
                               CTp.t[gl * 64:(gl + 1) * 64, :, st_, ri, 32 * st_ + gl * 16:32 * st_ + gl * 16 + 16],
                               cs4[gl * 64:(gl + 1) * 64, ri, :, ct * 4 + st_, :], R=[cs], W=[CTp])
                OP(TE, "tensor_copy", Cr.t[:, 0, :].rearrange("p (d s) -> p d s", d=2), cth3[:, :, ct * 4:(ct + 1) * 4],
                   R=[cthp], W=[Cr])
                OP(TE, "tensor_copy", Ci.t[:, 0, :].rearrange("p (d s) -> p d s", d=2), sth3[:, :, ct * 4:(ct + 1) * 4],
                   R=[sthp], W=[Ci])
                OP(TE, "memset", Tr.t[:, :, 0:1], 1.0, W=[Tr])
                OP(TE, "memset", Ti.t[:, :, 0:1], 0.0, W=[Ti])
                tA, tB = SCA[0], SCA[1]
                for k in range(8):
                    n = 1 << k
                    crb = col3(Cr.t[:, k, :], n)
                    cib = col3(Ci.t[:, k, :], n)
                    a3 = tA.t[:, 0:8 * n].rearrange("p (q n) -> p q n", q=8)
                    b3 = tB.t[:, 0:8 * n].rearrange("p (q n) -> p q n", q=8)
                    OP(TE, "tensor_tensor", a3, Tr.t[:, :, 0:n], crb, ALU.mult, R=[Tr, Cr], W=[tA])
                    OP(TE, "tensor_tensor", b3, Ti.t[:, :, 0:n], cib, ALU.mult, R=[Ti, Ci], W=[tB])
                    OP(TE, "tensor_tensor", Tr.t[:, :, n:2 * n], a3, b3, ALU.subtract, R=[tA, tB], W=[Tr])
                    OP(TE, "tensor_tensor", a3, Tr.t[:, :, 0:n], cib, ALU.mult, R=[Tr, Ci], W=[tA])
                    OP(TE, "tensor_tensor", b3, Ti.t[:, :, 0:n], crb, ALU.mult, R=[Ti, Cr], W=[tB])
                    OP(TE, "tensor_tensor", Ti.t[:, :, n:2 * n], a3, b3, ALU.add, R=[tA, tB], W=[Ti])
                    OP(TE, "tensor_tensor", ctmp.t[:, 0, :], Cr.t[:, k, :], Cr.t[:, k, :], ALU.mult, R=[Cr], W=[ctmp])
                    OP(TE, "tensor_tensor", ctmp.t[:, 1, :], Ci.t[:, k, :], Ci.t[:, k, :], ALU.mult, R=[Ci], W=[ctmp])
                    OP(TE, "tensor_tensor", Cr.t[:, k + 1, :], ctmp.t[:, 0, :], ctmp.t[:, 1, :], ALU.subtract,
                       R=[ctmp], W=[Cr])
                    OP(TE, "tensor_tensor", ctmp.t[:, 2, :], Cr.t[:, k, :], Ci.t[:, k, :], ALU.mult, R=[Cr, Ci], W=[ctmp])
                    OP(TE, "tensor_scalar", Ci.t[:, k + 1, :], ctmp.t[:, 2, :], 2.0, None, ALU.mult, R=[ctmp], W=[Ci])
                OP(TE, "tensor_scalar", Kr.t[:], Cr.t[:, 8, :], flag.t[:, 0:1], None, ALU.mult, R=[Cr, flag], W=[Kr])
                OP(TE, "tensor_scalar", Ki.t[:], Ci.t[:, 8, :], flag.t[:, 0:1], None, ALU.mult, R=[Ci, flag], W=[Ki])
                OP(TE, "tensor_scalar", nKi.t[:], Ki.t[:], -1.0, None, ALU.mult, R=[Ki], W=[nKi])
                OP(TE, "tensor_copy", c255.t[:], Tr.t[:, :, 255], R=[Tr], W=[c255])
                OP(TE, "tensor_copy", s255.t[:], Ti.t[:, :, 255], R=[Ti], W=[s255])
                OP(TE, "tensor_scalar", ns255.t[:], s255.t[:], -1.0, None, ALU.mult, R=[s255], W=[ns255])
                OP(TE, "tensor_copy", Trb_.t[:], Tr.t[:], R=[Tr], W=[Trb_])
                OP(TE, "tensor_copy", Tib_.t[:], Ti.t[:], R=[Ti], W=[Tib_])
                if l == 0 and ct == 0:
                    tap("Tr", Tr.t[:], [128, 8, 256], [Tr])
                    tap("Ti", Ti.t[:], [128, 8, 256], [Ti])

            def scan_ct(ct):
                Bbf, CTp, Tr, Ti, Cr, Ci, ctmp, Kr, Ki, nKi, c255, s255, ns255 = unpack(ct)
                pipe = pipe_[0]
                first = True

                def emit_bu(dr, st_, pp_):
                    for ri in range(2):
                        for half in range(2):
                            MM(P[ri], P[ri].t[:, half * 512:(half + 1) * 512], Bbf, Bbf.t[:, dr, st_, ri, :],
                               uT.k[ct], uT.t[:, ct, half * 512:(half + 1) * 512])
                    for ri in range(2):
                        src = v3(P[ri].t[:])
                        if dr == 1:
                            src = src[:, :, ::-1]
                        ACT(merged.k[pp_ * 4 + ri], v3(bfv(pp_ * 4 + ri)), P[ri], src, AF.Copy)

                pairs = [(dr, st_) for dr in range(2) for st_ in range(4)]
                emit_bu(pairs[0][0], pairs[0][1], pipe)
                for ip_, (dr, st_) in enumerate(pairs):
                    if True:
                        q = dr * 4 + st_
                        tp = ct * 4 + st_
                        E = "dve"
                        wk = [merged.k[pipe * 4 + i] for i in range(4)]
                        wa = [bfv(pipe * 4 + i) for i in range(4)]
                        hr, hi = HB[pipe * 2], HB[pipe * 2 + 1]
                        pipe = (pipe + 1) % 2
                        BUr, BUi, TA, TB_ = wk
                        bur, bui, ta, tb = wa
                        if ip_ + 1 < len(pairs):
                            emit_bu(pairs[ip_ + 1][0], pairs[ip_ + 1][1], pipe)
                        cosb = Tr.t[:, q:q + 1, :].broadcast_to([128, 4, 256])
                        sinb = Ti.t[:, q:q + 1, :].broadcast_to([128, 4, 256])
                        OP(E, "tensor_tensor", v3(ta), v3(bur), cosb, ALU.mult, R=[BUr, Tr], W=[TA])
                        OP(E, "tensor_tensor", v3(tb), v3(bui), sinb, ALU.mult, R=[BUi, Ti], W=[TB_])
                        OP(E, "tensor_tensor", ta, ta, tb, ALU.add, R=[TA, TB_], W=[TA])
                        OP(E, "tensor_tensor", v3(tb), v3(bui), cosb, ALU.mult, R=[BUi, Tr], W=[TB_])
                        OP(E, "tensor_tensor", v3(bur), v3(bur), sinb, ALU.mult, R=[BUr, Ti], W=[BUr])
                        OP(E, "tensor_tensor", tb, tb, bur, ALU.subtract, R=[TB_, BUr], W=[TB_])
                        order = [0, 1, 2, 3] if dr == 0 else [3, 2, 1, 0]
                        rcol = rho3[:, dr, tp:tp + 1]
                        rb = rcol.broadcast_to([128, 256])
                        for idx, sg in enumerate(order):
                            sl = slice(sg * 256, (sg + 1) * 256)
                            if idx == 0:
                                ir = g0r.t[:, dr * 16 + tp:dr * 16 + tp + 1]
                                ii = g0i.t[:, dr * 16 + tp:dr * 16 + tp + 1]
                                RI = [g0r, g0i]
                            else:
                                ir = carry.t[:, pipe, 0, idx:idx + 1]
                                ii = carry.t[:, pipe, 1, idx:idx + 1]
                                RI = [carry]
                            OP(E, "tensor_tensor_scan", bur[:, sl], rb, ta[:, sl], ir, ALU.mult, ALU.add,
                               R=[rho, TA] + RI, W=[BUr])
                            OP(E, "tensor_tensor_scan", bui[:, sl], rb, tb[:, sl], ii, ALU.mult, ALU.add,
                               R=[rho, TB_] + RI, W=[BUi])
                            if idx < 3:
                                last = sg * 256 + 255
                                glr = bur[:, last:last + 1]
                                gli = bui[:, last:last + 1]
                                nr = carry.t[:, pipe, 0, idx + 1:idx + 2]
                                ni = carry.t[:, pipe, 1, idx + 1:idx + 2]
                                OP(E, "tensor_scalar", nr, glr, Kr.t[:, q:q + 1], None, ALU.mult, R=[BUr, Kr], W=[carry])
                                OP(E, "scalar_tensor_tensor", nr, gli, nKi.t[:, q:q + 1], nr, ALU.mult, ALU.add,
                                   R=[BUi, nKi, carry], W=[carry])
                                OP(E, "tensor_scalar", ni, gli, Kr.t[:, q:q + 1], None, ALU.mult, R=[BUi, Kr], W=[carry])
                                OP(E, "scalar_tensor_tensor", ni, glr, Ki.t[:, q:q + 1], ni, ALU.mult, ALU.add,
                                   R=[BUr, Ki, carry], W=[carry])
                        g255r = v3(bur)[:, :, 255]
                        g255i = v3(bui)[:, :, 255]
                        fr_ = ns54[:, 0, dr, tp, :]
                        fi_ = ns54[:, 1, dr, tp, :]
                        OP(E, "tensor_scalar", fr_, g255r, c255.t[:, q:q + 1], None, ALU.mult, R=[BUr, c255], W=[ns5])
                        OP(E, "scalar_tensor_tensor", fr_, g255i, ns255.t[:, q:q + 1], fr_, ALU.mult, ALU.add,
                           R=[BUi, ns255, ns5], W=[ns5])
                        OP(E, "tensor_scalar", fi_, g255i, c255.t[:, q:q + 1], None, ALU.mult, R=[BUi, c255], W=[ns5])
                        OP(E, "scalar_tensor_tensor", fi_, g255r, s255.t[:, q:q + 1], fi_, ALU.mult, ALU.add,
                           R=[BUr, s255, ns5], W=[ns5])
                        ohr = v3(hr.t[:]) if dr == 0 else v3(hr.t[:])[:, :, ::-1]
                        ohi = v3(hi.t[:]) if dr == 0 else v3(hi.t[:])[:, :, ::-1]
                        OP(E, "tensor_tensor", v3(ta), v3(bur), cosb, ALU.mult, R=[BUr, Tr], W=[TA])
                        OP(E, "tensor_tensor", v3(tb), v3(bui), sinb, ALU.mult, R=[BUi, Ti], W=[TB_])
                        OP(E, "tensor_tensor", ohr, v3(ta), v3(tb), ALU.subtract, R=[TA, TB_], W=[hr])
                        OP(E, "tensor_tensor", v3(ta), v3(bur), sinb, ALU.mult, R=[BUr, Ti], W=[TA])
                        OP(E, "tensor_tensor", v3(tb), v3(bui), cosb, ALU.mult, R=[BUi, Tr], W=[TB_])
                        OP(E, "tensor_tensor", ohi, v3(ta), v3(tb), ALU.add, R=[TA, TB_], W=[hi])
                        if l == 0 and ct == 0 and st_ == 0:
                            tap("hr%d" % dr, hr.t[:], [128, 1024], [hr])
                        for ri, hh in ((0, hr), (1, hi)):
                            lastmm = (dr == 1 and st_ == 3 and ri == 1)
                            for half in range(2):
                                MM(P[3], P[3].t[:, half * 512:(half + 1) * 512], CTp, CTp.t[:, dr, st_, ri, :],
                                   hh, hh.t[:, half * 512:(half + 1) * 512], start=(first and ri == 0), stop=lastmm)
                        first = False
                sa = scA()
                STT("dve", sa, sa.t[:], uT.k[ct], uT.t[:, ct, :], s5d.t[:, ct:ct + 1], P[3], P[3].t[:],
                    ALU.mult, ALU.add, RS=[s5d])
                if ct == 0:
                    tap("ypre%d" % l, sa.t[:], [128, 1024], [sa])
                ACT(ya.k[ct], ya.t[:, ct, :], sa, sa.t[:], AF.Gelu)
                pipe_[0] = pipe

            build_ct(0)
            for ct in range(4):
                if ct + 1 < 4:
                    build_ct(ct + 1)
                scan_ct(ct)
            DMA("sp", O["ns5r"][:, l].rearrange("p d t s -> p (d t s)"), ns5.t[:, 0, :], R=[ns5])
            DMA("sp", O["ns5i"][:, l].rearrange("p d t s -> p (d t s)"), ns5.t[:, 1, :], R=[ns5])
            wgl = load_w(D["wglu"][l], 4, 512)
            wz = load_w(D["w_in"][l][:, OFF["z_a"]:OFF["z_a"] + 512], 8, 512)
            for m in range(4):
                pg_, pz_ = (P[0], P[1]) if m % 2 == 0 else (P[2], P[3])
                project(wgl, 4, m * 128, k_rhs(ya), pg_)
                project(wz, 8, m * 128, hn_rhs, pz_)
                g_ = scB()
                ACT(g_, g_.t[:], pg_, pg_.t[:], AF.Sigmoid)
                z_ = scB()
                ACT(z_, z_.t[:], pz_, pz_.t[:], AF.Silu)
                OP("dve", "tensor_tensor", g_.t[:], g_.t[:], ya.t[:, m, :], ALU.mult, R=[g_, ya.k[m]], W=[g_])
                OP("dve", "tensor_tensor", uT.t[:, m, :], g_.t[:], z_.t[:], ALU.mult, R=[g_, z_], W=[uT.k[m]])
            tap("outa%d" % l, uT.t[:, 0, :], [128, 1024], [uT.k[0]])
            merge_branch(l, 0, uT)
            fw.barrier()

    AX = mybir.AxisListType.X

    def branch_b(l):
        with ExitStack() as sb_:
            kT = sbt(sb_, "kTall", [128, 4, 1536], BF16, nsub=4)
            qT = sbt(sb_, "qTall", [128, 4, 1024], BF16, nsub=4)
            Vt = sbt(sb_, "Vt", [128, 12, 4, 129], BF16, nsub=12)
            szb = sbt(sb_, "szb", [128, 4, 1024], BF16, nsub=4)
            outb = sbt(sb_, "outb", [128, 4, 1024], BF16, nsub=4)
            ropec = sbt(sb_, "ropec", [128, 1024]); ropes = sbt(sb_, "ropes", [128, 1024])
            maskb = sbt(sb_, "maskb", [128, 48])
            dalam = sbt(sb_, "dalam", [128, 256]); gn = sbt(sb_, "gn", [128, 128])
            lamv = sbt(sb_, "lamv", [128, 4]); lt = sbt(sb_, "lt", [128, 128])
            EX = [sbt(sb_, "ex%d" % i, [128, 512], BF16) for i in range(3)]
            ACC = [sbt(sb_, "accS%d" % i, [128, 8, 160]) for i in range(2)]
            OB4 = [sbt(sb_, "ob4%d" % i, [128, 4, 128]) for i in range(2)]
            SS4 = [sbt(sb_, "ss4%d" % i, [128, 16]) for i in range(2)]
            lt4 = sbt(sb_, "lt4", [128, 4, 128])
            DMA("sp", ropec.t[:], D["ropec"], W=[ropec])
            DMA("sp", ropes.t[:], D["ropes"], W=[ropes])
            DMA("sp", maskb.t[:], D["maskb"], W=[maskb])
            DMA("sp", dalam.t[:], D["dalam"][l], W=[dalam])
            DMA("sp", gn.t[:], D["dang"][l], W=[gn])
            TT("dve", lt, lt.t[:, 0:64], dalam, dalam.t[:, 0:64], dalam, dalam.t[:, 64:128], ALU.mult)
            TT("dve", lt, lt.t[:, 64:128], dalam, dalam.t[:, 128:192], dalam, dalam.t[:, 192:256], ALU.mult)
            OP("dve", "tensor_reduce", lamv.t[:, 0:1], lt.t[:, 0:64], AX, ALU.add, R=[lt], W=[lamv])
            OP("dve", "tensor_reduce", lamv.t[:, 1:2], lt.t[:, 64:128], AX, ALU.add, R=[lt], W=[lamv])
            ACT(lamv, lamv.t[:, 0:2], lamv, lamv.t[:, 0:2], AF.Exp)
            TT("dve", lamv, lamv.t[:, 2:3], lamv, lamv.t[:, 1:2], lamv, lamv.t[:, 0:1], ALU.subtract)
            TS("dve", lamv, lamv.t[:, 2:3], lamv, lamv.t[:, 2:3], -LAM_INIT[l], None, ALU.add)
            OP("pool", "memset", Vt.t[:, :, :, 128:129], 1.0, W=Vt.k)
            lnc = sbt(sb_, "lnc", [128, 1])
            OP("pool", "memset", lnc.t[:], float(np.log(1.0 - LAM_INIT[l])), W=[lnc])

            for which, dst, c0 in (("q", qT, 0), ("k", kT, 512)):
                wa_ = load_w(D["w_in"][l][:, OFF["q_b"] + c0:OFF["q_b"] + c0 + 512], 8, 512)
                wp_ = load_w(D["w_qkp"][l][:, c0:c0 + 512], 8, 512)
                for h in range(4):
                    pa_, pb_ = (P[0], P[1]) if h % 2 == 0 else (P[2], P[3])
                    project(wa_, 8, h * 128, hn_rhs, pa_)
                    project(wp_, 8, h * 128, hn_rhs, pb_)
                    s1, s2 = scA(), scA()
                    TT("dve", s1, s1.t[:], pa_, pa_.t[:], ropec, ropec.t[:], ALU.mult)
                    TT("dve", s2, s2.t[:], pb_, pb_.t[:], ropes, ropes.t[:], ALU.mult)
                    if which == "q":
                        TT("dve", dst.k[h], dst.t[:, h, :], s1, s1.t[:], s2, s2.t[:], ALU.add)
                    else:
                        TT("dve", s1, s1.t[:], s1, s1.t[:], s2, s2.t[:], ALU.add)
                        DMA("sp", O["newk"][:, l, h, :], s1.t[:], R=[s1])
                        ACT(dst.k[h], dst.t[:, h, 0:1024], s1, s1.t[:], AF.Copy)
                        DMA("pool", dst.t[:, h, 1024:1536], D["kctx"][:, l, h, :], W=[dst.k[h]])
                        if h == 0:
                            tap("krot%d" % l, s1.t[:], [128, 1024], [s1])
            wv_ = load_w(D["w_in"][l][:, OFF["v_b"]:OFF["v_b"] + 512], 8, 512)
            wz_ = load_w(D["w_in"][l][:, OFF["z_b"]:OFF["z_b"] + 512], 8, 512)
            for h in range(4):
                project(wv_, 8, h * 128, hn_rhs, P[2])
                s1 = scA()
                ACT(s1, s1.t[:], P[2], P[2].t[:], AF.Copy)
                DMA("sp", O["newv"][:, l, h, :], s1.t[:], R=[s1])
                for tt in range(8):
                    OP("pe", "transpose", P[3].t[:, tt * 128:(tt + 1) * 128], s1.t[:, tt * 128:(tt + 1) * 128], ident.t[:],
                       R=[s1, ident], W=[P[3]])
                OP("act", "activation", Vt.t[:, 0:8, h, 0:128], P[3].t[:].rearrange("p (a e) -> p a e", a=8), AF.Copy,
                   R=[P[3]], W=Vt.k[0:8])
                DMA("pool", Vt.t[:, 8:12, h, 0:128], D["vctx"][:, l, :, h * 128:(h + 1) * 128], W=Vt.k[8:12])
                project(wz_, 8, h * 128, hn_rhs, P[h % 2])
                ACT(szb.k[h], szb.t[:, h, :], P[h % 2], P[h % 2].t[:], AF.Silu)
            PH = [TV(P[i // 2].t, "PH%d" % i) for i in range(4)]
            PT3 = TV(P[3].t, "PT3")
            for sub_, whole_ in [(PH[0], P[0]), (PH[1], P[0]), (PH[2], P[1]), (PH[3], P[1]), (PT3, P[3])]:
                sub_.b.writer = whole_.b.writer
                sub_.b.readers = dict(whole_.b.readers)
            blk_ = 0
            sci_ = 0
            exi = 0
            scale = 64 ** -0.5
            for h in range(4):
                for qb in range(2):
                    def acc(qt, m):
                        s = qt * 2 + m
                        bank, pos = s // 3, s % 3
                        pt = P[2] if bank < 2 else P[3]
                        off = (bank % 2) * 512 + pos * 160
                        return pt, pt.t[:, off:off + 129]
                    tiles = [(m, kt) for m in range(2) for kt in range(12)]

                    def emit_score(m, kt):
                        nonlocal sci_
                        ph = PH[sci_ % 4]
                        sc_ap = ph.t[:, (sci_ % 2) * 512:(sci_ % 2) * 512 + 512]
                        sci_ += 1
                        MM(ph, sc_ap, kT.k[h], kT.t[m * 64:(m + 1) * 64, h, kt * 128:(kt + 1) * 128],
                           qT.k[h], qT.t[m * 64:(m + 1) * 64, h, qb * 512:(qb + 1) * 512])
                        return ph, sc_ap

                    nxt = emit_score(*tiles[0])
                    for it_, (m, kt) in enumerate(tiles):
                        ph, sc_ap = nxt
                        if it_ + 1 < len(tiles):
                            nxt = emit_score(*tiles[it_ + 1])
                        ex = EX[exi % 3]
                        exi += 1
                        for sgl in range(2):
                            seg = qb * 2 + sgl
                            OP("act", "activation", ex.t[:, sgl * 256:(sgl + 1) * 256],
                               sc_ap[:, sgl * 256:(sgl + 1) * 256], AF.Exp,
                               bias=maskb.t[:, kt * 4 + seg:kt * 4 + seg + 1], scale=scale,
                               R=[ph, maskb], W=[ex])
                        for qt in range(4):
                            pt, aap = acc(qt, m)
                            first_in_bank = (qt * 2 + m) in (0, 4, 6, 1, 3, 7)
                            fw.op("pe", lambda e, aap=aap, ex=ex, qt=qt, kt=kt, fb=first_in_bank: e.matmul(
                                aap, lhsT=ex.t[:, qt * 128:(qt + 1) * 128], rhs=Vt.t[:, kt, h, :],
                                start=(kt == 0 and fb), stop=(kt == 11), skip_group_check=True),
                                reads=[ex.b, Vt.k[kt].b], writes=[pt.b])
                    accS = ACC[blk_ % 2]
                    o4 = OB4[blk_ % 2]
                    ss = SS4[blk_ % 2]
                    blk_ += 1
                    OP("dve", "tensor_copy", accS.t[:, 0:3, :], P[2].t[:, 0:480].rearrange("p (a e) -> p a e", a=3),
                       R=[P[2]], W=[accS])
                    OP("dve", "tensor_copy", accS.t[:, 3:6, :], P[2].t[:, 512:992].rearrange("p (a e) -> p a e", a=3),
                       R=[P[2]], W=[accS])
                    OP("dve", "tensor_copy", accS.t[:, 6:8, :], P[3].t[:, 0:320].rearrange("p (a e) -> p a e", a=2),
                       R=[P[3]], W=[accS])
                    OP("dve", "reciprocal", ss.t[:, 0:8], accS.t[:, :, 128], R=[accS], W=[ss])
                    TS("dve", ss, ss.t[:, 1:8:2], ss, ss.t[:, 1:8:2], lamv.t[:, 2:3], None, ALU.mult, RS=[lamv])
                    TT("dve", o4, o4.t[:], accS, accS.t[:, 0:8:2, 0:128], ss, col3(ss.t[:, 0:8:2], 128), ALU.mult)
                    TT("dve", lt4, lt4.t[:], accS, accS.t[:, 1:8:2, 0:128], ss, col3(ss.t[:, 1:8:2], 128), ALU.mult)
                    TT("dve", o4, o4.t[:], o4, o4.t[:], lt4, lt4.t[:], ALU.add)
                    TT("dve", lt4, lt4.t[:], o4, o4.t[:], o4, o4.t[:], ALU.mult)
                    OP("dve", "tensor_reduce", ss.t[:, 8:12], lt4.t[:], AX, ALU.add, R=[lt4], W=[ss])
                    TS("dve", ss, ss.t[:, 8:12], ss, ss.t[:, 8:12], 1.0 / 128, EPS, ALU.mult, ALU.add)
                    ACT(ss, ss.t[:, 8:12], ss, ss.t[:, 8:12], AF.Ln)
                    ACT(ss, ss.t[:, 12:16], ss, ss.t[:, 8:12], AF.Exp, RS=[lnc], scale=-0.5, bias=lnc.t[:, 0:1])
                    TT("dve", o4, o4.t[:], o4, o4.t[:], ss, col3(ss.t[:, 12:16], 128), ALU.mult)
                    TT("dve", o4, o4.t[:], o4, o4.t[:], gn,
                       gn.t[:].rearrange("p (o e) -> p o e", o=1).broadcast_to([128, 4, 128]), ALU.mult)
                    for qt in range(4):
                        OP("pe", "transpose", P[3].t[:, 512 + qt * 128:512 + (qt + 1) * 128], o4.t[:, qt, :], ident.t[:],
                           R=[o4, ident], W=[PT3])
                    TT("dve", outb.k[h], outb.t[:, h, qb * 512:(qb + 1) * 512], PT3, P[3].t[:, 512:1024],
                       szb.k[h], szb.t[:, h, qb * 512:(qb + 1) * 512], ALU.mult)
            tap("outb%d" % l, outb.t[:, 0, :], [128, 1024], [outb.k[0]])
            for whole_, subs_ in [(P[0], PH[0:2]), (P[1], PH[2:4]), (P[3], [PT3])]:
                for sub_ in subs_:
                    for k_, v_ in sub_.b.readers.items():
                        if whole_.b.readers.get(k_, 0) < v_:
                            whole_.b.readers[k_] = v_
                    w_ = sub_.b.writer
                    if w_ is not None:
                        if whole_.b.writer is None or (whole_.b.writer[0] == w_[0] and whole_.b.writer[1] < w_[1]):
                            whole_.b.writer = w_
                        elif whole_.b.writer[0] != w_[0]:
                            if whole_.b.readers.get(w_[0], 0) < w_[1]:
                                whole_.b.readers[w_[0]] = w_[1]
            merge_branch(l, 1, outb)
            fw.barrier()

    def branch_c(l):
        with ExitStack() as sc_:
            outc = sbt(sc_, "outc", [128, 4, 1024], BF16, nsub=4)
            cw = sbt(sc_, "cw", [128, 60])
            beta_t = sbt(sc_, "beta_t", [128, 8, 8]); g_t = sbt(sc_, "g_t", [128, 8, 8])
            alog = sbt(sc_, "alog", [128, 8]); dtb = sbt(sc_, "dtb", [128, 8]); gnc = sbt(sc_, "gnc", [128, 128])
            xp = sbt(sc_, "xp", [128, 4, 260])
            qf = sbt(sc_, "qf", [128, 1024]); kf = sbt(sc_, "kf", [128, 1024])
            k_tok = sbt(sc_, "k_tok", [128, 1024]); v_tok = sbt(sc_, "v_tok", [128, 1024])
            o_acc = sbt(sc_, "o_acc", [128, 1024])
            PTm = sbt(sc_, "PTm", [128, 1024]); RT = sbt(sc_, "RTm", [128, 1024])
            Dm = [sbt(sc_, "Dm%d" % d, [128, 1024]) for d in range(2)]
            DT = [sbt(sc_, "DT%d" % d, [128, 1024]) for d in range(2)]
            U = [sbt(sc_, "U%d" % d, [128, 1024]) for d in range(2)]
            WT = [sbt(sc_, "WT%d" % d, [128, 1024]) for d in range(2)]
            sml = [{n_: sbt(sc_, "%s%d" % (n_, d), [128, 8]) for n_ in ("gcol", "egc", "nbeta", "bge", "kds")}
                   for d in range(2)]
            egl = [sbt(sc_, "egl%d" % d, [128, 16]) for d in range(2)]
            S = [[sbt(sc_, "S%d%d" % (d, i), [128, 128]) for i in range(2)] for d in range(2)]
            VN = [[sbt(sc_, "VN%d%d" % (d, i), [128, 128]) for i in range(1)] * 2 for d in range(2)]
            otmp = [[sbt(sc_, "otmp%d%d" % (d, i), [128, 128]) for i in range(1)] * 2 for d in range(2)]
            rs8 = sbt(sc_, "rs8", [128, 8])
            gcr, Pm = SCA[0], rstd
            X1, X2 = SCA[1], SCA[2]
            epsc = sbt(sc_, "epsc", [128, 3])
            OP("pool", "memset", epsc.t[:, 0:1], EPS, W=[epsc])
            OP("pool", "memset", epsc.t[:, 1:2], 0.0, W=[epsc])
            OP("pool", "memset", epsc.t[:, 2:3], float(np.log(128 ** -0.5)), W=[epsc])
            DMA("sp", cw.t[:], D["dnconv"][l].rearrange("p j k -> p (j k)"), W=[cw])
            DMA("sp", alog.t[:], D["dnalog"][l], W=[alog])
            DMA("sp", dtb.t[:], D["dndt"][l], W=[dtb])
            DMA("sp", gnc.t[:], D["dnng"][l], W=[gnc])

            def r3(ap):
                return ap.rearrange("p (a e) -> p a e", a=8)

            def bc_tt(ap128):
                return ap128.rearrange("p (o e) -> p o e", o=1).broadcast_to([128, 8, 128])

            wba = load_w(D["w_in"][l][:, OFF["beta"]:OFF["beta"] + 16], 8, 16)
            for tt in range(8):
                for kt in range(8):
                    MM(P[0], P[0].t[:, tt * 16:(tt + 1) * 16], hnT.k[kt], hnT.t[:, kt, tt * 128:(tt + 1) * 128],
                       wba, wba.t[:, kt, 0:16], start=(kt == 0), stop=(kt == 7))
            pb = P[0].t[:, 0:128].rearrange("p (a c) -> p a c", a=8)
            ACT(beta_t, beta_t.t[:], P[0], pb[:, :, 0:8], AF.Sigmoid)
            dtbB = dtb.t[:].rearrange("p (o c) -> p o c", o=1).broadcast_to([128, 8, 8])
            TT("dve", g_t, g_t.t[:], P[0], pb[:, :, 8:16], dtb, dtbB, ALU.add)
            ACT(g_t, g_t.t[:], g_t, g_t.t[:], AF.Exp)
            ACT(g_t, g_t.t[:], g_t, g_t.t[:], AF.Ln, bias=1.0)
            ACT(alog, alog.t[:], alog, alog.t[:], AF.Exp)
            TS("dve", alog, alog.t[:], alog, alog.t[:], -1.0, None, ALU.mult)
            alB = alog.t[:].rearrange("p (o c) -> p o c", o=1).broadcast_to([128, 8, 8])
            TT("dve", g_t, g_t.t[:], g_t, g_t.t[:], alog, alB, ALU.mult)
            tap("g_t%d" % l, g_t.t[:], [128, 8, 8], [g_t])
            tap("beta_t%d" % l, beta_t.t[:], [128, 8, 8], [beta_t])
            if stop_after == "c0":
                halt[0] = True
                return


            QF = [qf, sbt(sc_, "qf1", [128, 1024])]

            def conv_phase(h):
                th = []
                wh = wb_next()
                for i_, nm_ in enumerate(("q_c", "k_c", "v_c", "z_c")):
                    c0_ = OFF[nm_] + h * 128
                    load_into(wh, i_ * 128, D["w_in"][l][:, c0_:c0_ + 128], 8, 128)
                qdst = QF[h % 2]
                tmp, l2t = X1, gcr

                def conv_tile(m, j):
                    pp = P[2]
                    a_ = X2
                    th.append(lambda: project(wh, 8, m * 128, hn_rhs, pp))
                    th.append(lambda: ACT(xp, xp.t[:, :, 2:258], pp, v3(pp.t[:]), AF.Copy))

                    def halo():
                        OP("dve", "memset", xp.t[:, 0, 0:2], 0.0, W=[xp])
                        OP("dve", "memset", xp.t[:, 3, 258:260], 0.0, W=[xp])
                        OP("dve", "tensor_scalar", xp.t[:, 1:4, 0:2], xp.t[:, 0:3, 256:258], flag.t[:, 0:1], None,
                           ALU.mult, R=[xp, flag], W=[xp])
                        OP("dve", "tensor_scalar", xp.t[:, 0:3, 258:260], xp.t[:, 1:4, 2:4], flag.t[:, 0:1], None,
                           ALU.mult, R=[xp, flag], W=[xp])
                    th.append(halo)
                    th.append(lambda: TS("dve", a_, v3(a_.t[:]), xp, xp.t[:, :, 0:256], cw.t[:, j * 5:j * 5 + 1], None,
                                         ALU.mult, RS=[cw]))
                    for k in range(1, 5):
                        th.append(lambda k=k: STT("dve", a_, v3(a_.t[:]), xp, xp.t[:, :, k:k + 256],
                                                  cw.t[:, j * 5 + k:j * 5 + k + 1], a_, v3(a_.t[:]), ALU.mult, ALU.add,
                                                  RS=[cw]))
                    th.append(lambda: ACT(tmp, tmp.t[:], a_, a_.t[:], AF.Silu))

                def l2n(dstT, scl):
                    def f1():
                        sb_ = scB()
                        ACT(sb_, sb_.t[:], tmp, tmp.t[:], AF.Square)
                        for half in range(2):
                            MM(P[3], P[3].t[:, half * 512:(half + 1) * 512], onesb, onesb.t[:], sb_,
                               sb_.t[:, half * 512:(half + 1) * 512])
                    th.append(f1)
                    th.append(lambda: ACT(l2t, l2t.t[:], P[3], P[3].t[:], AF.Ln, RS=[epsc], bias=epsc.t[:, 0:1]))
                    th.append(lambda: ACT(l2t, l2t.t[:], l2t, l2t.t[:], AF.Exp, RS=[epsc], scale=-0.5,
                                          bias=epsc.t[:, 1:2] if scl == 1.0 else epsc.t[:, 2:3]))
                    th.append(lambda: TT("dve", dstT, dstT.t[:], tmp, tmp.t[:], l2t, l2t.t[:], ALU.mult))

                conv_tile(0, h)
                l2n(qdst, 128 ** -0.5)
                conv_tile(1, 4 + h)
                l2n(kf, 1.0)

                def ktr():
                    for tt in range(8):
                        OP("pe", "transpose", P[3].t[:, tt * 128:(tt + 1) * 128], kf.t[:, tt * 128:(tt + 1) * 128],
                           ident.t[:], R=[kf, ident], W=[P[3]])
                    ACT(k_tok, k_tok.t[:], P[3], P[3].t[:], AF.Copy)
                th.append(ktr)
                conv_tile(2, 8 + h)

                def vtr():
                    for tt in range(8):
                        OP("pe", "transpose", P[3].t[:, tt * 128:(tt + 1) * 128], tmp.t[:, tt * 128:(tt + 1) * 128],
                           ident.t[:], R=[tmp, ident], W=[P[3]])
                    ACT(v_tok, v_tok.t[:], P[3], P[3].t[:], AF.Copy)
                th.append(vtr)
                return th, wh

            pending, wh_next = conv_phase(0)
            for h in range(4):
                for f_ in pending:
                    f_()
                pending = []
                wh = wh_next
                qf = QF[h % 2]
                if h == 0:
                    tap("qf%d" % l, qf.t[:], [128, 1024], [qf])
                    tap("kf%d" % l, kf.t[:], [128, 1024], [kf])
                    tap("vtok%d" % l, v_tok.t[:], [128, 1024], [v_tok])

                for dr in range(2):
                    cb = dr * 4 + h
                    sm_ = sml[dr]
                    gcol, egc, nbeta, bge, kds = (sm_[n_] for n_ in ("gcol", "egc", "nbeta", "bge", "kds"))
                    TRI = cm.t[:, dr, :]
                    MASK = cm.t[:, 1 - dr, :]
                    for tt in range(8):
                        OP("act", "activation", X1.t[:, tt * 128:(tt + 1) * 128], cm.t[:, 2, :], AF.Copy,
                           scale=g_t.t[:, tt, cb:cb + 1], R=[cm, g_t], W=[X1])
                    for tt in range(8):
                        MM(P[0], P[0].t[:, tt * 128:(tt + 1) * 128], X1, X1.t[:, tt * 128:(tt + 1) * 128], cm, TRI)
                        MM(P[1], P[1].t[:, tt:tt + 1], cm, TRI, g_t, g_t.t[:, tt, cb:cb + 1])
                        MM(P[2], P[2].t[:, tt * 128:(tt + 1) * 128], kf, kf.t[:, tt * 128:(tt + 1) * 128],
                           kf, kf.t[:, tt * 128:(tt + 1) * 128])
                    ACT(gcr, gcr.t[:], P[0], P[0].t[:], AF.Copy)
                    OP("dve", "tensor_copy", gcol.t[:], P[1].t[:, 0:8], R=[P[1]], W=[gcol])
                    ACT(egc, egc.t[:], gcol, gcol.t[:], AF.Exp)
                    TS("dve", nbeta, nbeta.t[:], beta_t, beta_t.t[:, :, cb], -1.0, None, ALU.mult)
                    TT("dve", bge, bge.t[:], beta_t, beta_t.t[:, :, cb], egc, egc.t[:], ALU.mult)
                    if stop_after == "c2a0":
                        halt[0] = True
                        return
                    D_ = Dm[dr]
                    TT("dve", D_, r3(D_.t[:]), gcol, col3(gcol.t[:], 128), gcr, r3(gcr.t[:]), ALU.subtract)
                    TS("dve", D_, D_.t[:], D_, D_.t[:], 0.0, None, ALU.min)
                    ACT(D_, D_.t[:], D_, D_.t[:], AF.Exp)
                    TT("dve", D_, r3(D_.t[:]), D_, r3(D_.t[:]), cm, bc_tt(MASK), ALU.mult)
                    TT("dve", X2, r3(X2.t[:]), D_, r3(D_.t[:]), ident, bc_tt(ident.t[:]), ALU.subtract)
                    TT("dve", Pm, r3(Pm.t[:]), P[2], r3(P[2].t[:]), nbeta, col3(nbeta.t[:], 128), ALU.mult)
                    TT("dve", Pm, Pm.t[:], Pm, Pm.t[:], X2, X2.t[:], ALU.mult)
                    if stop_after == "c2a1":
                        halt[0] = True
                        return
                    for tt in range(8):
                        OP("pe", "transpose", P[3].t[:, tt * 128:(tt + 1) * 128], Pm.t[:, tt * 128:(tt + 1) * 128],
                           ident.t[:], R=[Pm, ident], W=[P[3]])
                    ACT(PTm, PTm.t[:], P[3], P[3].t[:], AF.Copy)
                    if stop_after == "c2a2":
                        halt[0] = True
                        return
                    TT("dve", RT, r3(RT.t[:]), PTm, r3(PTm.t[:]), ident, bc_tt(ident.t[:]), ALU.add)
                    if stop_after == "c2a3":
                        halt[0] = True
                        return
                    for tt in range(8):
                        OP("pe", "transpose", P[0].t[:, tt * 128:(tt + 1) * 128], D_.t[:, tt * 128:(tt + 1) * 128],
                           ident.t[:], R=[D_, ident], W=[P[0]])
                    ACT(DT[dr], DT[dr].t[:], P[0], P[0].t[:], AF.Copy)
                    if stop_after == "c2a":
                        halt[0] = True
                        return
                    for k in range(1, 6):
                        for tt in range(8):
                            sl = slice(tt * 128, (tt + 1) * 128)
                            MM(P[0], P[0].t[:, sl], PTm, PTm.t[:, sl], Pm, Pm.t[:, sl])
                            if k < 5:
                                MM(P[1], P[1].t[:, sl], Pm, Pm.t[:, sl], PTm, PTm.t[:, sl])
                        ACT(Pm, Pm.t[:], P[0], P[0].t[:], AF.Copy)
                        if k < 5:
                            OP("dve", "tensor_copy", PTm.t[:], P[1].t[:], R=[P[1]], W=[PTm])
                        for tt in range(8):
                            sl = slice(tt * 128, (tt + 1) * 128)
                            MM(P[2], P[2].t[:, sl], Pm, Pm.t[:, sl], RT, RT.t[:, sl])
                        TT("dve", RT, RT.t[:], RT, RT.t[:], P[2], P[2].t[:], ALU.add)
                    if stop_after == "c2b":
                        halt[0] = True
                        return
                    TT("dve", X1, r3(X1.t[:]), v_tok, r3(v_tok.t[:]), beta_t, col3(beta_t.t[:, :, cb], 128), ALU.mult)
                    TT("dve", X2, r3(X2.t[:]), k_tok, r3(k_tok.t[:]), bge, col3(bge.t[:], 128), ALU.mult)
                    for tt in range(8):
                        sl = slice(tt * 128, (tt + 1) * 128)
                        MM(P[0], P[0].t[:, sl], RT, RT.t[:, sl], X1, X1.t[:, sl])
                        MM(P[1], P[1].t[:, sl], X2, X2.t[:, sl], RT, RT.t[:, sl])
                        MM(P[2], P[2].t[:, sl], kf, kf.t[:, sl], qf, qf.t[:, sl])
                    ACT(U[dr], U[dr].t[:], P[0], P[0].t[:], AF.Copy)
                    ACT(WT[dr], WT[dr].t[:], P[1], P[1].t[:], AF.Copy)
                    TT("dve", DT[dr], DT[dr].t[:], P[2], P[2].t[:], DT[dr], DT[dr].t[:], ALU.mult)
                    g3 = r3(gcr.t[:])
                    for hp in (0, 64):
                        lc = hp + 63 if dr == 0 else hp
                        TT("dve", kds, kds.t[hp:hp + 64, :], gcr, g3[hp:hp + 64, :, lc], gcol, gcol.t[hp:hp + 64, :],
                           ALU.subtract)
                    ACT(kds, kds.t[:], kds, kds.t[:], AF.Exp)
                    lc0 = 63 if dr == 0 else 0
                    ACT(egl[dr], egl[dr].t[:].rearrange("p (a c) -> p a c", a=8), gcr, g3[:, :, lc0::64], AF.Exp)
                    TT("dve", D_, r3(D_.t[:]), k_tok, r3(k_tok.t[:]), kds, col3(kds.t[:], 128), ALU.mult)
                    if h == 0 and dr == 0:
                        tap("U%d" % l, U[0].t[:], [128, 1024], [U[0]])
                        tap("RT%d" % l, RT.t[:], [128, 1024], [RT])
                if stop_after == "c2":
                    halt[0] = True
                    return
                OP("pool", "memset", o_acc.t[:], 0.0, W=[o_acc])
                if h + 1 < 4:
                    pending, wh_next = conv_phase(h + 1)
                per_step = (len(pending) + 15) // 16
                cur = [0, 0]
                for dr in range(2):
                    DMA("sp", S[dr][0].t[:], D["dns0"][:, l, dr, h, :], W=[S[dr][0]])
                for step in range(16):
                    for dr in range(2):
                        c = step if dr == 0 else 15 - step
                        tt, hp, ci = c // 2, (c % 2) * 64, c % 2
                        sm_ = sml[dr]
                        egc = sm_["egc"]
                        Sc = S[dr][cur[dr]]
                        Sn = S[dr][1 - cur[dr]]
                        boundary = (dr == 0 and c % 4 == 0 and c > 0) or (dr == 1 and c % 4 == 3 and c < 15)
                        if boundary:
                            seq = c // 4 - 1 if dr == 0 else c // 4 + 1
                            DMA("sp", O["ndn"][:, l, dr, h, seq, :], Sc.t[:], R=[Sc])
                            TS("dve", Sn, Sn.t[:], Sc, Sc.t[:], flag.t[:, 0:1], None, ALU.mult, RS=[flag])
                            cur[dr] = 1 - cur[dr]
                            Sc, Sn = Sn, Sc
                        pd = P[dr]
                        pc = (step % 2) * 512
                        vn = VN[dr][step % 2]
                        ot = otmp[dr][step % 2]
                        MM(pd, pd.t[hp:hp + 64, pc:pc + 128], WT[dr], WT[dr].t[:, tt * 128 + hp:tt * 128 + hp + 64], Sc, Sc.t[:])
                        MM(pd, pd.t[hp:hp + 64, pc + 128:pc + 256], qf, qf.t[:, c * 64:(c + 1) * 64], Sc, Sc.t[:])
                        TT("dve", vn, vn.t[hp:hp + 64, :], U[dr], U[dr].t[hp:hp + 64, tt * 128:(tt + 1) * 128],
                           pd, pd.t[hp:hp + 64, pc:pc + 128], ALU.subtract)
                        MM(pd, pd.t[hp:hp + 64, pc + 256:pc + 384],
                           DT[dr], DT[dr].t[hp:hp + 64, tt * 128 + hp:tt * 128 + hp + 64], vn, vn.t[hp:hp + 64, :])
                        MM(pd, pd.t[:, pc + 384:pc + 512], Dm[dr], Dm[dr].t[hp:hp + 64, tt * 128:(tt + 1) * 128],
                           vn, vn.t[hp:hp + 64, :])
                        TS("dve", ot, ot.t[hp:hp + 64, :], pd, pd.t[hp:hp + 64, pc + 128:pc + 256],
                           egc.t[hp:hp + 64, tt:tt + 1], None, ALU.mult, RS=[egc])
                        TT("dve", ot, ot.t[hp:hp + 64, :], ot, ot.t[hp:hp + 64, :], pd, pd.t[hp:hp + 64, pc + 256:pc + 384],
                           ALU.add)
                        TT("pool", o_acc, o_acc.t[hp:hp + 64, tt * 128:(tt + 1) * 128],
                           o_acc, o_acc.t[hp:hp + 64, tt * 128:(tt + 1) * 128], ot, ot.t[hp:hp + 64, :], ALU.add)
                        STT("dve", Sn, Sn.t[:], Sc, Sc.t[:], egl[dr].t[:, tt * 2 + ci:tt * 2 + ci + 1],
                            pd, pd.t[:, pc + 384:pc + 512], ALU.mult, ALU.add, RS=[egl[dr]])
                        cur[dr] = 1 - cur[dr]
                    for _ in range(per_step):
                        if pending:
                            pending.pop(0)()
                for dr in range(2):
                    seq = 3 if dr == 0 else 0
                    Sc = S[dr][cur[dr]]
                    DMA("sp", O["ndn"][:, l, dr, h, seq, :], Sc.t[:], R=[Sc])
                if h == 0:
                    tap("oacc%d" % l, o_acc.t[:], [128, 1024], [o_acc])
                if stop_after == "c3":
                    halt[0] = True
                    return
                TT("dve", X1, X1.t[:], o_acc, o_acc.t[:], o_acc, o_acc.t[:], ALU.mult)
                OP("dve", "tensor_reduce", rs8.t[:], r3(X1.t[:]), AX, ALU.add, R=[X1], W=[rs8])
                TS("dve", rs8, rs8.t[:], rs8, rs8.t[:], 1.0 / 128, EPS, ALU.mult, ALU.add)
                OP("dve", "reciprocal", rs8.t[:], rs8.t[:], R=[rs8], W=[rs8])
                ACT(rs8, rs8.t[:], rs8, rs8.t[:], AF.Sqrt)
                TT("dve", X1, r3(X1.t[:]), o_acc, r3(o_acc.t[:]), rs8, col3(rs8.t[:], 128), ALU.mult)
                TT("dve", X1, r3(X1.t[:]), X1, r3(X1.t[:]), gnc, bc_tt(gnc.t[:]), ALU.mult)
                for tt in range(8):
                    OP("pe", "transpose", P[2].t[:, tt * 128:(tt + 1) * 128], X1.t[:, tt * 128:(tt + 1) * 128], ident.t[:],
                       R=[X1, ident], W=[P[2]])
                project(wh, 8, 3 * 128, hn_rhs, P[3])
                z_ = scB()
                ACT(z_, z_.t[:], P[3], P[3].t[:], AF.Silu)
                TT("dve", outc.k[h], outc.t[:, h, :], P[2], P[2].t[:], z_, z_.t[:], ALU.mult)
            tap("outc%d" % l, outc.t[:, 0, :], [128, 1024], [outc.k[0]])
            merge_branch(l, 2, outc)
            fw.barrier()

    for l in range(nlayers):
        norm_modulate(l)
        if stop_after == "norm":
            break
        if "a" not in skip:
            branch_a(l)
        if stop_after == "a":
            break
        if "b" not in skip:
            branch_b(l)
        if stop_after == "b":
            break
        branch_c(l)
        if stop_after == "c" or halt[0]:
            break
        out_proj_residual(l)
    rms_stats()
    fng = sbt(top, "fng", [128, 8])
    DMA("sp", fng.t[:], D["fng"], W=[fng])
    for kt in range(8):
        sa = scA()
        OP("dve", "scalar_tensor_tensor", sa.t[:], xT.t[:, kt, :], fng.t[:, kt:kt + 1], rstd.t[:],
           ALU.mult, ALU.mult, R=[xT.k[kt], fng, rstd], W=[sa])
        DMA("sp", O["yT"][:, kt, :], sa.t[:], R=[sa])
    fw.finish()
    n_inst = fw.n_inst
    top.close()
    return nc, list(DBG.keys()), n_inst


_CACHE = {}


def kernel(**inputs):
    maps = prep_inputs(inputs)
    if "nc" not in _CACHE:
        _CACHE["nc"] = build()[0]
    nc = _CACHE["nc"]
    res = run_bass_kernel_spmd(nc, maps, core_ids=list(range(8)))
    return assemble(res.results)
```

```python
import numpy as np
import concourse.bass as bass
import concourse.mybir as mybir
from concourse.bass_utils import run_bass_kernel_spmd

F32 = mybir.dt.float32
BF16 = mybir.dt.bfloat16
ALU = mybir.AluOpType
AF = mybir.ActivationFunctionType


class Buf:
    def __init__(self, name):
        self.name = name
        self.writer = None
        self.readers = {}


class Eng:
    def __init__(self, fw, name, eng, sem):
        self.fw, self.name, self.eng, self.sem = fw, name, eng, sem
        self.count = 0
        self.known = {}


class FW:
    NDMA = 8

    def __init__(self, nc, stack):
        self.nc = nc
        self.sems = {}
        self.engs = {}
        for name, eng in (("pe", nc.tensor), ("dve", nc.vector), ("act", nc.scalar),
                          ("pool", nc.gpsimd), ("sp", nc.sync)):
            sem = stack.enter_context(nc.semaphore("s_" + name))
            self.sems[name] = sem
            self.engs[name] = Eng(self, name, eng, sem)
        self.dma_sems = {}
        self.dma_cnt = {}
        self.dma_idx = {}
        for q in ("sp", "act", "pool"):
            self.dma_sems[q] = []
            for i in range(self.NDMA):
                key = "d_%s%d" % (q, i)
                sem = stack.enter_context(nc.semaphore(key))
                self.sems[key] = sem
                self.dma_sems[q].append(key)
                self.dma_cnt[key] = 0
            self.dma_idx[q] = 0
        self.n_inst = 0

    def _deps(self, ename, reads, writes):
        deps = {}

        def add(k, v):
            if v > deps.get(k, 0):
                deps[k] = v

        for b in reads:
            if b.writer is not None:
                add(*b.writer)
        for b in writes:
            if b.writer is not None:
                k, v = b.writer
                if not (k == "pe" and ename == "pe"):
                    add(k, v)
            for k, v in b.readers.items():
                add(k, v)
        return deps

    def _emit_waits(self, e, deps):
        for k, v in deps.items():
            if e.known.get(k, 0) < v:
                e.eng.wait_ge(self.sems[k], v)
                e.known[k] = v

    def _record(self, key, val, reads, writes):
        for b in reads:
            if b.readers.get(key, 0) < val:
                b.readers[key] = val
        for b in writes:
            b.writer = (key, val)
            b.readers = {}

    def op(self, ename, fn, reads=(), writes=()):
        e = self.engs[ename]
        deps = self._deps(ename, reads, writes)
        if ename == "pe":
            deps.pop("pe", None)
        self._emit_waits(e, deps)
        ins = fn(e.eng)
        e.count += 1
        ins.then_inc(e.sem, 1)
        e.known[ename] = max(e.known.get(ename, 0), 0)
        self._record(ename, e.count, reads, writes)
        self.n_inst += 1
        return ins

    def dma(self, q, out, in_, reads=(), writes=(), **kw):
        e = self.engs[q]
        ring = self.dma_sems[q]
        key = ring[self.dma_idx[q] % self.NDMA]
        self.dma_idx[q] += 1
        deps = self._deps("dma", reads, writes)
        if self.dma_cnt[key] > 0:
            deps[key] = max(deps.get(key, 0), self.dma_cnt[key])
        self._emit_waits(e, deps)
        e.eng.dma_start(out=out, in_=in_, **kw).then_inc(self.sems[key], 16)
        self.dma_cnt[key] += 16
        self._record(key, self.dma_cnt[key], reads, writes)
        self.n_inst += 1

    def finish(self):
        e = self.engs["sp"]
        for key, cnt in self.dma_cnt.items():
            if cnt > 0 and e.known.get(key, 0) < cnt:
                e.eng.wait_ge(self.sems[key], cnt)
                e.known[key] = cnt
        for name in ("pe", "dve", "act", "pool"):
            c = self.engs[name].count
            if c > 0:
                e.eng.wait_ge(self.sems[name], c)

    def barrier(self):
        snap = {n: self.engs[n].count for n in ("pe", "dve", "act", "pool")}
        dsnap = dict(self.dma_cnt)
        for n in ("pe", "dve", "act", "pool", "sp"):
            e = self.engs[n]
            for k, v in list(snap.items()) + list(dsnap.items()):
                if v > 0 and k != n and e.known.get(k, 0) < v:
                    e.eng.wait_ge(self.sems[k], v)
                    e.known[k] = v


class TV:
    def __init__(self, t, name):
        self.t = t
        self.b = Buf(name)


class T:
    def __init__(self, t, name, nsub=0):
        self.t = t
        self.b = None if nsub else Buf(name)
        self.k = [TV(t, "%s.%d" % (name, i)) for i in range(nsub)]


NT = 1024
DM = 1024
NIN = 8208
EPS = 1e-6
OFF = dict(u_a=0, z_a=512, q_b=1024, k_b=1536, v_b=2048, z_b=2560, q_c=3072, k_c=3584,
           v_c=4096, z_c=4608, beta=5120, alpha=5128, gates=5136)
LAM_INIT = [0.8 - 0.6 * float(np.exp(-0.3 * l)) for l in range(2)]
S5_BF16 = True

IN_SPECS = [
    ("xT", [128, 8, 1024]), ("cond", [128, 8]), ("flag", [128, 1]),
    ("ropec", [128, 1024]), ("ropes", [128, 1024]), ("maskb", [128, 48]),
    ("kctx", [128, 2, 4, 512]), ("vctx", [128, 2, 4, 512]),
    ("s5h0r", [128, 2, 2, 16]), ("s5h0i", [128, 2, 2, 16]), ("dns0", [128, 2, 2, 4, 128]),
    ("w_ada", [2, 1024, 3072]), ("b_ada", [2, 128, 24]), ("norm_g", [2, 128, 8]), ("fng", [128, 8]),
    ("w_in", [2, 1024, NIN]), ("w_qkp", [2, 1024, 1024]),
    ("lamre_c", [2, 128, 2, 4, 64]), ("lamim_c", [2, 128, 2, 4, 64]), ("lstep_c", [2, 128, 2, 4, 64]),
    ("bre_c", [2, 128, 2, 4, 64]), ("bim_c", [2, 128, 2, 4, 64]),
    ("lamre_s", [2, 128, 2, 16]), ("lamim_s", [2, 128, 2, 16]), ("lstep_s", [2, 128, 2, 16]),
    ("cre_s", [2, 128, 2, 16, 16]), ("cim_s", [2, 128, 2, 16, 16]),
    ("s5d", [2, 128, 4]), ("wglu", [2, 512, 512]), ("dalam", [2, 128, 256]), ("dang", [2, 128, 128]),
    ("dnconv", [2, 128, 12, 5]), ("dnalog", [2, 128, 8]), ("dndt", [2, 128, 8]), ("dnng", [2, 128, 128]),
    ("w_branch", [2, 3, 512, 1024]), ("w_out", [2, 1024, 1024]),
    ("ident", [128, 128]), ("maskB", [128, 8]), ("cm", [128, 3, 128]),
]
OUT_SPECS = [
    ("yT", [128, 8, 1024]), ("newk", [128, 2, 4, 1024]), ("newv", [128, 2, 4, 1024]),
    ("ns5r", [128, 2, 2, 16, 4]), ("ns5i", [128, 2, 2, 16, 4]), ("ndn", [128, 2, 2, 4, 4, 128]),
]


def _f(a):
    return np.ascontiguousarray(np.asarray(a, dtype=np.float32))


def _rope_tables():
    t = np.arange(1024)
    row = (t // 64).astype(np.float32)
    col = (t % 64).astype(np.float32)
    nf = 16
    inv = (np.float32(10000.0) ** (-np.arange(nf, dtype=np.float32) / np.float32(nf))).astype(np.float32)
    cosT = np.zeros((128, 1024), np.float32)
    sinT = np.zeros((128, 1024), np.float32)
    for p in range(128):
        d = p % 64
        pos = row if d < 32 else col
        j = d % 16
        ang = (pos * inv[j]).astype(np.float32)
        first = (d % 32) < 16
        cosT[p] = np.cos(ang)
        sinT[p] = -np.sin(ang) if first else np.sin(ang)
    return cosT, sinT


def _const_masks():
    i = np.arange(128)
    same = (i[:, None] // 64) == (i[None, :] // 64)
    cm = np.zeros((128, 3, 128), np.float32)
    cm[:, 0, :] = same & (i[:, None] <= i[None, :])
    cm[:, 1, :] = same & (i[:, None] >= i[None, :])
    cm[:, 2, :] = 1.0
    return cm


def prep_inputs(inp):
    g = {k: np.asarray(v) for k, v in inp.items()}
    sh = {}
    sh["w_ada"] = _f(g["w_ada"])
    sh["b_ada"] = _f(g["b_ada"].reshape(2, 24, 128).transpose(0, 2, 1))
    sh["norm_g"] = _f(g["norm_g"].reshape(2, 8, 128).transpose(0, 2, 1))
    sh["fng"] = _f(g["final_norm_g"].reshape(8, 128).T)
    sh["w_in"] = _f(g["w_in"])
    d = np.arange(64)
    perm = np.where((d % 32) < 16, d + 16, d - 16)
    cols = []
    for base in (OFF["q_b"], OFF["k_b"]):
        for hm in range(8):
            cols.append(base + hm * 64 + perm)
    cols = np.concatenate(cols)
    sh["w_qkp"] = _f(g["w_in"][:, :, cols])
    def chmaj(a):
        a = a.reshape(2, 2, 4, 8, 64)
        a = np.broadcast_to(a[:, :, :, :, None, :], (2, 2, 4, 8, 16, 64))
        return _f(a.transpose(0, 3, 4, 1, 2, 5).reshape(2, 128, 2, 4, 64))
    sh["lamre_c"] = chmaj(g["s5_lam_re"])
    sh["lamim_c"] = chmaj(g["s5_lam_im"])
    sh["lstep_c"] = chmaj(np.broadcast_to(g["s5_log_step"][..., None], (2, 2, 32, 64)))
    def bmaj(a):
        a = a.reshape(2, 2, 4, 8, 64, 16)
        return _f(a.transpose(0, 3, 5, 1, 2, 4).reshape(2, 128, 2, 4, 64))
    sh["bre_c"] = bmaj(g["s5_b_re"])
    sh["bim_c"] = bmaj(g["s5_b_im"])
    def stmaj(a):
        a = a.reshape(2, 2, 16, 2, 64)
        return _f(a.transpose(0, 3, 4, 1, 2).reshape(2, 128, 2, 16))
    sh["lamre_s"] = stmaj(g["s5_lam_re"])
    sh["lamim_s"] = stmaj(g["s5_lam_im"])
    sh["lstep_s"] = stmaj(np.broadcast_to(g["s5_log_step"][..., None], (2, 2, 32, 64)))
    def cmaj(a):
        a = a.reshape(2, 2, 16, 2, 16, 64)
        return _f(a.transpose(0, 3, 5, 1, 2, 4).reshape(2, 128, 2, 16, 16))
    sh["cre_s"] = cmaj(g["s5_c_re"])
    sh["cim_s"] = cmaj(g["s5_c_im"])
    sh["s5d"] = _f(g["s5_d"].reshape(2, 4, 128).transpose(0, 2, 1))
    sh["wglu"] = _f(g["s5_w_glu"])
    sh["dalam"] = _f(np.broadcast_to(g["da_lam"].reshape(2, 1, 256), (2, 128, 256)))
    sh["dang"] = _f(np.broadcast_to(g["da_norm_g"].reshape(2, 1, 128), (2, 128, 128)))
    sh["dnconv"] = _f(g["dn_conv"].reshape(2, 5, 12, 128).transpose(0, 3, 2, 1))
    sh["dnalog"] = _f(np.broadcast_to(g["dn_a_log"].reshape(2, 1, 8), (2, 128, 8)))
    sh["dndt"] = _f(np.broadcast_to(g["dn_dt_bias"].reshape(2, 1, 8), (2, 128, 8)))
    sh["dnng"] = _f(np.broadcast_to(g["dn_norm_g"].reshape(2, 1, 128), (2, 128, 128)))
    sh["w_branch"] = _f(g["w_branch"])
    sh["w_out"] = _f(g["w_out"])
    sh["ident"] = np.eye(128, dtype=np.float32)
    p = np.arange(128)
    mB = np.zeros((128, 8), np.float32)
    for st in range(4):
        for gl in range(2):
            mB[:, st * 2 + gl] = ((p // 32) == st) & (((p // 16) % 2) == gl)
    sh["maskB"] = mB
    sh["cm"] = _const_masks()
    cosT, sinT = _rope_tables()
    maps = []
    for c in range(8):
        m = dict(sh)
        if c < 4:
            X = g["x_prompt"][4 * c:4 * c + 4].reshape(1024, 1024)
            cond = g["c_ctx"]
            m["flag"] = np.zeros((128, 1), np.float32)
            m["ropec"] = np.ones((128, 1024), np.float32)
            m["ropes"] = np.zeros((128, 1024), np.float32)
            mb = np.full((12, 4), -30000.0, np.float32)
            for kt in range(8):
                mb[kt, kt // 2] = 0.0
            m["kctx"] = np.zeros((128, 2, 4, 512), np.float32)
            m["vctx"] = np.zeros((128, 2, 4, 512), np.float32)
            m["s5h0r"] = np.zeros((128, 2, 2, 16), np.float32)
            m["s5h0i"] = np.zeros((128, 2, 2, 16), np.float32)
            m["dns0"] = np.zeros((128, 2, 2, 4, 128), np.float32)
        else:
            b = c - 4
            X = g["x_sample"][b]
            cond = g["c"][b]
            m["flag"] = np.ones((128, 1), np.float32)
            m["ropec"] = cosT
            m["ropes"] = sinT
            mb = np.zeros((12, 4), np.float32)
            ck = g["cache_k"][b]
            m["kctx"] = _f(ck.transpose(3, 4, 0, 2, 1).reshape(128, 2, 4, 512))
            cv = g["cache_v"][b]
            m["vctx"] = _f(cv.reshape(2, 4, 128, 512).transpose(2, 0, 1, 3))
            def st5(a):
                a = a.reshape(2, 2, 16, 2, 64)
                return _f(a.transpose(3, 4, 0, 1, 2).reshape(128, 2, 2, 16))
            m["s5h0r"] = st5(g["state_s5_re"][b])
            m["s5h0i"] = st5(g["state_s5_im"][b])
            m["dns0"] = _f(g["state_dn"][b].transpose(3, 0, 1, 2, 4))
        m["maskb"] = _f(np.broadcast_to(mb.reshape(1, 48), (128, 48)))
        m["xT"] = _f(X.T.reshape(8, 128, 1024).transpose(1, 0, 2))
        m["cond"] = _f(cond.reshape(8, 128).T)
        maps.append(m)
    return maps


def assemble(results):
    y_prompt = np.zeros((16, 256, 1024), np.float32)
    y_sample = np.zeros((4, 1024, 1024), np.float32)
    nk = np.zeros((16, 2, 256, 4, 2, 64), np.float32)
    nv = np.zeros((16, 2, 256, 4, 128), np.float32)
    s5r = np.zeros((16, 2, 2, 32, 64), np.float32)
    s5i = np.zeros((16, 2, 2, 32, 64), np.float32)
    sdn = np.zeros((16, 2, 2, 4, 128, 128), np.float32)
    for c in range(8):
        r = results[c]
        Y = np.asarray(r["yT"]).transpose(1, 0, 2).reshape(1024, 1024).T
        if c < 4:
            y_prompt[4 * c:4 * c + 4] = Y.reshape(4, 256, 1024)
            k = np.asarray(r["newk"]).reshape(2, 64, 2, 4, 4, 256)
            nk[4 * c:4 * c + 4] = k.transpose(4, 2, 5, 3, 0, 1)
            v = np.asarray(r["newv"]).reshape(128, 2, 4, 4, 256)
            nv[4 * c:4 * c + 4] = v.transpose(3, 1, 4, 2, 0)
            for nm, dst in (("ns5r", s5r), ("ns5i", s5i)):
                a = np.asarray(r[nm]).reshape(2, 64, 2, 2, 16, 4)
                dst[4 * c:4 * c + 4] = a.transpose(5, 2, 3, 4, 0, 1).reshape(4, 2, 2, 32, 64)
            a = np.asarray(r["ndn"])
            sdn[4 * c:4 * c + 4] = a.transpose(4, 1, 2, 3, 0, 5)
        else:
            y_sample[c - 4] = Y
    return (y_prompt, y_sample, nk, nv, s5r, s5i, sdn)


def build(dbg=(), stop_after=None, nlayers=2, skip=()):
    from contextlib import ExitStack
    nc = bass.Bass("TRN2", target_bir_lowering=False)
    D = {n: nc.dram_tensor(n, list(s), F32, kind="ExternalInput").ap() for n, s in IN_SPECS}
    O = {n: nc.dram_tensor(n, list(s), F32, kind="ExternalOutput").ap() for n, s in OUT_SPECS}
    DBG = {}
    top = ExitStack()
    fw = FW(nc, top)

    uniq = [0]
    halt = [False]

    def sbt(stk, name, shape, dt=F32, nsub=0):
        uniq[0] += 1
        return T(stk.enter_context(nc.sbuf_tensor("t%d_%s" % (uniq[0], name), list(shape), dt)), name, nsub)

    def OP(eng, meth, *a, R=(), W=(), **kw):
        return fw.op(eng, lambda e: getattr(e, meth)(*a, **kw),
                     reads=[x.b for x in R], writes=[x.b for x in W])

    def DMA(q, out, in_, R=(), W=(), **kw):
        fw.dma(q, out, in_, reads=[x.b for x in R], writes=[x.b for x in W], **kw)

    def tap(name, src_ap, shape, R):
        if name in dbg:
            DBG[name] = nc.dram_tensor("dbg_" + name, list(shape), F32, kind="ExternalOutput").ap()
            q = "sp" if src_ap.tensor.dtype == F32 else "pool"
            DMA(q, DBG[name], src_ap, R=R)

    def MM(outT, out_ap, lhsT_T, lhsT_ap, rhs_T, rhs_ap, start=True, stop=True):
        return fw.op("pe", lambda e: e.matmul(out_ap, lhsT=lhsT_ap, rhs=rhs_ap, start=start, stop=stop),
                     reads=[lhsT_T.b, rhs_T.b], writes=[outT.b])

    def v3(ap, s=4):
        return ap.rearrange("p (s t) -> p s t", s=s)

    def col3(ap_p_q, n):
        q = ap_p_q.shape[1]
        return ap_p_q.rearrange("p (q o) -> p q o", o=1).broadcast_to([128, q, n])

    P = [T(top.enter_context(nc.psum_tensor("P%d" % i, [128, 1024], F32)), "P%d" % i) for i in range(4)]
    xT = sbt(top, "xT", [128, 8, 1024], nsub=8)
    hnT = sbt(top, "hnT", [128, 8, 1024], BF16, nsub=8)
    merged = sbt(top, "merged", [128, 8, 1024], nsub=8)
    WB = [sbt(top, "wb%d" % i, [128, 8, 512], BF16) for i in range(2)]
    wb_i = [0]
    ident = sbt(top, "ident", [128, 128])
    cm = sbt(top, "cm", [128, 3, 128])
    onesb = sbt(top, "onesb", [128, 128], BF16)
    maskB = sbt(top, "maskB", [128, 8])
    flag = sbt(top, "flag", [128, 1])
    modt = sbt(top, "modt", [128, 48])
    gmod = sbt(top, "gmod", [128, 16])
    rstd = sbt(top, "rstd", [128, 1024])
    SCA = [sbt(top, "sca%d" % i, [128, 1024]) for i in range(3)]
    SCB = [sbt(top, "scb%d" % i, [128, 1024], BF16) for i in range(2)]
    sci = [0, 0]

    def scA():
        sci[0] += 1
        return SCA[sci[0] % 3]

    def scB():
        sci[1] += 1
        return SCB[sci[1] % 2]

    for kt in range(8):
        DMA("sp", xT.t[:, kt, :], D["xT"][:, kt, :], W=[xT.k[kt]])
    DMA("sp", ident.t[:], D["ident"], W=[ident])
    DMA("sp", cm.t[:], D["cm"], W=[cm])
    DMA("sp", maskB.t[:], D["maskB"], W=[maskB])
    DMA("sp", flag.t[:], D["flag"], W=[flag])
    OP("pool", "memset", onesb.t[:], 1.0, W=[onesb])

    def wb_next():
        buf = WB[wb_i[0] % 2]
        wb_i[0] += 1
        return buf

    def load_into(buf, coff, dram_ap, ktiles, ncols):
        DMA("pool", buf.t[:, 0:ktiles, coff:coff + ncols], dram_ap.rearrange("(kt p) n -> p kt n", p=128), W=[buf])

    def load_w(dram_ap, ktiles, ncols):
        buf = wb_next()
        load_into(buf, 0, dram_ap, ktiles, ncols)
        return buf

    def project(wbuf, ktiles, c0, rhs_fn, pT):
        for half in range(2):
            for kt in range(ktiles):
                rT, rap = rhs_fn(kt, half)
                MM(pT, pT.t[:, half * 512:(half + 1) * 512], wbuf, wbuf.t[:, kt, c0:c0 + 128],
                   rT, rap, start=(kt == 0), stop=(kt == ktiles - 1))

    def hn_rhs(kt, half):
        return hnT.k[kt], hnT.t[:, kt, half * 512:(half + 1) * 512]

    def k_rhs(tt):
        return lambda kt, half: (tt.k[kt], tt.t[:, kt, half * 512:(half + 1) * 512])

    with ExitStack() as s0:
        cond = sbt(s0, "cond", [128, 8])
        scond = sbt(s0, "scond", [128, 8])
        bada = sbt(s0, "bada", [128, 48])
        ng = sbt(s0, "ng", [128, 16])
        wst = [sbt(s0, "wst%d" % i, [128, 8, 512]) for i in range(2)]
        DMA("sp", cond.t[:], D["cond"], W=[cond])
        for l in range(2):
            DMA("sp", bada.t[:, l * 24:(l + 1) * 24], D["b_ada"][l], W=[bada])
            DMA("sp", ng.t[:, l * 8:(l + 1) * 8], D["norm_g"][l], W=[ng])
        OP("act", "activation", scond.t[:], cond.t[:], AF.Silu, R=[cond], W=[scond])
        i = 0
        for l in range(2):
            wv = D["w_ada"][l].rearrange("(kt p) n -> p kt n", p=128)
            for nb in range(6):
                w = wst[i % 2]
                i += 1
                DMA("sp" if i % 2 == 1 else "act", w.t[:], wv[:, :, nb * 512:(nb + 1) * 512], W=[w])
                for m in range(4):
                    j = l * 24 + nb * 4 + m
                    for kt in range(8):
                        MM(P[0], P[0].t[:, j:j + 1], w, w.t[:, kt, m * 128:(m + 1) * 128],
                           scond, scond.t[:, kt:kt + 1], start=(kt == 0), stop=(kt == 7))
        OP("dve", "tensor_tensor", modt.t[:], P[0].t[:, 0:48], bada.t[:], ALU.add, R=[P[0], bada], W=[modt])
        for l in range(2):
            OP("dve", "scalar_tensor_tensor", gmod.t[:, l * 8:(l + 1) * 8], modt.t[:, l * 24 + 8:l * 24 + 16], 1.0,
               ng.t[:, l * 8:(l + 1) * 8], ALU.add, ALU.mult, R=[modt, ng], W=[gmod])
        tap("modt", modt.t[:], [128, 48], [modt])
        fw.barrier()

    def rms_stats():
        for kt in range(8):
            sb_ = scB()
            OP("act", "activation", sb_.t[:], xT.t[:, kt, :], AF.Square, R=[xT.k[kt]], W=[sb_])
            for half in range(2):
                MM(P[0], P[0].t[:, half * 512:(half + 1) * 512], onesb, onesb.t[:], sb_,
                   sb_.t[:, half * 512:(half + 1) * 512], start=(kt == 0), stop=(kt == 7))
        OP("dve", "tensor_scalar", rstd.t[:], P[0].t[:], 1.0 / DM, EPS, ALU.mult, ALU.add, R=[P[0]], W=[rstd])
        OP("dve", "reciprocal", rstd.t[:], rstd.t[:], R=[rstd], W=[rstd])
        OP("act", "activation", rstd.t[:], rstd.t[:], AF.Sqrt, R=[rstd], W=[rstd])

    def norm_modulate(l):
        rms_stats()
        for kt in range(8):
            sa = scA()
            OP("dve", "scalar_tensor_tensor", sa.t[:], xT.t[:, kt, :], gmod.t[:, l * 8 + kt:l * 8 + kt + 1],
               rstd.t[:], ALU.mult, ALU.mult, R=[xT.k[kt], gmod, rstd], W=[sa])
            OP("act", "activation", hnT.t[:, kt, :], sa.t[:], AF.Identity,
               bias=modt.t[:, l * 24 + kt:l * 24 + kt + 1], R=[sa, modt], W=[hnT.k[kt]])

    def merge_branch(l, n, outTs):
        for blk in range(2):
            wbr = load_w(D["w_branch"][l, n][:, blk * 512:(blk + 1) * 512], 4, 512)
            g0 = OFF["gates"] + n * 1024 + blk * 512
            wg = load_w(D["w_in"][l][:, g0:g0 + 512], 8, 512)
            for m in range(4):
                dt_ = blk * 4 + m
                pr, gp = P[(dt_ % 2) * 2], P[(dt_ % 2) * 2 + 1]
                project(wbr, 4, m * 128, k_rhs(outTs), pr)
                project(wg, 8, m * 128, hn_rhs, gp)
                sa = scA()
                OP("act", "activation", sa.t[:], gp.t[:], AF.Sigmoid, R=[gp], W=[sa])
                if n == 0:
                    OP("dve", "tensor_tensor", merged.t[:, dt_, :], sa.t[:], pr.t[:], ALU.mult,
                       R=[sa, pr], W=[merged.k[dt_]])
                else:
                    OP("dve", "tensor_tensor", sa.t[:], sa.t[:], pr.t[:], ALU.mult, R=[sa, pr], W=[sa])
                    OP("dve", "tensor_tensor", merged.t[:, dt_, :], merged.t[:, dt_, :], sa.t[:], ALU.add,
                       R=[sa, merged.k[dt_]], W=[merged.k[dt_]])

    def out_proj_residual(l):
        for kt in range(8):
            OP("act", "activation", hnT.t[:, kt, :], merged.t[:, kt, :], AF.Copy, R=[merged.k[kt]], W=[hnT.k[kt]])
        for blk in range(2):
            wo = load_w(D["w_out"][l][:, blk * 512:(blk + 1) * 512], 8, 512)
            for m in range(4):
                dt_ = blk * 4 + m
                pp = P[2 + m % 2]
                project(wo, 8, m * 128, hn_rhs, pp)
                OP("dve", "scalar_tensor_tensor", xT.t[:, dt_, :], pp.t[:],
                   modt.t[:, l * 24 + 16 + dt_:l * 24 + 17 + dt_],
                   xT.t[:, dt_, :], ALU.mult, ALU.add, R=[pp, modt, xT.k[dt_]], W=[xT.k[dt_]])

    def TT(eng, oT, o, aT, a, bT, b, op):
        OP(eng, "tensor_tensor", o, a, b, op, R=[aT, bT], W=[oT])

    def TS(eng, oT, o, aT, a, s1, s2, op0, op1=None, RS=()):
        if op1 is None:
            OP(eng, "tensor_scalar", o, a, s1, None, op0, R=[aT] + list(RS), W=[oT])
        else:
            OP(eng, "tensor_scalar", o, a, s1, s2, op0, op1, R=[aT] + list(RS), W=[oT])

    def STT(eng, oT, o, aT, a, sc, bT, b, op0, op1, RS=()):
        OP(eng, "scalar_tensor_tensor", o, a, sc, b, op0, op1, R=[aT, bT] + list(RS), W=[oT])

    def ACT(oT, o, aT, a, func, RS=(), **kw):
        OP("act", "activation", o, a, func, R=[aT] + list(RS), W=[oT], **kw)

    def cos_sin(stk, eng, th, n, tag):
        c = [sbt(stk, "cs_c%d%s" % (i, tag), [128, n]) for i in range(2)]
        s = [sbt(stk, "cs_s%d%s" % (i, tag), [128, n]) for i in range(2)]
        t = sbt(stk, "cs_t" + tag, [128, n])
        hp = sbt(stk, "cs_hp" + tag, [128, 1])
        OP(eng, "memset", hp.t[:], float(np.pi / 2), W=[hp])
        ACT(s[0], s[0].t[:], th, th.t[:], AF.Sin, scale=1.0 / 16)
        ACT(c[0], c[0].t[:], th, th.t[:], AF.Sin, RS=[hp], scale=1.0 / 16, bias=hp.t[:])
        for it in range(4):
            a, b = it % 2, (it + 1) % 2
            TT(eng, t, t.t[:], s[a], s[a].t[:], s[a], s[a].t[:], ALU.mult)
            STT(eng, s[b], s[b].t[:], s[a], s[a].t[:], 2.0, c[a], c[a].t[:], ALU.mult, ALU.mult)
            TS(eng, c[b], c[b].t[:], t, t.t[:], -2.0, 1.0, ALU.mult, ALU.add)
        return c[0], s[0]

    def branch_a(l):
        with ExitStack() as sa_:
            uT = sbt(sa_, "uT", [128, 4, 1024], BF16, nsub=4)
            ya = sbt(sa_, "ya", [128, 4, 1024], BF16, nsub=4)
            bbc = sbt(sa_, "bbc", [128, 2, 512])
            rho = sbt(sa_, "rho", [128, 32])
            g0r = sbt(sa_, "g0r", [128, 32])
            g0i = sbt(sa_, "g0i", [128, 32])
            cs = sbt(sa_, "cs", [128, 2, 512])
            ns5 = sbt(sa_, "ns5", [128, 2, 128])
            s5d = sbt(sa_, "s5d", [128, 4])
            cth0 = None
            DMA("sp", s5d.t[:], D["s5d"][l], W=[s5d])
            DMA("sp", cs.t[:, 0, :], D["cre_s"][l].rearrange("p d t i -> p (d t i)"), W=[cs])
            DMA("sp", cs.t[:, 1, :], D["cim_s"][l].rearrange("p d t i -> p (d t i)"), W=[cs])
            OP("pool", "tensor_scalar", cs.t[:, 1, :], cs.t[:, 1, :], -1.0, None, ALU.mult, R=[cs], W=[cs])

            wu = load_w(D["w_in"][l][:, OFF["u_a"]:OFF["u_a"] + 512], 8, 512)
            for m in range(4):
                pp = P[m % 2]
                project(wu, 8, m * 128, hn_rhs, pp)
                ACT(uT.k[m], uT.t[:, m, :], pp, pp.t[:], AF.Copy)

            cthp = sbt(sa_, "cthp", [128, 32]); sthp = sbt(sa_, "sthp", [128, 32])
            sd = ExitStack()
            lr = sbt(sd, "lr_s", [128, 32]); li = sbt(sd, "li_s", [128, 32]); ls = sbt(sd, "ls_s", [128, 32])
            h0r = sbt(sd, "h0r", [128, 32]); h0i = sbt(sd, "h0i", [128, 32])
            th = sbt(sd, "th_s", [128, 32]); tmp = sbt(sd, "tmp_s", [128, 32])
            for tns, nm in ((lr, "lamre_s"), (li, "lamim_s"), (ls, "lstep_s")):
                DMA("sp", tns.t[:], D[nm][l].rearrange("p d t -> p (d t)"), W=[tns])
            DMA("sp", h0r.t[:], D["s5h0r"][:, l].rearrange("p d t -> p (d t)"), W=[h0r])
            DMA("sp", h0i.t[:], D["s5h0i"][:, l].rearrange("p d t -> p (d t)"), W=[h0i])
            ACT(ls, ls.t[:], ls, ls.t[:], AF.Exp)
            TT("dve", tmp, tmp.t[:], lr, lr.t[:], ls, ls.t[:], ALU.mult)
            ACT(rho, rho.t[:], tmp, tmp.t[:], AF.Exp)
            TT("dve", th, th.t[:], li, li.t[:], ls, ls.t[:], ALU.mult)
            tap("li%d" % l, li.t[:], [128, 32], [li])
            tap("th%d" % l, th.t[:], [128, 32], [th])
            c_, s_ = cos_sin(sd, "dve", th, 32, "s")
            OP("dve", "tensor_copy", cthp.t[:], c_.t[:], R=[c_], W=[cthp])
            OP("dve", "tensor_copy", sthp.t[:], s_.t[:], R=[s_], W=[sthp])
            TT("dve", tmp, tmp.t[:], s_, s_.t[:], h0i, h0i.t[:], ALU.mult)
            TT("dve", g0r, g0r.t[:], c_, c_.t[:], h0r, h0r.t[:], ALU.mult)
            TT("dve", g0r, g0r.t[:], g0r, g0r.t[:], tmp, tmp.t[:], ALU.subtract)
            TT("dve", tmp, tmp.t[:], s_, s_.t[:], h0r, h0r.t[:], ALU.mult)
            TT("dve", g0i, g0i.t[:], c_, c_.t[:], h0i, h0i.t[:], ALU.mult)
            TT("dve", g0i, g0i.t[:], g0i, g0i.t[:], tmp, tmp.t[:], ALU.add)

            names = ["lrc", "lic", "lsc", "brc", "bic", "thc", "mag", "ar1", "ai", "den", "t1", "t2", "fr", "fi"]
            A = {n_: sbt(sd, n_ + "_c", [128, 512]) for n_ in names}
            for tns, nm in ((A["lrc"], "lamre_c"), (A["lic"], "lamim_c"), (A["lsc"], "lstep_c"),
                            (A["brc"], "bre_c"), (A["bic"], "bim_c")):
                DMA("sp", tns.t[:], D[nm][l].rearrange("p d c q -> p (d c q)"), W=[tns])

            def e2(o, a, b, op, eng="dve"):
                TT(eng, A[o], A[o].t[:], A[a], A[a].t[:], A[b], A[b].t[:], op)
            ACT(A["lsc"], A["lsc"].t[:], A["lsc"], A["lsc"].t[:], AF.Exp)
            e2("t1", "lrc", "lsc", ALU.mult)
            ACT(A["mag"], A["mag"].t[:], A["t1"], A["t1"].t[:], AF.Exp)
            e2("thc", "lic", "lsc", ALU.mult)
            cc, ss = cos_sin(sd, "dve", A["thc"], 512, "c")
            TT("dve", A["ai"], A["ai"].t[:], A["mag"], A["mag"].t[:], ss, ss.t[:], ALU.mult)
            TT("dve", A["ar1"], A["ar1"].t[:], A["mag"], A["mag"].t[:], cc, cc.t[:], ALU.mult)
            TS("dve", A["ar1"], A["ar1"].t[:], A["ar1"], A["ar1"].t[:], -1.0, None, ALU.add)
            e2("den", "lrc", "lrc", ALU.mult)
            e2("t1", "lic", "lic", ALU.mult)
            e2("den", "den", "t1", ALU.add)
            OP("dve", "reciprocal", A["den"].t[:], A["den"].t[:], R=[A["den"]], W=[A["den"]])
            e2("t1", "ar1", "lrc", ALU.mult)
            e2("t2", "ai", "lic", ALU.mult)
            e2("fr", "t1", "t2", ALU.add)
            e2("fr", "fr", "den", ALU.mult)
            e2("t1", "ai", "lrc", ALU.mult)
            e2("t2", "ar1", "lic", ALU.mult)
            e2("fi", "t1", "t2", ALU.subtract)
            e2("fi", "fi", "den", ALU.mult)
            e2("t1", "fr", "brc", ALU.mult)
            e2("t2", "fi", "bic", ALU.mult)
            TT("dve", bbc, bbc.t[:, 0, :], A["t1"], A["t1"].t[:], A["t2"], A["t2"].t[:], ALU.subtract)
            e2("t1", "fr", "bic", ALU.mult)
            e2("t2", "fi", "brc", ALU.mult)
            TT("dve", bbc, bbc.t[:, 1, :], A["t1"], A["t1"].t[:], A["t2"], A["t2"].t[:], ALU.add)
            fw.barrier()
            sd.close()
            tap("bbc%d" % l, bbc.t[:], [128, 2, 512], [bbc])
            tap("rho%d" % l, rho.t[:], [128, 32], [rho])
            tap("cth%d" % l, cthp.t[:], [128, 32], [cthp])
            tap("sth%d" % l, sthp.t[:], [128, 32], [sthp])

            Tr32 = sbt(sa_, "Tr32", [128, 8, 256]); Ti32 = sbt(sa_, "Ti32", [128, 8, 256])
            SETS = []
            for i_ in range(2):
                SETS.append(dict(
                    Bbf=sbt(sa_, "Bbf", [128, 2, 4, 2, 128], BF16), CTp=sbt(sa_, "CTp", [128, 2, 4, 2, 128], BF16),
                    Tr=sbt(sa_, "Trb", [128, 8, 256], BF16 if S5_BF16 else F32),
                    Ti=sbt(sa_, "Tib", [128, 8, 256], BF16 if S5_BF16 else F32),
                    Cr=sbt(sa_, "Cr", [128, 9, 8]), Ci=sbt(sa_, "Ci", [128, 9, 8]), ctmp=sbt(sa_, "ctmp", [128, 3, 8]),
                    Kr=sbt(sa_, "Kr", [128, 8]), Ki=sbt(sa_, "Ki", [128, 8]), nKi=sbt(sa_, "nKi", [128, 8]),
                    c255=sbt(sa_, "c255", [128, 8]), s255=sbt(sa_, "s255", [128, 8]), ns255=sbt(sa_, "ns255", [128, 8])))
            HB = [sbt(sa_, "hb%d" % i, [128, 1024], BF16) for i in range(4)]
            carry = sbt(sa_, "carry", [128, 2, 2, 4])
            bbc4 = bbc.t[:].rearrange("p r (d c q) -> p r d c q", d=2, c=4)
            cs4 = cs.t[:].rearrange("p r (d t i) -> p r d t i", d=2, t=16)
            cth3 = cthp.t[:].rearrange("p (d t) -> p d t", d=2)
            sth3 = sthp.t[:].rearrange("p (d t) -> p d t", d=2)
            rho3 = rho.t[:].rearrange("p (d t) -> p d t", d=2)
            ns54 = ns5.t[:].rearrange("p r (d t s) -> p r d t s", d=2, t=16)
            TE = "pool"
            pipe_ = [0]

            def bfv(i):
                if not S5_BF16:
                    return merged.t[:, i, :]
                return merged.t[:, i, :].bitcast(BF16)[:, 0:1024]

            def unpack(ct):
                S_ = SETS[ct % 2]
                return tuple(S_[n_] for n_ in ("Bbf", "CTp", "Tr", "Ti", "Cr", "Ci", "ctmp", "Kr", "Ki", "nKi",
                                               "c255", "s255", "ns255"))

            def build_ct(ct):
                Bbf, CTp, Trb_, Tib_, Cr, Ci, ctmp, Kr, Ki, nKi, c255, s255, ns255 = unpack(ct)
                Tr, Ti = Tr32, Ti32
                for st_ in range(4):
                    for gl in range(2):
                        for ri in range(2):
                            OP(TE, "tensor_scalar", Bbf.t[:, :, st_, ri, gl * 64:(gl + 1) * 64], bbc4[:, ri, :, ct, :],
                               maskB.t[:, st_ * 2 + gl:st_ * 2 + gl + 1], None, ALU.mult, R=[bbc, maskB], W=[Bbf])
                OP(TE, "memset", CTp.t[:], 0.0, W=[CTp])
                for st_ in range(4):
                    for gl in range(2):
                        for ri in range(2):
                            OP(TE, "tensor_copy",
                               CTp.t[gl * 64:(gl + 1) * 64, :, st_, ri, 32 * st_ + gl * 16:32 * st_ + gl * 16 + 16],
                               cs4[gl * 64:(gl + 1) * 64, ri, :, ct * 4 + st_, :], R=[cs], W=[CTp])
                OP(TE, "tensor_copy", Cr.t[:, 0, :].rearrange("p (d s) -> p d s", d=2), cth3[:, :, ct * 4:(ct + 1) * 4],
                   R=[cthp], W=[Cr])
                OP(TE, "tensor_copy", Ci.t[:, 0, :].rearrange("p (d s) -> p d s", d=2), sth3[:, :, ct * 4:(ct + 1) * 4],
                   R=[sthp], W=[Ci])
                OP(TE, "memset", Tr.t[:, :, 0:1], 1.0, W=[Tr])
                OP(TE, "memset", Ti.t[:, :, 0:1], 0.0, W=[Ti])
                tA, tB = SCA[0], SCA[1]
                for k in range(8):
                    n = 1 << k
                    crb = col3(Cr.t[:, k, :], n)
                    cib = col3(Ci.t[:, k, :], n)
                    a3 = tA.t[:, 0:8 * n].rearrange("p (q n) -> p q n", q=8)
                    b3 = tB.t[:, 0:8 * n].rearrange("p (q n) -> p q n", q=8)
                    OP(TE, "tensor_tensor", a3, Tr.t[:, :, 0:n], crb, ALU.mult, R=[Tr, Cr], W=[tA])
                    OP(TE, "tensor_tensor", b3, Ti.t[:, :, 0:n], cib, ALU.mult, R=[Ti, Ci], W=[tB])
                    OP(TE, "tensor_tensor", Tr.t[:, :, n:2 * n], a3, b3, ALU.subtract, R=[tA, tB], W=[Tr])
                    OP(TE, "tensor_tensor", a3, Tr.t[:, :, 0:n], cib, ALU.mult, R=[Tr, Ci], W=[tA])
                    OP(TE, "tensor_tensor", b3, Ti.t[:, :, 0:n], crb, ALU.mult, R=[Ti, Cr], W=[tB])
                    OP(TE, "tensor_tensor", Ti.t[:, :, n:2 * n], a3, b3, ALU.add, R=[tA, tB], W=[Ti])
                    OP(TE, "tensor_tensor", ctmp.t[:, 0, :], Cr.t[:, k, :], Cr.t[:, k, :], ALU.mult, R=[Cr], W=[ctmp])
                    OP(TE, "tensor_tensor", ctmp.t[:, 1, :], Ci.t[:, k, :], Ci.t[:, k, :], ALU.mult, R=[Ci], W=[ctmp])
                    OP(TE, "tensor_tensor", Cr.t[:, k + 1, :], ctmp.t[:, 0, :], ctmp.t[:, 1, :], ALU.subtract,
                       R=[ctmp], W=[Cr])
                    OP(TE, "tensor_tensor", ctmp.t[:, 2, :], Cr.t[:, k, :], Ci.t[:, k, :], ALU.mult, R=[Cr, Ci], W=[ctmp])
                    OP(TE, "tensor_scalar", Ci.t[:, k + 1, :], ctmp.t[:, 2, :], 2.0, None, ALU.mult, R=[ctmp], W=[Ci])
                OP(TE, "tensor_scalar", Kr.t[:], Cr.t[:, 8, :], flag.t[:, 0:1], None, ALU.mult, R=[Cr, flag], W=[Kr])
                OP(TE, "tensor_scalar", Ki.t[:], Ci.t[:, 8, :], flag.t[:, 0:1], None, ALU.mult, R=[Ci, flag], W=[Ki])
                OP(TE, "tensor_scalar", nKi.t[:], Ki.t[:], -1.0, None, ALU.mult, R=[Ki], W=[nKi])
                OP(TE, "tensor_copy", c255.t[:], Tr.t[:, :, 255], R=[Tr], W=[c255])
                OP(TE, "tensor_copy", s255.t[:], Ti.t[:, :, 255], R=[Ti], W=[s255])
                OP(TE, "tensor_scalar", ns255.t[:], s255.t[:], -1.0, None, ALU.mult, R=[s255], W=[ns255])
                OP(TE, "tensor_copy", Trb_.t[:], Tr.t[:], R=[Tr], W=[Trb_])
                OP(TE, "tensor_copy", Tib_.t[:], Ti.t[:], R=[Ti], W=[Tib_])
                if l == 0 and ct == 0:
                    tap("Tr", Tr.t[:], [128, 8, 256], [Tr])
                    tap("Ti", Ti.t[:], [128, 8, 256], [Ti])

            def scan_ct(ct):
                Bbf, CTp, Tr, Ti, Cr, Ci, ctmp, Kr, Ki, nKi, c255, s255, ns255 = unpack(ct)
                pipe = pipe_[0]
                first = True

                def emit_bu(dr, st_, pp_):
                    for ri in range(2):
                        for half in range(2):
                            MM(P[ri], P[ri].t[:, half * 512:(half + 1) * 512], Bbf, Bbf.t[:, dr, st_, ri, :],
                               uT.k[ct], uT.t[:, ct, half * 512:(half + 1) * 512])
                    for ri in range(2):
                        src = v3(P[ri].t[:])
                        if dr == 1:
                            src = src[:, :, ::-1]
                        ACT(merged.k[pp_ * 4 + ri], v3(bfv(pp_ * 4 + ri)), P[ri], src, AF.Copy)

                pairs = [(dr, st_) for dr in range(2) for st_ in range(4)]
                emit_bu(pairs[0][0], pairs[0][1], pipe)
                for ip_, (dr, st_) in enumerate(pairs):
                    if True:
                        q = dr * 4 + st_
                        tp = ct * 4 + st_
                        E = "dve"
                        wk = [merged.k[pipe * 4 + i] for i in range(4)]
                        wa = [bfv(pipe * 4 + i) for i in range(4)]
                        hr, hi = HB[pipe * 2], HB[pipe * 2 + 1]
                        pipe = (pipe + 1) % 2
                        BUr, BUi, TA, TB_ = wk
                        bur, bui, ta, tb = wa
                        if ip_ + 1 < len(pairs):
                            emit_bu(pairs[ip_ + 1][0], pairs[ip_ + 1][1], pipe)
                        cosb = Tr.t[:, q:q + 1, :].broadcast_to([128, 4, 256])
                        sinb = Ti.t[:, q:q + 1, :].broadcast_to([128, 4, 256])
                        OP(E, "tensor_tensor", v3(ta), v3(bur), cosb, ALU.mult, R=[BUr, Tr], W=[TA])
                        OP(E, "tensor_tensor", v3(tb), v3(bui), sinb, ALU.mult, R=[BUi, Ti], W=[TB_])
                        OP(E, "tensor_tensor", ta, ta, tb, ALU.add, R=[TA, TB_], W=[TA])
                        OP(E, "tensor_tensor", v3(tb), v3(bui), cosb, ALU.mult, R=[BUi, Tr], W=[TB_])
                        OP(E, "tensor_tensor", v3(bur), v3(bur), sinb, ALU.mult, R=[BUr, Ti], W=[BUr])
                        OP(E, "tensor_tensor", tb, tb, bur, ALU.subtract, R=[TB_, BUr], W=[TB_])
                        order = [0, 1, 2, 3] if dr == 0 else [3, 2, 1, 0]
                        rcol = rho3[:, dr, tp:tp + 1]
                        rb = rcol.broadcast_to([128, 256])
                        for idx, sg in enumerate(order):
                            sl = slice(sg * 256, (sg + 1) * 256)
                            if idx == 0:
                                ir = g0r.t[:, dr * 16 + tp:dr * 16 + tp + 1]
                                ii = g0i.t[:, dr * 16 + tp:dr * 16 + tp + 1]
                                RI = [g0r, g0i]
                            else:
                                ir = carry.t[:, pipe, 0, idx:idx + 1]
                                ii = carry.t[:, pipe, 1, idx:idx + 1]
                                RI = [carry]
                            OP(E, "tensor_tensor_scan", bur[:, sl], rb, ta[:, sl], ir, ALU.mult, ALU.add,
                               R=[rho, TA] + RI, W=[BUr])
                            OP(E, "tensor_tensor_scan", bui[:, sl], rb, tb[:, sl], ii, ALU.mult, ALU.add,
                               R=[rho, TB_] + RI, W=[BUi])
                            if idx < 3:
                                last = sg * 256 + 255
                                glr = bur[:, last:last + 1]
                                gli = bui[:, last:last + 1]
                                nr = carry.t[:, pipe, 0, idx + 1:idx + 2]
                                ni = carry.t[:, pipe, 1, idx + 1:idx + 2]
                                OP(E, "tensor_scalar", nr, glr, Kr.t[:, q:q + 1], None, ALU.mult, R=[BUr, Kr], W=[carry])
                                OP(E, "scalar_tensor_tensor", nr, gli, nKi.t[:, q:q + 1], nr, ALU.mult, ALU.add,
                                   R=[BUi, nKi, carry], W=[carry])
                                OP(E, "tensor_scalar", ni, gli, Kr.t[:, q:q + 1], None, ALU.mult, R=[BUi, Kr], W=[carry])
                                OP(E, "scalar_tensor_tensor", ni, glr, Ki.t[:, q:q + 1], ni, ALU.mult, ALU.add,
                                   R=[BUr, Ki, carry], W=[carry])
                        g255r = v3(bur)[:, :, 255]
                        g255i = v3(bui)[:, :, 255]
                        fr_ = ns54[:, 0, dr, tp, :]
                        fi_ = ns54[:, 1, dr, tp, :]
                        OP(E, "tensor_scalar", fr_, g255r, c255.t[:, q:q + 1], None, ALU.mult, R=[BUr, c255], W=[ns5])
                        OP(E, "scalar_tensor_tensor", fr_, g255i, ns255.t[:, q:q + 1], fr_, ALU.mult, ALU.add,
                           R=[BUi, ns255, ns5], W=[ns5])
                        OP(E, "tensor_scalar", fi_, g255i, c255.t[:, q:q + 1], None, ALU.mult, R=[BUi, c255], W=[ns5])
                        OP(E, "scalar_tensor_tensor", fi_, g255r, s255.t[:, q:q + 1], fi_, ALU.mult, ALU.add,
                           R=[BUr, s255, ns5], W=[ns5])
                        ohr = v3(hr.t[:]) if dr == 0 else v3(hr.t[:])[:, :, ::-1]
                        ohi = v3(hi.t[:]) if dr == 0 else v3(hi.t[:])[:, :, ::-1]
                        OP(E, "tensor_tensor", v3(ta), v3(bur), cosb, ALU.mult, R=[BUr, Tr], W=[TA])
                        OP(E, "tensor_tensor", v3(tb), v3(bui), sinb, ALU.mult, R=[BUi, Ti], W=[TB_])
                        OP(E, "tensor_tensor", ohr, v3(ta), v3(tb), ALU.subtract, R=[TA, TB_], W=[hr])
                        OP(E, "tensor_tensor", v3(ta), v3(bur), sinb, ALU.mult, R=[BUr, Ti], W=[TA])
                        OP(E, "tensor_tensor", v3(tb), v3(bui), cosb, ALU.mult, R=[BUi, Tr], W=[TB_])
                        OP(E, "tensor_tensor", ohi, v3(ta), v3(tb), ALU.add, R=[TA, TB_], W=[hi])
                        if l == 0 and ct == 0 and st_ == 0:
                            tap("hr%d" % dr, hr.t[:], [128, 1024], [hr])
                        for ri, hh in ((0, hr), (1, hi)):
                            lastmm = (dr == 1 and st_ == 3 and ri == 1)
                            for half in range(2):
                                MM(P[3], P[3].t[:, half * 512:(half + 1) * 512], CTp, CTp.t[:, dr, st_, ri, :],
                                   hh, hh.t[:, half * 512:(half + 1) * 512], start=(first and ri == 0), stop=lastmm)
                        first = False
                sa = scA()
                STT("dve", sa, sa.t[:], uT.k[ct], uT.t[:, ct, :], s5d.t[:, ct:ct + 1], P[3], P[3].t[:],
                    ALU.mult, ALU.add, RS=[s5d])
                if ct == 0:
                    tap("ypre%d" % l, sa.t[:], [128, 1024], [sa])
                ACT(ya.k[ct], ya.t[:, ct, :], sa, sa.t[:], AF.Gelu)
                pipe_[0] = pipe

            build_ct(0)
            for ct in range(4):
                if ct + 1 < 4:
                    build_ct(ct + 1)
                scan_ct(ct)
            DMA("sp", O["ns5r"][:, l].rearrange("p d t s -> p (d t s)"), ns5.t[:, 0, :], R=[ns5])
            DMA("sp", O["ns5i"][:, l].rearrange("p d t s -> p (d t s)"), ns5.t[:, 1, :], R=[ns5])
            wgl = load_w(D["wglu"][l], 4, 512)
            wz = load_w(D["w_in"][l][:, OFF["z_a"]:OFF["z_a"] + 512], 8, 512)
            for m in range(4):
                pg_, pz_ = (P[0], P[1]) if m % 2 == 0 else (P[2], P[3])
                project(wgl, 4, m * 128, k_rhs(ya), pg_)
                project(wz, 8, m * 128, hn_rhs, pz_)
                g_ = scB()
                ACT(g_, g_.t[:], pg_, pg_.t[:], AF.Sigmoid)
                z_ = scB()
                ACT(z_, z_.t[:], pz_, pz_.t[:], AF.Silu)
                OP("dve", "tensor_tensor", g_.t[:], g_.t[:], ya.t[:, m, :], ALU.mult, R=[g_, ya.k[m]], W=[g_])
                OP("dve", "tensor_tensor", uT.t[:, m, :], g_.t[:], z_.t[:], ALU.mult, R=[g_, z_], W=[uT.k[m]])
            tap("outa%d" % l, uT.t[:, 0, :], [128, 1024], [uT.k[0]])
            merge_branch(l, 0, uT)
            fw.barrier()

    AX = mybir.AxisListType.X

    def branch_b(l):
        with ExitStack() as sb_:
            kT = sbt(sb_, "kTall", [128, 4, 1536], BF16, nsub=4)
            qT = sbt(sb_, "qTall", [128, 4, 1024], BF16, nsub=4)
            Vt = sbt(sb_, "Vt", [128, 12, 4, 129], BF16, nsub=12)
            szb = sbt(sb_, "szb", [128, 4, 1024], BF16, nsub=4)
            outb = sbt(sb_, "outb", [128, 4, 1024], BF16, nsub=4)
            ropec = sbt(sb_, "ropec", [128, 1024]); ropes = sbt(sb_, "ropes", [128, 1024])
            maskb = sbt(sb_, "maskb", [128, 48])
            dalam = sbt(sb_, "dalam", [128, 256]); gn = sbt(sb_, "gn", [128, 128])
            lamv = sbt(sb_, "lamv", [128, 4]); lt = sbt(sb_, "lt", [128, 128])
            EX = [sbt(sb_, "ex%d" % i, [128, 512], BF16) for i in range(3)]
            ACC = [sbt(sb_, "accS%d" % i, [128, 8, 160]) for i in range(2)]
            OB4 = [sbt(sb_, "ob4%d" % i, [128, 4, 128]) for i in range(2)]
            SS4 = [sbt(sb_, "ss4%d" % i, [128, 16]) for i in range(2)]
            lt4 = sbt(sb_, "lt4", [128, 4, 128])
            DMA("sp", ropec.t[:], D["ropec"], W=[ropec])
            DMA("sp", ropes.t[:], D["ropes"], W=[ropes])
            DMA("sp", maskb.t[:], D["maskb"], W=[maskb])
            DMA("sp", dalam.t[:], D["dalam"][l], W=[dalam])
            DMA("sp", gn.t[:], D["dang"][l], W=[gn])
            TT("dve", lt, lt.t[:, 0:64], dalam, dalam.t[:, 0:64], dalam, dalam.t[:, 64:128], ALU.mult)
            TT("dve", lt, lt.t[:, 64:128], dalam, dalam.t[:, 128:192], dalam, dalam.t[:, 192:256], ALU.mult)
            OP("dve", "tensor_reduce", lamv.t[:, 0:1], lt.t[:, 0:64], AX, ALU.add, R=[lt], W=[lamv])
            OP("dve", "tensor_reduce", lamv.t[:, 1:2], lt.t[:, 64:128], AX, ALU.add, R=[lt], W=[lamv])
            ACT(lamv, lamv.t[:, 0:2], lamv, lamv.t[:, 0:2], AF.Exp)
            TT("dve", lamv, lamv.t[:, 2:3], lamv, lamv.t[:, 1:2], lamv, lamv.t[:, 0:1], ALU.subtract)
            TS("dve", lamv, lamv.t[:, 2:3], lamv, lamv.t[:, 2:3], -LAM_INIT[l], None, ALU.add)
            OP("pool", "memset", Vt.t[:, :, :, 128:129], 1.0, W=Vt.k)
            lnc = sbt(sb_, "lnc", [128, 1])
            OP("pool", "memset", lnc.t[:], float(np.log(1.0 - LAM_INIT[l])), W=[lnc])

            for which, dst, c0 in (("q", qT, 0), ("k", kT, 512)):
                wa_ = load_w(D["w_in"][l][:, OFF["q_b"] + c0:OFF["q_b"] + c0 + 512], 8, 512)
                wp_ = load_w(D["w_qkp"][l][:, c0:c0 + 512], 8, 512)
                for h in range(4):
                    pa_, pb_ = (P[0], P[1]) if h % 2 == 0 else (P[2], P[3])
                    project(wa_, 8, h * 128, hn_rhs, pa_)
                    project(wp_, 8, h * 128, hn_rhs, pb_)
                    s1, s2 = scA(), scA()
                    TT("dve", s1, s1.t[:], pa_, pa_.t[:], ropec, ropec.t[:], ALU.mult)
                    TT("dve", s2, s2.t[:], pb_, pb_.t[:], ropes, ropes.t[:], ALU.mult)
                    if which == "q":
                        TT("dve", dst.k[h], dst.t[:, h, :], s1, s1.t[:], s2, s2.t[:], ALU.add)
                    else:
                        TT("dve", s1, s1.t[:], s1, s1.t[:], s2, s2.t[:], ALU.add)
                        DMA("sp", O["newk"][:, l, h, :], s1.t[:], R=[s1])
                        ACT(dst.k[h], dst.t[:, h, 0:1024], s1, s1.t[:], AF.Copy)
                        DMA("pool", dst.t[:, h, 1024:1536], D["kctx"][:, l, h, :], W=[dst.k[h]])
                        if h == 0:
                            tap("krot%d" % l, s1.t[:], [128, 1024], [s1])
            wv_ = load_w(D["w_in"][l][:, OFF["v_b"]:OFF["v_b"] + 512], 8, 512)
            wz_ = load_w(D["w_in"][l][:, OFF["z_b"]:OFF["z_b"] + 512], 8, 512)
            for h in range(4):
                project(wv_, 8, h * 128, hn_rhs, P[2])
                s1 = scA()
                ACT(s1, s1.t[:], P[2], P[2].t[:], AF.Copy)
                DMA("sp", O["newv"][:, l, h, :], s1.t[:], R=[s1])
                for tt in range(8):
                    OP("pe", "transpose", P[3].t[:, tt * 128:(tt + 1) * 128], s1.t[:, tt * 128:(tt + 1) * 128], ident.t[:],
                       R=[s1, ident], W=[P[3]])
                OP("act", "activation", Vt.t[:, 0:8, h, 0:128], P[3].t[:].rearrange("p (a e) -> p a e", a=8), AF.Copy,
                   R=[P[3]], W=Vt.k[0:8])
                DMA("pool", Vt.t[:, 8:12, h, 0:128], D["vctx"][:, l, :, h * 128:(h + 1) * 128], W=Vt.k[8:12])
                project(wz_, 8, h * 128, hn_rhs, P[h % 2])
                ACT(szb.k[h], szb.t[:, h, :], P[h % 2], P[h % 2].t[:], AF.Silu)
            PH = [TV(P[i // 2].t, "PH%d" % i) for i in range(4)]
            PT3 = TV(P[3].t, "PT3")
            for sub_, whole_ in [(PH[0], P[0]), (PH[1], P[0]), (PH[2], P[1]), (PH[3], P[1]), (PT3, P[3])]:
                sub_.b.writer = whole_.b.writer
                sub_.b.readers = dict(whole_.b.readers)
            blk_ = 0
            sci_ = 0
            exi = 0
            scale = 64 ** -0.5
            for h in range(4):
                for qb in range(2):
                    def acc(qt, m):
                        s = qt * 2 + m
                        bank, pos = s // 3, s % 3
                        pt = P[2] if bank < 2 else P[3]
                        off = (bank % 2) * 512 + pos * 160
                        return pt, pt.t[:, off:off + 129]
                    tiles = [(m, kt) for m in range(2) for kt in range(12)]

                    def emit_score(m, kt):
                        nonlocal sci_
                        ph = PH[sci_ % 4]
                        sc_ap = ph.t[:, (sci_ % 2) * 512:(sci_ % 2) * 512 + 512]
                        sci_ += 1
                        MM(ph, sc_ap, kT.k[h], kT.t[m * 64:(m + 1) * 64, h, kt * 128:(kt + 1) * 128],
                           qT.k[h], qT.t[m * 64:(m + 1) * 64, h, qb * 512:(qb + 1) * 512])
                        return ph, sc_ap

                    nxt = emit_score(*tiles[0])
                    for it_, (m, kt) in enumerate(tiles):
                        ph, sc_ap = nxt
                        if it_ + 1 < len(tiles):
                            nxt = emit_score(*tiles[it_ + 1])
                        ex = EX[exi % 3]
                        exi += 1
                        for sgl in range(2):
                            seg = qb * 2 + sgl
                            OP("act", "activation", ex.t[:, sgl * 256:(sgl + 1) * 256],
                               sc_ap[:, sgl * 256:(sgl + 1) * 256], AF.Exp,
                               bias=maskb.t[:, kt * 4 + seg:kt * 4 + seg + 1], scale=scale,
                               R=[ph, maskb], W=[ex])
                        for qt in range(4):
                            pt, aap = acc(qt, m)
                            first_in_bank = (qt * 2 + m) in (0, 4, 6, 1, 3, 7)
                            fw.op("pe", lambda e, aap=aap, ex=ex, qt=qt, kt=kt, fb=first_in_bank: e.matmul(
                                aap, lhsT=ex.t[:, qt * 128:(qt + 1) * 128], rhs=Vt.t[:, kt, h, :],
                                start=(kt == 0 and fb), stop=(kt == 11), skip_group_check=True),
                                reads=[ex.b, Vt.k[kt].b], writes=[pt.b])
                    accS = ACC[blk_ % 2]
                    o4 = OB4[blk_ % 2]
                    ss = SS4[blk_ % 2]
                    blk_ += 1
                    OP("dve", "tensor_copy", accS.t[:, 0:3, :], P[2].t[:, 0:480].rearrange("p (a e) -> p a e", a=3),
                       R=[P[2]], W=[accS])
                    OP("dve", "tensor_copy", accS.t[:, 3:6, :], P[2].t[:, 512:992].rearrange("p (a e) -> p a e", a=3),
                       R=[P[2]], W=[accS])
                    OP("dve", "tensor_copy", accS.t[:, 6:8, :], P[3].t[:, 0:320].rearrange("p (a e) -> p a e", a=2),
                       R=[P[3]], W=[accS])
                    OP("dve", "reciprocal", ss.t[:, 0:8], accS.t[:, :, 128], R=[accS], W=[ss])
                    TS("dve", ss, ss.t[:, 1:8:2], ss, ss.t[:, 1:8:2], lamv.t[:, 2:3], None, ALU.mult, RS=[lamv])
                    TT("dve", o4, o4.t[:], accS, accS.t[:, 0:8:2, 0:128], ss, col3(ss.t[:, 0:8:2], 128), ALU.mult)
                    TT("dve", lt4, lt4.t[:], accS, accS.t[:, 1:8:2, 0:128], ss, col3(ss.t[:, 1:8:2], 128), ALU.mult)
                    TT("dve", o4, o4.t[:], o4, o4.t[:], lt4, lt4.t[:], ALU.add)
                    TT("dve", lt4, lt4.t[:], o4, o4.t[:], o4, o4.t[:], ALU.mult)
                    OP("dve", "tensor_reduce", ss.t[:, 8:12], lt4.t[:], AX, ALU.add, R=[lt4], W=[ss])
                    TS("dve", ss, ss.t[:, 8:12], ss, ss.t[:, 8:12], 1.0 / 128, EPS, ALU.mult, ALU.add)
                    ACT(ss, ss.t[:, 8:12], ss, ss.t[:, 8:12], AF.Ln)
                    ACT(ss, ss.t[:, 12:16], ss, ss.t[:, 8:12], AF.Exp, RS=[lnc], scale=-0.5, bias=lnc.t[:, 0:1])
                    TT("dve", o4, o4.t[:], o4, o4.t[:], ss, col3(ss.t[:, 12:16], 128), ALU.mult)
                    TT("dve", o4, o4.t[:], o4, o4.t[:], gn,
                       gn.t[:].rearrange("p (o e) -> p o e", o=1).broadcast_to([128, 4, 128]), ALU.mult)
                    for qt in range(4):
                        OP("pe", "transpose", P[3].t[:, 512 + qt * 128:512 + (qt + 1) * 128], o4.t[:, qt, :], ident.t[:],
                           R=[o4, ident], W=[PT3])
                    TT("dve", outb.k[h], outb.t[:, h, qb * 512:(qb + 1) * 512], PT3, P[3].t[:, 512:1024],
                       szb.k[h], szb.t[:, h, qb * 512:(qb + 1) * 512], ALU.mult)
            tap("outb%d" % l, outb.t[:, 0, :], [128, 1024], [outb.k[0]])
            for whole_, subs_ in [(P[0], PH[0:2]), (P[1], PH[2:4]), (P[3], [PT3])]:
                for sub_ in subs_:
                    for k_, v_ in sub_.b.readers.items():
                        if whole_.b.readers.get(k_, 0) < v_:
                            whole_.b.readers[k_] = v_
                    w_ = sub_.b.writer
                    if w_ is not None:
                        if whole_.b.writer is None or (whole_.b.writer[0] == w_[0] and whole_.b.writer[1] < w_[1]):
                            whole_.b.writer = w_
                        elif whole_.b.writer[0] != w_[0]:
                            if whole_.b.readers.get(w_[0], 0) < w_[1]:
                                whole_.b.readers[w_[0]] = w_[1]
            merge_branch(l, 1, outb)
            fw.barrier()

    def branch_c(l):
        with ExitStack() as sc_:
            outc = sbt(sc_, "outc", [128, 4, 1024], BF16, nsub=4)
            cw = sbt(sc_, "cw", [128, 60])
            beta_t = sbt(sc_, "beta_t", [128, 8, 8]); g_t = sbt(sc_, "g_t", [128, 8, 8])
            alog = sbt(sc_, "alog", [128, 8]); dtb = sbt(sc_, "dtb", [128, 8]); gnc = sbt(sc_, "gnc", [128, 128])
            xp = sbt(sc_, "xp", [128, 4, 260])
            qf = sbt(sc_, "qf", [128, 1024]); kf = sbt(sc_, "kf", [128, 1024])
            k_tok = sbt(sc_, "k_tok", [128, 1024]); v_tok = sbt(sc_, "v_tok", [128, 1024])
            o_acc = sbt(sc_, "o_acc", [128, 1024])
            PTm = sbt(sc_, "PTm", [128, 1024]); RT = sbt(sc_, "RTm", [128, 1024])
            Dm = [sbt(sc_, "Dm%d" % d, [128, 1024]) for d in range(2)]
            DT = [sbt(sc_, "DT%d" % d, [128, 1024]) for d in range(2)]
            U = [sbt(sc_, "U%d" % d, [128, 1024]) for d in range(2)]
            WT = [sbt(sc_, "WT%d" % d, [128, 1024]) for d in range(2)]
            sml = [{n_: sbt(sc_, "%s%d" % (n_, d), [128, 8]) for n_ in ("gcol", "egc", "nbeta", "bge", "kds")}
                   for d in range(2)]
            egl = [sbt(sc_, "egl%d" % d, [128, 16]) for d in range(2)]
            S = [[sbt(sc_, "S%d%d" % (d, i), [128, 128]) for i in range(2)] for d in range(2)]
            VN = [[sbt(sc_, "VN%d%d" % (d, i), [128, 128]) for i in range(1)] * 2 for d in range(2)]
            otmp = [[sbt(sc_, "otmp%d%d" % (d, i), [128, 128]) for i in range(1)] * 2 for d in range(2)]
            rs8 = sbt(sc_, "rs8", [128, 8])
            gcr, Pm = SCA[0], rstd
            X1, X2 = SCA[1], SCA[2]
            epsc = sbt(sc_, "epsc", [128, 3])
            OP("pool", "memset", epsc.t[:, 0:1], EPS, W=[epsc])
            OP("pool", "memset", epsc.t[:, 1:2], 0.0, W=[epsc])
            OP("pool", "memset", epsc.t[:, 2:3], float(np.log(128 ** -0.5)), W=[epsc])
            DMA("sp", cw.t[:], D["dnconv"][l].rearrange("p j k -> p (j k)"), W=[cw])
            DMA("sp", alog.t[:], D["dnalog"][l], W=[alog])
            DMA("sp", dtb.t[:], D["dndt"][l], W=[dtb])
            DMA("sp", gnc.t[:], D["dnng"][l], W=[gnc])

            def r3(ap):
                return ap.rearrange("p (a e) -> p a e", a=8)

            def bc_tt(ap128):
                return ap128.rearrange("p (o e) -> p o e", o=1).broadcast_to([128, 8, 128])

            wba = load_w(D["w_in"][l][:, OFF["beta"]:OFF["beta"] + 16], 8, 16)
            for tt in range(8):
                for kt in range(8):
                    MM(P[0], P[0].t[:, tt * 16:(tt + 1) * 16], hnT.k[kt], hnT.t[:, kt, tt * 128:(tt + 1) * 128],
                       wba, wba.t[:, kt, 0:16], start=(kt == 0), stop=(kt == 7))
            pb = P[0].t[:, 0:128].rearrange("p (a c) -> p a c", a=8)
            ACT(beta_t, beta_t.t[:], P[0], pb[:, :, 0:8], AF.Sigmoid)
            dtbB = dtb.t[:].rearrange("p (o c) -> p o c", o=1).broadcast_to([128, 8, 8])
            TT("dve", g_t, g_t.t[:], P[0], pb[:, :, 8:16], dtb, dtbB, ALU.add)
            ACT(g_t, g_t.t[:], g_t, g_t.t[:], AF.Exp)
            ACT(g_t, g_t.t[:], g_t, g_t.t[:], AF.Ln, bias=1.0)
            ACT(alog, alog.t[:], alog, alog.t[:], AF.Exp)
            TS("dve", alog, alog.t[:], alog, alog.t[:], -1.0, None, ALU.mult)
            alB = alog.t[:].rearrange("p (o c) -> p o c", o=1).broadcast_to([128, 8, 8])
            TT("dve", g_t, g_t.t[:], g_t, g_t.t[:], alog, alB, ALU.mult)
            tap("g_t%d" % l, g_t.t[:], [128, 8, 8], [g_t])
            tap("beta_t%d" % l, beta_t.t[:], [128, 8, 8], [beta_t])
            if stop_after == "c0":
                halt[0] = True
                return


            QF = [qf, sbt(sc_, "qf1", [128, 1024])]

            def conv_phase(h):
                th = []
                wh = wb_next()
                for i_, nm_ in enumerate(("q_c", "k_c", "v_c", "z_c")):
                    c0_ = OFF[nm_] + h * 128
                    load_into(wh, i_ * 128, D["w_in"][l][:, c0_:c0_ + 128], 8, 128)
                qdst = QF[h % 2]
                tmp, l2t = X1, gcr

                def conv_tile(m, j):
                    pp = P[2]
                    a_ = X2
                    th.append(lambda: project(wh, 8, m * 128, hn_rhs, pp))
                    th.append(lambda: ACT(xp, xp.t[:, :, 2:258], pp, v3(pp.t[:]), AF.Copy))

                    def halo():
                        OP("dve", "memset", xp.t[:, 0, 0:2], 0.0, W=[xp])
                        OP("dve", "memset", xp.t[:, 3, 258:260], 0.0, W=[xp])
                        OP("dve", "tensor_scalar", xp.t[:, 1:4, 0:2], xp.t[:, 0:3, 256:258], flag.t[:, 0:1], None,
                           ALU.mult, R=[xp, flag], W=[xp])
                        OP("dve", "tensor_scalar", xp.t[:, 0:3, 258:260], xp.t[:, 1:4, 2:4], flag.t[:, 0:1], None,
                           ALU.mult, R=[xp, flag], W=[xp])
                    th.append(halo)
                    th.append(lambda: TS("dve", a_, v3(a_.t[:]), xp, xp.t[:, :, 0:256], cw.t[:, j * 5:j * 5 + 1], None,
                                         ALU.mult, RS=[cw]))
                    for k in range(1, 5):
                        th.append(lambda k=k: STT("dve", a_, v3(a_.t[:]), xp, xp.t[:, :, k:k + 256],
                                                  cw.t[:, j * 5 + k:j * 5 + k + 1], a_, v3(a_.t[:]), ALU.mult, ALU.add,
                                                  RS=[cw]))
                    th.append(lambda: ACT(tmp, tmp.t[:], a_, a_.t[:], AF.Silu))

                def l2n(dstT, scl):
                    def f1():
                        sb_ = scB()
                        ACT(sb_, sb_.t[:], tmp, tmp.t[:], AF.Square)
                        for half in range(2):
                            MM(P[3], P[3].t[:, half * 512:(half + 1) * 512], onesb, onesb.t[:], sb_,
                               sb_.t[:, half * 512:(half + 1) * 512])
                    th.append(f1)
                    th.append(lambda: ACT(l2t, l2t.t[:], P[3], P[3].t[:], AF.Ln, RS=[epsc], bias=epsc.t[:, 0:1]))
                    th.append(lambda: ACT(l2t, l2t.t[:], l2t, l2t.t[:], AF.Exp, RS=[epsc], scale=-0.5,
                                          bias=epsc.t[:, 1:2] if scl == 1.0 else epsc.t[:, 2:3]))
                    th.append(lambda: TT("dve", dstT, dstT.t[:], tmp, tmp.t[:], l2t, l2t.t[:], ALU.mult))

                conv_tile(0, h)
                l2n(qdst, 128 ** -0.5)
                conv_tile(1, 4 + h)
                l2n(kf, 1.0)

                def ktr():
                    for tt in range(8):
                        OP("pe", "transpose", P[3].t[:, tt * 128:(tt + 1) * 128], kf.t[:, tt * 128:(tt + 1) * 128],
                           ident.t[:], R=[kf, ident], W=[P[3]])
                    ACT(k_tok, k_tok.t[:], P[3], P[3].t[:], AF.Copy)
                th.append(ktr)
                conv_tile(2, 8 + h)

                def vtr():
                    for tt in range(8):
                        OP("pe", "transpose", P[3].t[:, tt * 128:(tt + 1) * 128], tmp.t[:, tt * 128:(tt + 1) * 128],
                           ident.t[:], R=[tmp, ident], W=[P[3]])
                    ACT(v_tok, v_tok.t[:], P[3], P[3].t[:], AF.Copy)
                th.append(vtr)
                return th, wh

            pending, wh_next = conv_phase(0)
            for h in range(4):
                for f_ in pending:
                    f_()
                pending = []
                wh = wh_next
                qf = QF[h % 2]
                if h == 0:
                    tap("qf%d" % l, qf.t[:], [128, 1024], [qf])
                    tap("kf%d" % l, kf.t[:], [128, 1024], [kf])
                    tap("vtok%d" % l, v_tok.t[:], [128, 1024], [v_tok])

                for dr in range(2):
                    cb = dr * 4 + h
                    sm_ = sml[dr]
                    gcol, egc, nbeta, bge, kds = (sm_[n_] for n_ in ("gcol", "egc", "nbeta", "bge", "kds"))
                    TRI = cm.t[:, dr, :]
                    MASK = cm.t[:, 1 - dr, :]
                    for tt in range(8):
                        OP("act", "activation", X1.t[:, tt * 128:(tt + 1) * 128], cm.t[:, 2, :], AF.Copy,
                           scale=g_t.t[:, tt, cb:cb + 1], R=[cm, g_t], W=[X1])
                    for tt in range(8):
                        MM(P[0], P[0].t[:, tt * 128:(tt + 1) * 128], X1, X1.t[:, tt * 128:(tt + 1) * 128], cm, TRI)
                        MM(P[1], P[1].t[:, tt:tt + 1], cm, TRI, g_t, g_t.t[:, tt, cb:cb + 1])
                        MM(P[2], P[2].t[:, tt * 128:(tt + 1) * 128], kf, kf.t[:, tt * 128:(tt + 1) * 128],
                           kf, kf.t[:, tt * 128:(tt + 1) * 128])
                    ACT(gcr, gcr.t[:], P[0], P[0].t[:], AF.Copy)
                    OP("dve", "tensor_copy", gcol.t[:], P[1].t[:, 0:8], R=[P[1]], W=[gcol])
                    ACT(egc, egc.t[:], gcol, gcol.t[:], AF.Exp)
                    TS("dve", nbeta, nbeta.t[:], beta_t, beta_t.t[:, :, cb], -1.0, None, ALU.mult)
                    TT("dve", bge, bge.t[:], beta_t, beta_t.t[:, :, cb], egc, egc.t[:], ALU.mult)
                    if stop_after == "c2a0":
                        halt[0] = True
                        return
                    D_ = Dm[dr]
                    TT("dve", D_, r3(D_.t[:]), gcol, col3(gcol.t[:], 128), gcr, r3(gcr.t[:]), ALU.subtract)
                    TS("dve", D_, D_.t[:], D_, D_.t[:], 0.0, None, ALU.min)
                    ACT(D_, D_.t[:], D_, D_.t[:], AF.Exp)
                    TT("dve", D_, r3(D_.t[:]), D_, r3(D_.t[:]), cm, bc_tt(MASK), ALU.mult)
                    TT("dve", X2, r3(X2.t[:]), D_, r3(D_.t[:]), ident, bc_tt(ident.t[:]), ALU.subtract)
                    TT("dve", Pm, r3(Pm.t[:]), P[2], r3(P[2].t[:]), nbeta, col3(nbeta.t[:], 128), ALU.mult)
                    TT("dve", Pm, Pm.t[:], Pm, Pm.t[:], X2, X2.t[:], ALU.mult)
                    if stop_after == "c2a1":
                        halt[0] = True
                        return
                    for tt in range(8):
                        OP("pe", "transpose", P[3].t[:, tt * 128:(tt + 1) * 128], Pm.t[:, tt * 128:(tt + 1) * 128],
                           ident.t[:], R=[Pm, ident], W=[P[3]])
                    ACT(PTm, PTm.t[:], P[3], P[3].t[:], AF.Copy)
                    if stop_after == "c2a2":
                        halt[0] = True
                        return
                    TT("dve", RT, r3(RT.t[:]), PTm, r3(PTm.t[:]), ident, bc_tt(ident.t[:]), ALU.add)
                    if stop_after == "c2a3":
                        halt[0] = True
                        return
                    for tt in range(8):
                        OP("pe", "transpose", P[0].t[:, tt * 128:(tt + 1) * 128], D_.t[:, tt * 128:(tt + 1) * 128],
                           ident.t[:], R=[D_, ident], W=[P[0]])
                    ACT(DT[dr], DT[dr].t[:], P[0], P[0].t[:], AF.Copy)
                    if stop_after == "c2a":
                        halt[0] = True
                        return
                    for k in range(1, 6):
                        for tt in range(8):
                            sl = slice(tt * 128, (tt + 1) * 128)
                            MM(P[0], P[0].t[:, sl], PTm, PTm.t[:, sl], Pm, Pm.t[:, sl])
                            if k < 5:
                                MM(P[1], P[1].t[:, sl], Pm, Pm.t[:, sl], PTm, PTm.t[:, sl])
                        ACT(Pm, Pm.t[:], P[0], P[0].t[:], AF.Copy)
                        if k < 5:
                            OP("dve", "tensor_copy", PTm.t[:], P[1].t[:], R=[P[1]], W=[PTm])
                        for tt in range(8):
                            sl = slice(tt * 128, (tt + 1) * 128)
                            MM(P[2], P[2].t[:, sl], Pm, Pm.t[:, sl], RT, RT.t[:, sl])
                        TT("dve", RT, RT.t[:], RT, RT.t[:], P[2], P[2].t[:], ALU.add)
                    if stop_after == "c2b":
                        halt[0] = True
                        return
                    TT("dve", X1, r3(X1.t[:]), v_tok, r3(v_tok.t[:]), beta_t, col3(beta_t.t[:, :, cb], 128), ALU.mult)
                    TT("dve", X2, r3(X2.t[:]), k_tok, r3(k_tok.t[:]), bge, col3(bge.t[:], 128), ALU.mult)
                    for tt in range(8):
                        sl = slice(tt * 128, (tt + 1) * 128)
                        MM(P[0], P[0].t[:, sl], RT, RT.t[:, sl], X1, X1.t[:, sl])
                        MM(P[1], P[1].t[:, sl], X2, X2.t[:, sl], RT, RT.t[:, sl])
                        MM(P[2], P[2].t[:, sl], kf, kf.t[:, sl], qf, qf.t[:, sl])
                    ACT(U[dr], U[dr].t[:], P[0], P[0].t[:], AF.Copy)
                    ACT(WT[dr], WT[dr].t[:], P[1], P[1].t[:], AF.Copy)
                    TT("dve", DT[dr], DT[dr].t[:], P[2], P[2].t[:], DT[dr], DT[dr].t[:], ALU.mult)
                    g3 = r3(gcr.t[:])
                    for hp in (0, 64):
                        lc = hp + 63 if dr == 0 else hp
                        TT("dve", kds, kds.t[hp:hp + 64, :], gcr, g3[hp:hp + 64, :, lc], gcol, gcol.t[hp:hp + 64, :],
                           ALU.subtract)
                    ACT(kds, kds.t[:], kds, kds.t[:], AF.Exp)
                    lc0 = 63 if dr == 0 else 0
                    ACT(egl[dr], egl[dr].t[:].rearrange("p (a c) -> p a c", a=8), gcr, g3[:, :, lc0::64], AF.Exp)
                    TT("dve", D_, r3(D_.t[:]), k_tok, r3(k_tok.t[:]), kds, col3(kds.t[:], 128), ALU.mult)
                    if h == 0 and dr == 0:
                        tap("U%d" % l, U[0].t[:], [128, 1024], [U[0]])
                        tap("RT%d" % l, RT.t[:], [128, 1024], [RT])
                if stop_after == "c2":
                    halt[0] = True
                    return
                OP("pool", "memset", o_acc.t[:], 0.0, W=[o_acc])
                if h + 1 < 4:
                    pending, wh_next = conv_phase(h + 1)
                per_step = (len(pending) + 15) // 16
                cur = [0, 0]
                for dr in range(2):
                    DMA("sp", S[dr][0].t[:], D["dns0"][:, l, dr, h, :], W=[S[dr][0]])
                for step in range(16):
                    for dr in range(2):
                        c = step if dr == 0 else 15 - step
                        tt, hp, ci = c // 2, (c % 2) * 64, c % 2
                        sm_ = sml[dr]
                        egc = sm_["egc"]
                        Sc = S[dr][cur[dr]]
                        Sn = S[dr][1 - cur[dr]]
                        boundary = (dr == 0 and c % 4 == 0 and c > 0) or (dr == 1 and c % 4 == 3 and c < 15)
                        if boundary:
                            seq = c // 4 - 1 if dr == 0 else c // 4 + 1
                            DMA("sp", O["ndn"][:, l, dr, h, seq, :], Sc.t[:], R=[Sc])
                            TS("dve", Sn, Sn.t[:], Sc, Sc.t[:], flag.t[:, 0:1], None, ALU.mult, RS=[flag])
                            cur[dr] = 1 - cur[dr]
                            Sc, Sn = Sn, Sc
                        pd = P[dr]
                        pc = (step % 2) * 512
                        vn = VN[dr][step % 2]
                        ot = otmp[dr][step % 2]
                        MM(pd, pd.t[hp:hp + 64, pc:pc + 128], WT[dr], WT[dr].t[:, tt * 128 + hp:tt * 128 + hp + 64], Sc, Sc.t[:])
                        MM(pd, pd.t[hp:hp + 64, pc + 128:pc + 256], qf, qf.t[:, c * 64:(c + 1) * 64], Sc, Sc.t[:])
                        TT("dve", vn, vn.t[hp:hp + 64, :], U[dr], U[dr].t[hp:hp + 64, tt * 128:(tt + 1) * 128],
                           pd, pd.t[hp:hp + 64, pc:pc + 128], ALU.subtract)
                        MM(pd, pd.t[hp:hp + 64, pc + 256:pc + 384],
                           DT[dr], DT[dr].t[hp:hp + 64, tt * 128 + hp:tt * 128 + hp + 64], vn, vn.t[hp:hp + 64, :])
                        MM(pd, pd.t[:, pc + 384:pc + 512], Dm[dr], Dm[dr].t[hp:hp + 64, tt * 128:(tt + 1) * 128],
                           vn, vn.t[hp:hp + 64, :])
                        TS("dve", ot, ot.t[hp:hp + 64, :], pd, pd.t[hp:hp + 64, pc + 128:pc + 256],
                           egc.t[hp:hp + 64, tt:tt + 1], None, ALU.mult, RS=[egc])
                        TT("dve", ot, ot.t[hp:hp + 64, :], ot, ot.t[hp:hp + 64, :], pd, pd.t[hp:hp + 64, pc + 256:pc + 384],
                           ALU.add)
                        TT("pool", o_acc, o_acc.t[hp:hp + 64, tt * 128:(tt + 1) * 128],
                           o_acc, o_acc.t[hp:hp + 64, tt * 128:(tt + 1) * 128], ot, ot.t[hp:hp + 64, :], ALU.add)
                        STT("dve", Sn, Sn.t[:], Sc, Sc.t[:], egl[dr].t[:, tt * 2 + ci:tt * 2 + ci + 1],
                            pd, pd.t[:, pc + 384:pc + 512], ALU.mult, ALU.add, RS=[egl[dr]])
                        cur[dr] = 1 - cur[dr]
                    for _ in range(per_step):
                        if pending:
                            pending.pop(0)()
                for dr in range(2):
                    seq = 3 if dr == 0 else 0
                    Sc = S[dr][cur[dr]]
                    DMA("sp", O["ndn"][:, l, dr, h, seq, :], Sc.t[:], R=[Sc])
                if h == 0:
                    tap("oacc%d" % l, o_acc.t[:], [128, 1024], [o_acc])
                if stop_after == "c3":
                    halt[0] = True
                    return
                TT("dve", X1, X1.t[:], o_acc, o_acc.t[:], o_acc, o_acc.t[:], ALU.mult)
                OP("dve", "tensor_reduce", rs8.t[:], r3(X1.t[:]), AX, ALU.add, R=[X1], W=[rs8])
                TS("dve", rs8, rs8.t[:], rs8, rs8.t[:], 1.0 / 128, EPS, ALU.mult, ALU.add)
                OP("dve", "reciprocal", rs8.t[:], rs8.t[:], R=[rs8], W=[rs8])
                ACT(rs8, rs8.t[:], rs8, rs8.t[:], AF.Sqrt)
                TT("dve", X1, r3(X1.t[:]), o_acc, r3(o_acc.t[:]), rs8, col3(rs8.t[:], 128), ALU.mult)
                TT("dve", X1, r3(X1.t[:]), X1, r3(X1.t[:]), gnc, bc_tt(gnc.t[:]), ALU.mult)
                for tt in range(8):
                    OP("pe", "transpose", P[2].t[:, tt * 128:(tt + 1) * 128], X1.t[:, tt * 128:(tt + 1) * 128], ident.t[:],
                       R=[X1, ident], W=[P[2]])
                project(wh, 8, 3 * 128, hn_rhs, P[3])
                z_ = scB()
                ACT(z_, z_.t[:], P[3], P[3].t[:], AF.Silu)
                TT("dve", outc.k[h], outc.t[:, h, :], P[2], P[2].t[:], z_, z_.t[:], ALU.mult)
            tap("outc%d" % l, outc.t[:, 0, :], [128, 1024], [outc.k[0]])
            merge_branch(l, 2, outc)
            fw.barrier()

    for l in range(nlayers):
        norm_modulate(l)
        if stop_after == "norm":
            break
        if "a" not in skip:
            branch_a(l)
        if stop_after == "a":
            break
        if "b" not in skip:
            branch_b(l)
        if stop_after == "b":
            break
        branch_c(l)
        if stop_after == "c" or halt[0]:
            break
        out_proj_residual(l)
    rms_stats()
    fng = sbt(top, "fng", [128, 8])
    DMA("sp", fng.t[:], D["fng"], W=[fng])
    for kt in range(8):
        sa = scA()
        OP("dve", "scalar_tensor_tensor", sa.t[:], xT.t[:, kt, :], fng.t[:, kt:kt + 1], rstd.t[:],
           ALU.mult, ALU.mult, R=[xT.k[kt], fng, rstd], W=[sa])
        DMA("sp", O["yT"][:, kt, :], sa.t[:], R=[sa])
    fw.finish()
    n_inst = fw.n_inst
    top.close()
    return nc, list(DBG.keys()), n_inst


_CACHE = {}


def kernel(**inputs):
    maps = prep_inputs(inputs)
    if "nc" not in _CACHE:
        _CACHE["nc"] = build()[0]
    nc = _CACHE["nc"]
    res = run_bass_kernel_spmd(nc, maps, core_ids=list(range(8)))
    return assemble(res.results)
```

```python
import numpy as np
import concourse.bass as bass
import concourse.mybir as mybir
from concourse.bass_utils import run_bass_kernel_spmd

F32 = mybir.dt.float32
BF16 = mybir.dt.bfloat16
ALU = mybir.AluOpType
AF = mybir.ActivationFunctionType


class Buf:
    def __init__(self, name):
        self.name = name
        self.writer = None
        self.readers = {}


class Eng:
    def __init__(self, fw, name, eng, sem):
        self.fw, self.name, self.eng, self.sem = fw, name, eng, sem
        self.count = 0
        self.known = {}


class FW:
    NDMA = 8

    def __init__(self, nc, stack):
        self.nc = nc
        self.sems = {}
        self.engs = {}
        for name, eng in (("pe", nc.tensor), ("dve", nc.vector), ("act", nc.scalar),
                          ("pool", nc.gpsimd), ("sp", nc.sync)):
            sem = stack.enter_context(nc.semaphore("s_" + name))
            self.sems[name] = sem
            self.engs[name] = Eng(self, name, eng, sem)
        self.dma_sems = {}
        self.dma_cnt = {}
        self.dma_idx = {}
        for q in ("sp", "act", "pool"):
            self.dma_sems[q] = []
            for i in range(self.NDMA):
                key = "d_%s%d" % (q, i)
                sem = stack.enter_context(nc.semaphore(key))
                self.sems[key] = sem
                self.dma_sems[q].append(key)
                self.dma_cnt[key] = 0
            self.dma_idx[q] = 0
        self.n_inst = 0

    def _deps(self, ename, reads, writes):
        deps = {}

        def add(k, v):
            if v > deps.get(k, 0):
                deps[k] = v

        for b in reads:
            if b.writer is not None:
                add(*b.writer)
        for b in writes:
            if b.writer is not None:
                k, v = b.writer
                if not (k == "pe" and ename == "pe"):
                    add(k, v)
            for k, v in b.readers.items():
                add(k, v)
        return deps

    def _emit_waits(self, e, deps):
        for k, v in deps.items():
            if e.known.get(k, 0) < v:
                e.eng.wait_ge(self.sems[k], v)
                e.known[k] = v

    def _record(self, key, val, reads, writes):
        for b in reads:
            if b.readers.get(key, 0) < val:
                b.readers[key] = val
        for b in writes:
            b.writer = (key, val)
            b.readers = {}

    def op(self, ename, fn, reads=(), writes=()):
        e = self.engs[ename]
        deps = self._deps(ename, reads, writes)
        if ename == "pe":
            deps.pop("pe", None)
        self._emit_waits(e, deps)
        ins = fn(e.eng)
        e.count += 1
        ins.then_inc(e.sem, 1)
        e.known[ename] = max(e.known.get(ename, 0), 0)
        self._record(ename, e.count, reads, writes)
        self.n_inst += 1
        return ins

    def dma(self, q, out, in_, reads=(), writes=(), **kw):
        e = self.engs[q]
        ring = self.dma_sems[q]
        key = ring[self.dma_idx[q] % self.NDMA]
        self.dma_idx[q] += 1
        deps = self._deps("dma", reads, writes)
        if self.dma_cnt[key] > 0:
            deps[key] = max(deps.get(key, 0), self.dma_cnt[key])
        self._emit_waits(e, deps)
        e.eng.dma_start(out=out, in_=in_, **kw).then_inc(self.sems[key], 16)
        self.dma_cnt[key] += 16
        self._record(key, self.dma_cnt[key], reads, writes)
        self.n_inst += 1

    def finish(self):
        e = self.engs["sp"]
        for key, cnt in self.dma_cnt.items():
            if cnt > 0 and e.known.get(key, 0) < cnt:
                e.eng.wait_ge(self.sems[key], cnt)
                e.known[key] = cnt
        for name in ("pe", "dve", "act", "pool"):
            c = self.engs[name].count
            if c > 0:
                e.eng.wait_ge(self.sems[name], c)

    def barrier(self):
        snap = {n: self.engs[n].count for n in ("pe", "dve", "act", "pool")}
        dsnap = dict(self.dma_cnt)
        for n in ("pe", "dve", "act", "pool", "sp"):
            e = self.engs[n]
            for k, v in list(snap.items()) + list(dsnap.items()):
                if v > 0 and k != n and e.known.get(k, 0) < v:
                    e.eng.wait_ge(self.sems[k], v)
                    e.known[k] = v


class TV:
    def __init__(self, t, name):
        self.t = t
        self.b = Buf(name)


class T:
    def __init__(self, t, name, nsub=0):
        self.t = t
        self.b = None if nsub else Buf(name)
        self.k = [TV(t, "%s.%d" % (name, i)) for i in range(nsub)]


NT = 1024
DM = 1024
NIN = 8208
EPS = 1e-6
OFF = dict(u_a=0, z_a=512, q_b=1024, k_b=1536, v_b=2048, z_b=2560, q_c=3072, k_c=3584,
           v_c=4096, z_c=4608, beta=5120, alpha=5128, gates=5136)
LAM_INIT = [0.8 - 0.6 * float(np.exp(-0.3 * l)) for l in range(2)]
S5_BF16 = True

IN_SPECS = [
    ("xT", [128, 8, 1024]), ("cond", [128, 8]), ("flag", [128, 1]),
    ("ropec", [128, 1024]), ("ropes", [128, 1024]), ("maskb", [128, 48]),
    ("kctx", [128, 2, 4, 512]), ("vctx", [128, 2, 4, 512]),
    ("s5h0r", [128, 2, 2, 16]), ("s5h0i", [128, 2, 2, 16]), ("dns0", [128, 2, 2, 4, 128]),
    ("w_ada", [2, 1024, 3072]), ("b_ada", [2, 128, 24]), ("norm_g", [2, 128, 8]), ("fng", [128, 8]),
    ("w_in", [2, 1024, NIN]), ("w_qkp", [2, 1024, 1024]),
    ("lamre_c", [2, 128, 2, 4, 64]), ("lamim_c", [2, 128, 2, 4, 64]), ("lstep_c", [2, 128, 2, 4, 64]),
    ("bre_c", [2, 128, 2, 4, 64]), ("bim_c", [2, 128, 2, 4, 64]),
    ("lamre_s", [2, 128, 2, 16]), ("lamim_s", [2, 128, 2, 16]), ("lstep_s", [2, 128, 2, 16]),
    ("cre_s", [2, 128, 2, 16, 16]), ("cim_s", [2, 128, 2, 16, 16]),
    ("s5d", [2, 128, 4]), ("wglu", [2, 512, 512]), ("dalam", [2, 128, 256]), ("dang", [2, 128, 128]),
    ("dnconv", [2, 128, 12, 5]), ("dnalog", [2, 128, 8]), ("dndt", [2, 128, 8]), ("dnng", [2, 128, 128]),
    ("w_branch", [2, 3, 512, 1024]), ("w_out", [2, 1024, 1024]),
    ("ident", [128, 128]), ("maskB", [128, 8]), ("cm", [128, 3, 128]),
]
OUT_SPECS = [
    ("yT", [128, 8, 1024]), ("newk", [128, 2, 4, 1024]), ("newv", [128, 2, 4, 1024]),
    ("ns5r", [128, 2, 2, 16, 4]), ("ns5i", [128, 2, 2, 16, 4]), ("ndn", [128, 2, 2, 4, 4, 128]),
]


def _f(a):
    return np.ascontiguousarray(np.asarray(a, dtype=np.float32))


def _rope_tables():
    t = np.arange(1024)
    row = (t // 64).astype(np.float32)
    col = (t % 64).astype(np.float32)
    nf = 16
    inv = (np.float32(10000.0) ** (-np.arange(nf, dtype=np.float32) / np.float32(nf))).astype(np.float32)
    cosT = np.zeros((128, 1024), np.float32)
    sinT = np.zeros((128, 1024), np.float32)
    for p in range(128):
        d = p % 64
        pos = row if d < 32 else col
        j = d % 16
        ang = (pos * inv[j]).astype(np.float32)
        first = (d % 32) < 16
        cosT[p] = np.cos(ang)
        sinT[p] = -np.sin(ang) if first else np.sin(ang)
    return cosT, sinT


def _const_masks():
    i = np.arange(128)
    same = (i[:, None] // 64) == (i[None, :] // 64)
    cm = np.zeros((128, 3, 128), np.float32)
    cm[:, 0, :] = same & (i[:, None] <= i[None, :])
    cm[:, 1, :] = same & (i[:, None] >= i[None, :])
    cm[:, 2, :] = 1.0
    return cm


def prep_inputs(inp):
    g = {k: np.asarray(v) for k, v in inp.items()}
    sh = {}
    sh["w_ada"] = _f(g["w_ada"])
    sh["b_ada"] = _f(g["b_ada"].reshape(2, 24, 128).transpose(0, 2, 1))
    sh["norm_g"] = _f(g["norm_g"].reshape(2, 8, 128).transpose(0, 2, 1))
    sh["fng"] = _f(g["final_norm_g"].reshape(8, 128).T)
    sh["w_in"] = _f(g["w_in"])
    d = np.arange(64)
    perm = np.where((d % 32) < 16, d + 16, d - 16)
    cols = []
    for base in (OFF["q_b"], OFF["k_b"]):
        for hm in range(8):
            cols.append(base + hm * 64 + perm)
    cols = np.concatenate(cols)
    sh["w_qkp"] = _f(g["w_in"][:, :, cols])
    def chmaj(a):
        a = a.reshape(2, 2, 4, 8, 64)
        a = np.broadcast_to(a[:, :, :, :, None, :], (2, 2, 4, 8, 16, 64))
        return _f(a.transpose(0, 3, 4, 1, 2, 5).reshape(2, 128, 2, 4, 64))
    sh["lamre_c"] = chmaj(g["s5_lam_re"])
    sh["lamim_c"] = chmaj(g["s5_lam_im"])
    sh["lstep_c"] = chmaj(np.broadcast_to(g["s5_log_step"][..., None], (2, 2, 32, 64)))
    def bmaj(a):
        a = a.reshape(2, 2, 4, 8, 64, 16)
        return _f(a.transpose(0, 3, 5, 1, 2, 4).reshape(2, 128, 2, 4, 64))
    sh["bre_c"] = bmaj(g["s5_b_re"])
    sh["bim_c"] = bmaj(g["s5_b_im"])
    def stmaj(a):
        a = a.reshape(2, 2, 16, 2, 64)
        return _f(a.transpose(0, 3, 4, 1, 2).reshape(2, 128, 2, 16))
    sh["lamre_s"] = stmaj(g["s5_lam_re"])
    sh["lamim_s"] = stmaj(g["s5_lam_im"])
    sh["lstep_s"] = stmaj(np.broadcast_to(g["s5_log_step"][..., None], (2, 2, 32, 64)))
    def cmaj(a):
        a = a.reshape(2, 2, 16, 2, 16, 64)
        return _f(a.transpose(0, 3, 5, 1, 2, 4).reshape(2, 128, 2, 16, 16))
    sh["cre_s"] = cmaj(g["s5_c_re"])
    sh["cim_s"] = cmaj(g["s5_c_im"])
    sh["s5d"] = _f(g["s5_d"].reshape(2, 4, 128).transpose(0, 2, 1))
    sh["wglu"] = _f(g["s5_w_glu"])
    sh["dalam"] = _f(np.broadcast_to(g["da_lam"].reshape(2, 1, 256), (2, 128, 256)))
    sh["dang"] = _f(np.broadcast_to(g["da_norm_g"].reshape(2, 1, 128), (2, 128, 128)))
    sh["dnconv"] = _f(g["dn_conv"].reshape(2, 5, 12, 128).transpose(0, 3, 2, 1))
    sh["dnalog"] = _f(np.broadcast_to(g["dn_a_log"].reshape(2, 1, 8), (2, 128, 8)))
    sh["dndt"] = _f(np.broadcast_to(g["dn_dt_bias"].reshape(2, 1, 8), (2, 128, 8)))
    sh["dnng"] = _f(np.broadcast_to(g["dn_norm_g"].reshape(2, 1, 128), (2, 128, 128)))
    sh["w_branch"] = _f(g["w_branch"])
    sh["w_out"] = _f(g["w_out"])
    sh["ident"] = np.eye(128, dtype=np.float32)
    p = np.arange(128)
    mB = np.zeros((128, 8), np.float32)
    for st in range(4):
        for gl in range(2):
            mB[:, st * 2 + gl] = ((p // 32) == st) & (((p // 16) % 2) == gl)
    sh["maskB"] = mB
    sh["cm"] = _const_masks()
    cosT, sinT = _rope_tables()
    maps = []
    for c in range(8):
        m = dict(sh)
        if c < 4:
            X = g["x_prompt"][4 * c:4 * c + 4].reshape(1024, 1024)
            cond = g["c_ctx"]
            m["flag"] = np.zeros((128, 1), np.float32)
            m["ropec"] = np.ones((128, 1024), np.float32)
            m["ropes"] = np.zeros((128, 1024), np.float32)
            mb = np.full((12, 4), -30000.0, np.float32)
            for kt in range(8):
                mb[kt, kt // 2] = 0.0
            m["kctx"] = np.zeros((128, 2, 4, 512), np.float32)
            m["vctx"] = np.zeros((128, 2, 4, 512), np.float32)
            m["s5h0r"] = np.zeros((128, 2, 2, 16), np.float32)
            m["s5h0i"] = np.zeros((128, 2, 2, 16), np.float32)
            m["dns0"] = np.zeros((128, 2, 2, 4, 128), np.float32)
        else:
            b = c - 4
            X = g["x_sample"][b]
            cond = g["c"][b]
            m["flag"] = np.ones((128, 1), np.float32)
            m["ropec"] = cosT
            m["ropes"] = sinT
            mb = np.zeros((12, 4), np.float32)
            ck = g["cache_k"][b]
            m["kctx"] = _f(ck.transpose(3, 4, 0, 2, 1).reshape(128, 2, 4, 512))
            cv = g["cache_v"][b]
            m["vctx"] = _f(cv.reshape(2, 4, 128, 512).transpose(2, 0, 1, 3))
            def st5(a):
                a = a.reshape(2, 2, 16, 2, 64)
                return _f(a.transpose(3, 4, 0, 1, 2).reshape(128, 2, 2, 16))
            m["s5h0r"] = st5(g["state_s5_re"][b])
            m["s5h0i"] = st5(g["state_s5_im"][b])
            m["dns0"] = _f(g["state_dn"][b].transpose(3, 0, 1, 2, 4))
        m["maskb"] = _f(np.broadcast_to(mb.reshape(1, 48), (128, 48)))
        m["xT"] = _f(X.T.reshape(8, 128, 1024).transpose(1, 0, 2))
        m["cond"] = _f(cond.reshape(8, 128).T)
        maps.append(m)
    return maps


def assemble(results):
    y_prompt = np.zeros((16, 256, 1024), np.float32)
    y_sample = np.zeros((4, 1024, 1024), np.float32)
    nk = np.zeros((16, 2, 256, 4, 2, 64), np.float32)
    nv = np.zeros((16, 2, 256, 4, 128), np.float32)
    s5r = np.zeros((16, 2, 2, 32, 64), np.float32)
    s5i = np.zeros((16, 2, 2, 32, 64), np.float32)
    sdn = np.zeros((16, 2, 2, 4, 128, 128), np.float32)
    for c in range(8):
        r = results[c]
        Y = np.asarray(r["yT"]).transpose(1, 0, 2).reshape(1024, 1024).T
        if c < 4:
            y_prompt[4 * c:4 * c + 4] = Y.reshape(4, 256, 1024)
            k = np.asarray(r["newk"]).reshape(2, 64, 2, 4, 4, 256)
            nk[4 * c:4 * c + 4] = k.transpose(4, 2, 5, 3, 0, 1)
            v = np.asarray(r["newv"]).reshape(128, 2, 4, 4, 256)
            nv[4 * c:4 * c + 4] = v.transpose(3, 1, 4, 2, 0)
            for nm, dst in (("ns5r", s5r), ("ns5i", s5i)):
                a = np.asarray(r[nm]).reshape(2, 64, 2, 2, 16, 4)
                dst[4 * c:4 * c + 4] = a.transpose(5, 2, 3, 4, 0, 1).reshape(4, 2, 2, 32, 64)
            a = np.asarray(r["ndn"])
            sdn[4 * c:4 * c + 4] = a.transpose(4, 1, 2, 3, 0, 5)
        else:
            y_sample[c - 4] = Y
    return (y_prompt, y_sample, nk, nv, s5r, s5i, sdn)


def build(dbg=(), stop_after=None, nlayers=2, skip=()):
    from contextlib import ExitStack
    nc = bass.Bass("TRN2", target_bir_lowering=False)
    D = {n: nc.dram_tensor(n, list(s), F32, kind="ExternalInput").ap() for n, s in IN_SPECS}
    O = {n: nc.dram_tensor(n, list(s), F32, kind="ExternalOutput").ap() for n, s in OUT_SPECS}
    DBG = {}
    top = ExitStack()
    fw = FW(nc, top)

    uniq = [0]
    halt = [False]

    def sbt(stk, name, shape, dt=F32, nsub=0):
        uniq[0] += 1
        return T(stk.enter_context(nc.sbuf_tensor("t%d_%s" % (uniq[0], name), list(shape), dt)), name, nsub)

    def OP(eng, meth, *a, R=(), W=(), **kw):
        return fw.op(eng, lambda e: getattr(e, meth)(*a, **kw),
                     reads=[x.b for x in R], writes=[x.b for x in W])

    def DMA(q, out, in_, R=(), W=(), **kw):
        fw.dma(q, out, in_, reads=[x.b for x in R], writes=[x.b for x in W], **kw)

    def tap(name, src_ap, shape, R):
        if name in dbg:
            DBG[name] = nc.dram_tensor("dbg_" + name, list(shape), F32, kind="ExternalOutput").ap()
            q = "sp" if src_ap.tensor.dtype == F32 else "pool"
            DMA(q, DBG[name], src_ap, R=R)

    def MM(outT, out_ap, lhsT_T, lhsT_ap, rhs_T, rhs_ap, start=True, stop=True):
        return fw.op("pe", lambda e: e.matmul(out_ap, lhsT=lhsT_ap, rhs=rhs_ap, start=start, stop=stop),
                     reads=[lhsT_T.b, rhs_T.b], writes=[outT.b])

    def v3(ap, s=4):
        return ap.rearrange("p (s t) -> p s t", s=s)

    def col3(ap_p_q, n):
        q = ap_p_q.shape[1]
        return ap_p_q.rearrange("p (q o) -> p q o", o=1).broadcast_to([128, q, n])

    P = [T(top.enter_context(nc.psum_tensor("P%d" % i, [128, 1024], F32)), "P%d" % i) for i in range(4)]
    xT = sbt(top, "xT", [128, 8, 1024], nsub=8)
    hnT = sbt(top, "hnT", [128, 8, 1024], BF16, nsub=8)
    merged = sbt(top, "merged", [128, 8, 1024], nsub=8)
    WB = [sbt(top, "wb%d" % i, [128, 8, 512], BF16) for i in range(2)]
    wb_i = [0]
    ident = sbt(top, "ident", [128, 128])
    cm = sbt(top, "cm", [128, 3, 128])
    onesb = sbt(top, "onesb", [128, 128], BF16)
    maskB = sbt(top, "maskB", [128, 8])
    flag = sbt(top, "flag", [128, 1])
    modt = sbt(top, "modt", [128, 48])
    gmod = sbt(top, "gmod", [128, 16])
    rstd = sbt(top, "rstd", [128, 1024])
    SCA = [sbt(top, "sca%d" % i, [128, 1024]) for i in range(3)]
    SCB = [sbt(top, "scb%d" % i, [128, 1024], BF16) for i in range(2)]
    sci = [0, 0]

    def scA():
        sci[0] += 1
        return SCA[sci[0] % 3]

    def scB():
        sci[1] += 1
        return SCB[sci[1] % 2]

    for kt in range(8):
        DMA("sp", xT.t[:, kt, :], D["xT"][:, kt, :], W=[xT.k[kt]])
    DMA("sp", ident.t[:], D["ident"], W=[ident])
    DMA("sp", cm.t[:], D["cm"], W=[cm])
    DMA("sp", maskB.t[:], D["maskB"], W=[maskB])
    DMA("sp", flag.t[:], D["flag"], W=[flag])
    OP("pool", "memset", onesb.t[:], 1.0, W=[onesb])
    epsP = sbt(top, "epsP", [128, 1])
    OP("pool", "memset", epsP.t[:], EPS, W=[epsP])

    def wb_next():
        buf = WB[wb_i[0] % 2]
        wb_i[0] += 1
        return buf

    def load_into(buf, coff, dram_ap, ktiles, ncols):
        DMA("pool", buf.t[:, 0:ktiles, coff:coff + ncols], dram_ap.rearrange("(kt p) n -> p kt n", p=128), W=[buf])

    def load_w(dram_ap, ktiles, ncols):
        buf = wb_next()
        load_into(buf, 0, dram_ap, ktiles, ncols)
        return buf

    def project(wbuf, ktiles, c0, rhs_fn, pT):
        for half in range(2):
            for kt in range(ktiles):
                rT, rap = rhs_fn(kt, half)
                MM(pT, pT.t[:, half * 512:(half + 1) * 512], wbuf, wbuf.t[:, kt, c0:c0 + 128],
                   rT, rap, start=(kt == 0), stop=(kt == ktiles - 1))

    def hn_rhs(kt, half):
        return hnT.k[kt], hnT.t[:, kt, half * 512:(half + 1) * 512]

    def k_rhs(tt):
        return lambda kt, half: (tt.k[kt], tt.t[:, kt, half * 512:(half + 1) * 512])

    with ExitStack() as s0:
        cond = sbt(s0, "cond", [128, 8])
        scond = sbt(s0, "scond", [128, 8])
        bada = sbt(s0, "bada", [128, 48])
        ng = sbt(s0, "ng", [128, 16])
        wst = [sbt(s0, "wst%d" % i, [128, 8, 512]) for i in range(2)]
        DMA("sp", cond.t[:], D["cond"], W=[cond])
        for l in range(2):
            DMA("sp", bada.t[:, l * 24:(l + 1) * 24], D["b_ada"][l], W=[bada])
            DMA("sp", ng.t[:, l * 8:(l + 1) * 8], D["norm_g"][l], W=[ng])
        OP("act", "activation", scond.t[:], cond.t[:], AF.Silu, R=[cond], W=[scond])
        i = 0
        for l in range(2):
            wv = D["w_ada"][l].rearrange("(kt p) n -> p kt n", p=128)
            for nb in range(6):
                w = wst[i % 2]
                i += 1
                DMA("sp" if i % 2 == 1 else "act", w.t[:], wv[:, :, nb * 512:(nb + 1) * 512], W=[w])
                for m in range(4):
                    j = l * 24 + nb * 4 + m
                    for kt in range(8):
                        MM(P[0], P[0].t[:, j:j + 1], w, w.t[:, kt, m * 128:(m + 1) * 128],
                           scond, scond.t[:, kt:kt + 1], start=(kt == 0), stop=(kt == 7))
        OP("dve", "tensor_tensor", modt.t[:], P[0].t[:, 0:48], bada.t[:], ALU.add, R=[P[0], bada], W=[modt])
        for l in range(2):
            OP("dve", "scalar_tensor_tensor", gmod.t[:, l * 8:(l + 1) * 8], modt.t[:, l * 24 + 8:l * 24 + 16], 1.0,
               ng.t[:, l * 8:(l + 1) * 8], ALU.add, ALU.mult, R=[modt, ng], W=[gmod])
        tap("modt", modt.t[:], [128, 48], [modt])
        fw.barrier()

    def rms_stats():
        for kt in range(8):
            sb_ = scB()
            OP("act", "activation", sb_.t[:], xT.t[:, kt, :], AF.Square, R=[xT.k[kt]], W=[sb_])
            for half in range(2):
                MM(P[0], P[0].t[:, half * 512:(half + 1) * 512], onesb, onesb.t[:], sb_,
                   sb_.t[:, half * 512:(half + 1) * 512], start=(kt == 0), stop=(kt == 7))
        OP("act", "activation", rstd.t[:], P[0].t[:], AF.Ln, bias=epsP.t[:, 0:1], scale=1.0 / DM,
           R=[P[0], epsP], W=[rstd])
        OP("act", "activation", rstd.t[:], rstd.t[:], AF.Exp, scale=-0.5, R=[rstd], W=[rstd])

    def norm_modulate(l):
        rms_stats()
        for kt in range(8):
            sa = scA()
            OP("dve", "scalar_tensor_tensor", sa.t[:], xT.t[:, kt, :], gmod.t[:, l * 8 + kt:l * 8 + kt + 1],
               rstd.t[:], ALU.mult, ALU.mult, R=[xT.k[kt], gmod, rstd], W=[sa])
            OP("act", "activation", hnT.t[:, kt, :], sa.t[:], AF.Identity,
               bias=modt.t[:, l * 24 + kt:l * 24 + kt + 1], R=[sa, modt], W=[hnT.k[kt]])

    def merge_branch(l, n, outTs):
        for blk in range(2):
            wbr = load_w(D["w_branch"][l, n][:, blk * 512:(blk + 1) * 512], 4, 512)
            g0 = OFF["gates"] + n * 1024 + blk * 512
            wg = load_w(D["w_in"][l][:, g0:g0 + 512], 8, 512)
            for m in range(4):
                dt_ = blk * 4 + m
                pr, gp = P[(dt_ % 2) * 2], P[(dt_ % 2) * 2 + 1]
                project(wbr, 4, m * 128, k_rhs(outTs), pr)
                project(wg, 8, m * 128, hn_rhs, gp)
                sa = scA()
                OP("act", "activation", sa.t[:], gp.t[:], AF.Sigmoid, R=[gp], W=[sa])
                if n == 0:
                    OP("dve", "tensor_tensor", merged.t[:, dt_, :], sa.t[:], pr.t[:], ALU.mult,
                       R=[sa, pr], W=[merged.k[dt_]])
                else:
                    OP("dve", "tensor_tensor", sa.t[:], sa.t[:], pr.t[:], ALU.mult, R=[sa, pr], W=[sa])
                    OP("dve", "tensor_tensor", merged.t[:, dt_, :], merged.t[:, dt_, :], sa.t[:], ALU.add,
                       R=[sa, merged.k[dt_]], W=[merged.k[dt_]])

    def out_proj_residual(l):
        for kt in range(8):
            OP("act", "activation", hnT.t[:, kt, :], merged.t[:, kt, :], AF.Copy, R=[merged.k[kt]], W=[hnT.k[kt]])
        for blk in range(2):
            wo = load_w(D["w_out"][l][:, blk * 512:(blk + 1) * 512], 8, 512)
            for m in range(4):
                dt_ = blk * 4 + m
                pp = P[2 + m % 2]
                project(wo, 8, m * 128, hn_rhs, pp)
                OP("dve", "scalar_tensor_tensor", xT.t[:, dt_, :], pp.t[:],
                   modt.t[:, l * 24 + 16 + dt_:l * 24 + 17 + dt_],
                   xT.t[:, dt_, :], ALU.mult, ALU.add, R=[pp, modt, xT.k[dt_]], W=[xT.k[dt_]])

    def TT(eng, oT, o, aT, a, bT, b, op):
        OP(eng, "tensor_tensor", o, a, b, op, R=[aT, bT], W=[oT])

    def TS(eng, oT, o, aT, a, s1, s2, op0, op1=None, RS=()):
        if op1 is None:
            OP(eng, "tensor_scalar", o, a, s1, None, op0, R=[aT] + list(RS), W=[oT])
        else:
            OP(eng, "tensor_scalar", o, a, s1, s2, op0, op1, R=[aT] + list(RS), W=[oT])

    def STT(eng, oT, o, aT, a, sc, bT, b, op0, op1, RS=()):
        OP(eng, "scalar_tensor_tensor", o, a, sc, b, op0, op1, R=[aT, bT] + list(RS), W=[oT])

    def ACT(oT, o, aT, a, func, RS=(), **kw):
        OP("act", "activation", o, a, func, R=[aT] + list(RS), W=[oT], **kw)

    def cos_sin(stk, eng, th, n, tag):
        c = [sbt(stk, "cs_c%d%s" % (i, tag), [128, n]) for i in range(2)]
        s = [sbt(stk, "cs_s%d%s" % (i, tag), [128, n]) for i in range(2)]
        t = sbt(stk, "cs_t" + tag, [128, n])
        hp = sbt(stk, "cs_hp" + tag, [128, 1])
        OP(eng, "memset", hp.t[:], float(np.pi / 2), W=[hp])
        ACT(s[0], s[0].t[:], th, th.t[:], AF.Sin, scale=1.0 / 16)
        ACT(c[0], c[0].t[:], th, th.t[:], AF.Sin, RS=[hp], scale=1.0 / 16, bias=hp.t[:])
        for it in range(4):
            a, b = it % 2, (it + 1) % 2
            TT(eng, t, t.t[:], s[a], s[a].t[:], s[a], s[a].t[:], ALU.mult)
            STT(eng, s[b], s[b].t[:], s[a], s[a].t[:], 2.0, c[a], c[a].t[:], ALU.mult, ALU.mult)
            TS(eng, c[b], c[b].t[:], t, t.t[:], -2.0, 1.0, ALU.mult, ALU.add)
        return c[0], s[0]

    def branch_a(l):
        with ExitStack() as sa_:
            uT = sbt(sa_, "uT", [128, 4, 1024], BF16, nsub=4)
            ya = sbt(sa_, "ya", [128, 4, 1024], BF16, nsub=4)
            bbc = sbt(sa_, "bbc", [128, 2, 512])
            rho = sbt(sa_, "rho", [128, 32])
            g0r = sbt(sa_, "g0r", [128, 32])
            g0i = sbt(sa_, "g0i", [128, 32])
            cs = sbt(sa_, "cs", [128, 2, 512])
            ns5 = sbt(sa_, "ns5", [128, 2, 128])
            s5d = sbt(sa_, "s5d", [128, 4])
            cth0 = None
            DMA("sp", s5d.t[:], D["s5d"][l], W=[s5d])
            DMA("sp", cs.t[:, 0, :], D["cre_s"][l].rearrange("p d t i -> p (d t i)"), W=[cs])
            DMA("sp", cs.t[:, 1, :], D["cim_s"][l].rearrange("p d t i -> p (d t i)"), W=[cs])
            OP("pool", "tensor_scalar", cs.t[:, 1, :], cs.t[:, 1, :], -1.0, None, ALU.mult, R=[cs], W=[cs])

            wu = load_w(D["w_in"][l][:, OFF["u_a"]:OFF["u_a"] + 512], 8, 512)
            for m in range(4):
                pp = P[m % 2]
                project(wu, 8, m * 128, hn_rhs, pp)
                ACT(uT.k[m], uT.t[:, m, :], pp, pp.t[:], AF.Copy)

            cthp = sbt(sa_, "cthp", [128, 32]); sthp = sbt(sa_, "sthp", [128, 32])
            sd = ExitStack()
            lr = sbt(sd, "lr_s", [128, 32]); li = sbt(sd, "li_s", [128, 32]); ls = sbt(sd, "ls_s", [128, 32])
            h0r = sbt(sd, "h0r", [128, 32]); h0i = sbt(sd, "h0i", [128, 32])
            th = sbt(sd, "th_s", [128, 32]); tmp = sbt(sd, "tmp_s", [128, 32])
            for tns, nm in ((lr, "lamre_s"), (li, "lamim_s"), (ls, "lstep_s")):
                DMA("sp", tns.t[:], D[nm][l].rearrange("p d t -> p (d t)"), W=[tns])
            DMA("sp", h0r.t[:], D["s5h0r"][:, l].rearrange("p d t -> p (d t)"), W=[h0r])
            DMA("sp", h0i.t[:], D["s5h0i"][:, l].rearrange("p d t -> p (d t)"), W=[h0i])
            ACT(ls, ls.t[:], ls, ls.t[:], AF.Exp)
            TT("dve", tmp, tmp.t[:], lr, lr.t[:], ls, ls.t[:], ALU.mult)
            ACT(rho, rho.t[:], tmp, tmp.t[:], AF.Exp)
            TT("dve", th, th.t[:], li, li.t[:], ls, ls.t[:], ALU.mult)
            tap("li%d" % l, li.t[:], [128, 32], [li])
            tap("th%d" % l, th.t[:], [128, 32], [th])
            c_, s_ = cos_sin(sd, "dve", th, 32, "s")
            OP("dve", "tensor_copy", cthp.t[:], c_.t[:], R=[c_], W=[cthp])
            OP("dve", "tensor_copy", sthp.t[:], s_.t[:], R=[s_], W=[sthp])
            TT("dve", tmp, tmp.t[:], s_, s_.t[:], h0i, h0i.t[:], ALU.mult)
            TT("dve", g0r, g0r.t[:], c_, c_.t[:], h0r, h0r.t[:], ALU.mult)
            TT("dve", g0r, g0r.t[:], g0r, g0r.t[:], tmp, tmp.t[:], ALU.subtract)
            TT("dve", tmp, tmp.t[:], s_, s_.t[:], h0r, h0r.t[:], ALU.mult)
            TT("dve", g0i, g0i.t[:], c_, c_.t[:], h0i, h0i.t[:], ALU.mult)
            TT("dve", g0i, g0i.t[:], g0i, g0i.t[:], tmp, tmp.t[:], ALU.add)

            names = ["lrc", "lic", "lsc", "brc", "bic", "thc", "mag", "ar1", "ai", "den", "t1", "t2", "fr", "fi"]
            A = {n_: sbt(sd, n_ + "_c", [128, 512]) for n_ in names}
            for tns, nm in ((A["lrc"], "lamre_c"), (A["lic"], "lamim_c"), (A["lsc"], "lstep_c"),
                            (A["brc"], "bre_c"), (A["bic"], "bim_c")):
                DMA("sp", tns.t[:], D[nm][l].rearrange("p d c q -> p (d c q)"), W=[tns])

            def e2(o, a, b, op, eng="dve"):
                TT(eng, A[o], A[o].t[:], A[a], A[a].t[:], A[b], A[b].t[:], op)
            ACT(A["lsc"], A["lsc"].t[:], A["lsc"], A["lsc"].t[:], AF.Exp)
            e2("t1", "lrc", "lsc", ALU.mult)
            ACT(A["mag"], A["mag"].t[:], A["t1"], A["t1"].t[:], AF.Exp)
            e2("thc", "lic", "lsc", ALU.mult)
            cc, ss = cos_sin(sd, "dve", A["thc"], 512, "c")
            TT("dve", A["ai"], A["ai"].t[:], A["mag"], A["mag"].t[:], ss, ss.t[:], ALU.mult)
            TT("dve", A["ar1"], A["ar1"].t[:], A["mag"], A["mag"].t[:], cc, cc.t[:], ALU.mult)
            TS("dve", A["ar1"], A["ar1"].t[:], A["ar1"], A["ar1"].t[:], -1.0, None, ALU.add)
            e2("den", "lrc", "lrc", ALU.mult)
            e2("t1", "lic", "lic", ALU.mult)
            e2("den", "den", "t1", ALU.add)
            OP("dve", "reciprocal", A["den"].t[:], A["den"].t[:], R=[A["den"]], W=[A["den"]])
            e2("t1", "ar1", "lrc", ALU.mult)
            e2("t2", "ai", "lic", ALU.mult)
            e2("fr", "t1", "t2", ALU.add)
            e2("fr", "fr", "den", ALU.mult)
            e2("t1", "ai", "lrc", ALU.mult)
            e2("t2", "ar1", "lic", ALU.mult)
            e2("fi", "t1", "t2", ALU.subtract)
            e2("fi", "fi", "den", ALU.mult)
            e2("t1", "fr", "brc", ALU.mult)
            e2("t2", "fi", "bic", ALU.mult)
            TT("dve", bbc, bbc.t[:, 0, :], A["t1"], A["t1"].t[:], A["t2"], A["t2"].t[:], ALU.subtract)
            e2("t1", "fr", "bic", ALU.mult)
            e2("t2", "fi", "brc", ALU.mult)
            TT("dve", bbc, bbc.t[:, 1, :], A["t1"], A["t1"].t[:], A["t2"], A["t2"].t[:], ALU.add)
            fw.barrier()
            sd.close()
            tap("bbc%d" % l, bbc.t[:], [128, 2, 512], [bbc])
            tap("rho%d" % l, rho.t[:], [128, 32], [rho])
            tap("cth%d" % l, cthp.t[:], [128, 32], [cthp])
            tap("sth%d" % l, sthp.t[:], [128, 32], [sthp])

            Tr32 = sbt(sa_, "Tr32", [128, 8, 256]); Ti32 = sbt(sa_, "Ti32", [128, 8, 256])
            SETS = []
            for i_ in range(2):
                SETS.append(dict(
                    Bbf=sbt(sa_, "Bbf", [128, 2, 4, 2, 128], BF16), CTp=sbt(sa_, "CTp", [128, 2, 4, 2, 128], BF16),
                    Tr=sbt(sa_, "Trb", [128, 8, 256], BF16 if S5_BF16 else F32),
                    Ti=sbt(sa_, "Tib", [128, 8, 256], BF16 if S5_BF16 else F32),
                    Cr=sbt(sa_, "Cr", [128, 9, 8]), Ci=sbt(sa_, "Ci", [128, 9, 8]), ctmp=sbt(sa_, "ctmp", [128, 3, 8]),
                    Kr=sbt(sa_, "Kr", [128, 8]), Ki=sbt(sa_, "Ki", [128, 8]), nKi=sbt(sa_, "nKi", [128, 8]),
                    c255=sbt(sa_, "c255", [128, 8]), s255=sbt(sa_, "s255", [128, 8]), ns255=sbt(sa_, "ns255", [128, 8])))
            HB = [sbt(sa_, "hb%d" % i, [128, 1024], BF16) for i in range(4)]
            carry = sbt(sa_, "carry", [128, 2, 2, 4])
            bbc4 = bbc.t[:].rearrange("p r (d c q) -> p r d c q", d=2, c=4)
            cs4 = cs.t[:].rearrange("p r (d t i) -> p r d t i", d=2, t=16)
            cth3 = cthp.t[:].rearrange("p (d t) -> p d t", d=2)
            sth3 = sthp.t[:].rearrange("p (d t) -> p d t", d=2)
            rho3 = rho.t[:].rearrange("p (d t) -> p d t", d=2)
            ns54 = ns5.t[:].rearrange("p r (d t s) -> p r d t s", d=2, t=16)
            TE = "pool"
            pipe_ = [0]

            def bfv(i):
                if not S5_BF16:
                    return merged.t[:, i, :]
                return merged.t[:, i, :].bitcast(BF16)[:, 0:1024]

            def unpack(ct):
                S_ = SETS[ct % 2]
                return tuple(S_[n_] for n_ in ("Bbf", "CTp", "Tr", "Ti", "Cr", "Ci", "ctmp", "Kr", "Ki", "nKi",
                                               "c255", "s255", "ns255"))

            def build_ct(ct):
                Bbf, CTp, Trb_, Tib_, Cr, Ci, ctmp, Kr, Ki, nKi, c255, s255, ns255 = unpack(ct)
                Tr, Ti = Tr32, Ti32
                for st_ in range(4):
                    for gl in range(2):
                        for ri in range(2):
                            OP(TE, "tensor_scalar", Bbf.t[:, :, st_, ri, gl * 64:(gl + 1) * 64], bbc4[:, ri, :, ct, :],
                               maskB.t[:, st_ * 2 + gl:st_ * 2 + gl + 1], None, ALU.mult, R=[bbc, maskB], W=[Bbf])
                OP(TE, "memset", CTp.t[:], 0.0, W=[CTp])
                for st_ in range(4):
                    for gl in range(2):
                        for ri in range(2):
                            OP(TE, "tensor_copy",
                               CTp.t[gl * 64:(gl + 1) * 64, :, st_, ri, 32 * st_ + gl * 16:32 * st_ + gl * 16 + 16],
                               cs4[gl * 64:(gl + 1) * 64, ri, :, ct * 4 + st_, :], R=[cs], W=[CTp])
                OP(TE, "tensor_copy", Cr.t[:, 0, :].rearrange("p (d s) -> p d s", d=2), cth3[:, :, ct * 4:(ct + 1) * 4],
                   R=[cthp], W=[Cr])
                OP(TE, "tensor_copy", Ci.t[:, 0, :].rearrange("p (d s) -> p d s", d=2), sth3[:, :, ct * 4:(ct + 1) * 4],
                   R=[sthp], W=[Ci])
                OP(TE, "memset", Tr.t[:, :, 0:1], 1.0, W=[Tr])
                OP(TE, "memset", Ti.t[:, :, 0:1], 0.0, W=[Ti])
                tA, tB = SCA[0], SCA[1]
                for k in range(8):
                    n = 1 << k
                    crb = col3(Cr.t[:, k, :], n)
                    cib = col3(Ci.t[:, k, :], n)
                    a3 = tA.t[:, 0:8 * n].rearrange("p (q n) -> p q n", q=8)
                    b3 = tB.t[:, 0:8 * n].rearrange("p (q n) -> p q n", q=8)
                    OP(TE, "tensor_tensor", a3, Tr.t[:, :, 0:n], crb, ALU.mult, R=[Tr, Cr], W=[tA])
                    OP(TE, "tensor_tensor", b3, Ti.t[:, :, 0:n], cib, ALU.mult, R=[Ti, Ci], W=[tB])
                    OP(TE, "tensor_tensor", Tr.t[:, :, n:2 * n], a3, b3, ALU.subtract, R=[tA, tB], W=[Tr])
                    OP(TE, "tensor_tensor", a3, Tr.t[:, :, 0:n], cib, ALU.mult, R=[Tr, Ci], W=[tA])
                    OP(TE, "tensor_tensor", b3, Ti.t[:, :, 0:n], crb, ALU.mult, R=[Ti, Cr], W=[tB])
                    OP(TE, "tensor_tensor", Ti.t[:, :, n:2 * n], a3, b3, ALU.add, R=[tA, tB], W=[Ti])
                    OP(TE, "tensor_tensor", ctmp.t[:, 0, :], Cr.t[:, k, :], Cr.t[:, k, :], ALU.mult, R=[Cr], W=[ctmp])
                    OP(TE, "tensor_tensor", ctmp.t[:, 1, :], Ci.t[:, k, :], Ci.t[:, k, :], ALU.mult, R=[Ci], W=[ctmp])
                    OP(TE, "tensor_tensor", Cr.t[:, k + 1, :], ctmp.t[:, 0, :], ctmp.t[:, 1, :], ALU.subtract,
                       R=[ctmp], W=[Cr])
                    OP(TE, "tensor_tensor", ctmp.t[:, 2, :], Cr.t[:, k, :], Ci.t[:, k, :], ALU.mult, R=[Cr, Ci], W=[ctmp])
                    OP(TE, "tensor_scalar", Ci.t[:, k + 1, :], ctmp.t[:, 2, :], 2.0, None, ALU.mult, R=[ctmp], W=[Ci])
                OP(TE, "tensor_scalar", Kr.t[:], Cr.t[:, 8, :], flag.t[:, 0:1], None, ALU.mult, R=[Cr, flag], W=[Kr])
                OP(TE, "tensor_scalar", Ki.t[:], Ci.t[:, 8, :], flag.t[:, 0:1], None, ALU.mult, R=[Ci, flag], W=[Ki])
                OP(TE, "tensor_scalar", nKi.t[:], Ki.t[:], -1.0, None, ALU.mult, R=[Ki], W=[nKi])
                OP(TE, "tensor_copy", c255.t[:], Tr.t[:, :, 255], R=[Tr], W=[c255])
                OP(TE, "tensor_copy", s255.t[:], Ti.t[:, :, 255], R=[Ti], W=[s255])
                OP(TE, "tensor_scalar", ns255.t[:], s255.t[:], -1.0, None, ALU.mult, R=[s255], W=[ns255])
                OP(TE, "tensor_copy", Trb_.t[:], Tr.t[:], R=[Tr], W=[Trb_])
                OP(TE, "tensor_copy", Tib_.t[:], Ti.t[:], R=[Ti], W=[Tib_])
                if l == 0 and ct == 0:
                    tap("Tr", Tr.t[:], [128, 8, 256], [Tr])
                    tap("Ti", Ti.t[:], [128, 8, 256], [Ti])

            def scan_ct(ct):
                Bbf, CTp, Tr, Ti, Cr, Ci, ctmp, Kr, Ki, nKi, c255, s255, ns255 = unpack(ct)
                pipe = pipe_[0]
                first = True

                def emit_bu(dr, st_, pp_):
                    for ri in range(2):
                        for half in range(2):
                            MM(P[ri], P[ri].t[:, half * 512:(half + 1) * 512], Bbf, Bbf.t[:, dr, st_, ri, :],
                               uT.k[ct], uT.t[:, ct, half * 512:(half + 1) * 512])
                    for ri in range(2):
                        src = v3(P[ri].t[:])
                        if dr == 1:
                            src = src[:, :, ::-1]
                        ACT(merged.k[pp_ * 4 + ri], v3(bfv(pp_ * 4 + ri)), P[ri], src, AF.Copy)

                pairs = [(dr, st_) for dr in range(2) for st_ in range(4)]
                emit_bu(pairs[0][0], pairs[0][1], pipe)
                for ip_, (dr, st_) in enumerate(pairs):
                    if True:
                        q = dr * 4 + st_
                        tp = ct * 4 + st_
                        E = "dve"
                        wk = [merged.k[pipe * 4 + i] for i in range(4)]
                        wa = [bfv(pipe * 4 + i) for i in range(4)]
                        hr, hi = HB[pipe * 2], HB[pipe * 2 + 1]
                        pipe = (pipe + 1) % 2
                        BUr, BUi, TA, TB_ = wk
                        bur, bui, ta, tb = wa
                        if ip_ + 1 < len(pairs):
                            emit_bu(pairs[ip_ + 1][0], pairs[ip_ + 1][1], pipe)
                        cosb = Tr.t[:, q:q + 1, :].broadcast_to([128, 4, 256])
                        sinb = Ti.t[:, q:q + 1, :].broadcast_to([128, 4, 256])
                        OP(E, "tensor_tensor", v3(ta), v3(bur), cosb, ALU.mult, R=[BUr, Tr], W=[TA])
                        OP(E, "tensor_tensor", v3(tb), v3(bui), sinb, ALU.mult, R=[BUi, Ti], W=[TB_])
                        OP(E, "tensor_tensor", ta, ta, tb, ALU.add, R=[TA, TB_], W=[TA])
                        OP(E, "tensor_tensor", v3(tb), v3(bui), cosb, ALU.mult, R=[BUi, Tr], W=[TB_])
                        OP(E, "tensor_tensor", v3(bur), v3(bur), sinb, ALU.mult, R=[BUr, Ti], W=[BUr])
                        OP(E, "tensor_tensor", tb, tb, bur, ALU.subtract, R=[TB_, BUr], W=[TB_])
                        order = [0, 1, 2, 3] if dr == 0 else [3, 2, 1, 0]
                        rcol = rho3[:, dr, tp:tp + 1]
                        rb = rcol.broadcast_to([128, 256])
                        for idx, sg in enumerate(order):
                            sl = slice(sg * 256, (sg + 1) * 256)
                            if idx == 0:
                                ir = g0r.t[:, dr * 16 + tp:dr * 16 + tp + 1]
                                ii = g0i.t[:, dr * 16 + tp:dr * 16 + tp + 1]
                                RI = [g0r, g0i]
                            else:
                                ir = carry.t[:, pipe, 0, idx:idx + 1]
                                ii = carry.t[:, pipe, 1, idx:idx + 1]
                                RI = [carry]
                            OP(E, "tensor_tensor_scan", bur[:, sl], rb, ta[:, sl], ir, ALU.mult, ALU.add,
                               R=[rho, TA] + RI, W=[BUr])
                            OP(E, "tensor_tensor_scan", bui[:, sl], rb, tb[:, sl], ii, ALU.mult, ALU.add,
                               R=[rho, TB_] + RI, W=[BUi])
                            if idx < 3:
                                last = sg * 256 + 255
                                glr = bur[:, last:last + 1]
                                gli = bui[:, last:last + 1]
                                nr = carry.t[:, pipe, 0, idx + 1:idx + 2]
                                ni = carry.t[:, pipe, 1, idx + 1:idx + 2]
                                OP(E, "tensor_scalar", nr, glr, Kr.t[:, q:q + 1], None, ALU.mult, R=[BUr, Kr], W=[carry])
                                OP(E, "scalar_tensor_tensor", nr, gli, nKi.t[:, q:q + 1], nr, ALU.mult, ALU.add,
                                   R=[BUi, nKi, carry], W=[carry])
                                OP(E, "tensor_scalar", ni, gli, Kr.t[:, q:q + 1], None, ALU.mult, R=[BUi, Kr], W=[carry])
                                OP(E, "scalar_tensor_tensor", ni, glr, Ki.t[:, q:q + 1], ni, ALU.mult, ALU.add,
                                   R=[BUr, Ki, carry], W=[carry])
                        g255r = v3(bur)[:, :, 255]
                        g255i = v3(bui)[:, :, 255]
                        fr_ = ns54[:, 0, dr, tp, :]
                        fi_ = ns54[:, 1, dr, tp, :]
                        OP(E, "tensor_scalar", fr_, g255r, c255.t[:, q:q + 1], None, ALU.mult, R=[BUr, c255], W=[ns5])
                        OP(E, "scalar_tensor_tensor", fr_, g255i, ns255.t[:, q:q + 1], fr_, ALU.mult, ALU.add,
                           R=[BUi, ns255, ns5], W=[ns5])
                        OP(E, "tensor_scalar", fi_, g255i, c255.t[:, q:q + 1], None, ALU.mult, R=[BUi, c255], W=[ns5])
                        OP(E, "scalar_tensor_tensor", fi_, g255r, s255.t[:, q:q + 1], fi_, ALU.mult, ALU.add,
                           R=[BUr, s255, ns5], W=[ns5])
                        ohr = v3(hr.t[:]) if dr == 0 else v3(hr.t[:])[:, :, ::-1]
                        ohi = v3(hi.t[:]) if dr == 0 else v3(hi.t[:])[:, :, ::-1]
                        OP(E, "tensor_tensor", v3(ta), v3(bur), cosb, ALU.mult, R=[BUr, Tr], W=[TA])
                        OP(E, "tensor_tensor", v3(tb), v3(bui), sinb, ALU.mult, R=[BUi, Ti], W=[TB_])
                        OP(E, "tensor_tensor", ohr, v3(ta), v3(tb), ALU.subtract, R=[TA, TB_], W=[hr])
                        OP(E, "tensor_tensor", v3(ta), v3(bur), sinb, ALU.mult, R=[BUr, Ti], W=[TA])
                        OP(E, "tensor_tensor", v3(tb), v3(bui), cosb, ALU.mult, R=[BUi, Tr], W=[TB_])
                        OP(E, "tensor_tensor", ohi, v3(ta), v3(tb), ALU.add, R=[TA, TB_], W=[hi])
                        if l == 0 and ct == 0 and st_ == 0:
                            tap("hr%d" % dr, hr.t[:], [128, 1024], [hr])
                        for ri, hh in ((0, hr), (1, hi)):
                            lastmm = (dr == 1 and st_ == 3 and ri == 1)
                            for half in range(2):
                                MM(P[3], P[3].t[:, half * 512:(half + 1) * 512], CTp, CTp.t[:, dr, st_, ri, :],
                                   hh, hh.t[:, half * 512:(half + 1) * 512], start=(first and ri == 0), stop=lastmm)
                        first = False
                sa = scA()
                STT("dve", sa, sa.t[:], uT.k[ct], uT.t[:, ct, :], s5d.t[:, ct:ct + 1], P[3], P[3].t[:],
                    ALU.mult, ALU.add, RS=[s5d])
                if ct == 0:
                    tap("ypre%d" % l, sa.t[:], [128, 1024], [sa])
                ACT(ya.k[ct], ya.t[:, ct, :], sa, sa.t[:], AF.Gelu)
                pipe_[0] = pipe

            build_ct(0)
            for ct in range(4):
                if ct + 1 < 4:
                    build_ct(ct + 1)
                scan_ct(ct)
            DMA("sp", O["ns5r"][:, l].rearrange("p d t s -> p (d t s)"), ns5.t[:, 0, :], R=[ns5])
            DMA("sp", O["ns5i"][:, l].rearrange("p d t s -> p (d t s)"), ns5.t[:, 1, :], R=[ns5])
            wgl = load_w(D["wglu"][l], 4, 512)
            wz = load_w(D["w_in"][l][:, OFF["z_a"]:OFF["z_a"] + 512], 8, 512)
            for m in range(4):
                pg_, pz_ = (P[0], P[1]) if m % 2 == 0 else (P[2], P[3])
                project(wgl, 4, m * 128, k_rhs(ya), pg_)
                project(wz, 8, m * 128, hn_rhs, pz_)
                g_ = scB()
                ACT(g_, g_.t[:], pg_, pg_.t[:], AF.Sigmoid)
                z_ = scB()
                ACT(z_, z_.t[:], pz_, pz_.t[:], AF.Silu)
                OP("dve", "tensor_tensor", g_.t[:], g_.t[:], ya.t[:, m, :], ALU.mult, R=[g_, ya.k[m]], W=[g_])
                OP("dve", "tensor_tensor", uT.t[:, m, :], g_.t[:], z_.t[:], ALU.mult, R=[g_, z_], W=[uT.k[m]])
            tap("outa%d" % l, uT.t[:, 0, :], [128, 1024], [uT.k[0]])
            merge_branch(l, 0, uT)
            fw.barrier()

    AX = mybir.AxisListType.X

    def branch_b(l):
        with ExitStack() as sb_:
            kT = sbt(sb_, "kTall", [128, 4, 1536], BF16, nsub=4)
            qT = sbt(sb_, "qTall", [128, 4, 1024], BF16, nsub=4)
            Vt = sbt(sb_, "Vt", [128, 12, 4, 129], BF16, nsub=12)
            szb = sbt(sb_, "szb", [128, 4, 1024], BF16, nsub=4)
            outb = sbt(sb_, "outb", [128, 4, 1024], BF16, nsub=4)
            ropec = sbt(sb_, "ropec", [128, 1024]); ropes = sbt(sb_, "ropes", [128, 1024])
            maskb = sbt(sb_, "maskb", [128, 48])
            dalam = sbt(sb_, "dalam", [128, 256]); gn = sbt(sb_, "gn", [128, 128])
            lamv = sbt(sb_, "lamv", [128, 4]); lt = sbt(sb_, "lt", [128, 128])
            EX = [sbt(sb_, "ex%d" % i, [128, 512], BF16) for i in range(3)]
            ACC = [sbt(sb_, "accS%d" % i, [128, 8, 160]) for i in range(2)]
            OB4 = [sbt(sb_, "ob4%d" % i, [128, 4, 128]) for i in range(2)]
            SS4 = [sbt(sb_, "ss4%d" % i, [128, 16]) for i in range(2)]
            lt4 = sbt(sb_, "lt4", [128, 4, 128])
            DMA("sp", ropec.t[:], D["ropec"], W=[ropec])
            DMA("sp", ropes.t[:], D["ropes"], W=[ropes])
            DMA("sp", maskb.t[:], D["maskb"], W=[maskb])
            DMA("sp", dalam.t[:], D["dalam"][l], W=[dalam])
            DMA("sp", gn.t[:], D["dang"][l], W=[gn])
            TT("dve", lt, lt.t[:, 0:64], dalam, dalam.t[:, 0:64], dalam, dalam.t[:, 64:128], ALU.mult)
            TT("dve", lt, lt.t[:, 64:128], dalam, dalam.t[:, 128:192], dalam, dalam.t[:, 192:256], ALU.mult)
            OP("dve", "tensor_reduce", lamv.t[:, 0:1], lt.t[:, 0:64], AX, ALU.add, R=[lt], W=[lamv])
            OP("dve", "tensor_reduce", lamv.t[:, 1:2], lt.t[:, 64:128], AX, ALU.add, R=[lt], W=[lamv])
            ACT(lamv, lamv.t[:, 0:2], lamv, lamv.t[:, 0:2], AF.Exp)
            TT("dve", lamv, lamv.t[:, 2:3], lamv, lamv.t[:, 1:2], lamv, lamv.t[:, 0:1], ALU.subtract)
            TS("dve", lamv, lamv.t[:, 2:3], lamv, lamv.t[:, 2:3], -LAM_INIT[l], None, ALU.add)
            OP("pool", "memset", Vt.t[:, :, :, 128:129], 1.0, W=Vt.k)
            lnc = sbt(sb_, "lnc", [128, 1])
            OP("pool", "memset", lnc.t[:], float(np.log(1.0 - LAM_INIT[l])), W=[lnc])

            for which, dst, c0 in (("q", qT, 0), ("k", kT, 512)):
                wa_ = load_w(D["w_in"][l][:, OFF["q_b"] + c0:OFF["q_b"] + c0 + 512], 8, 512)
                wp_ = load_w(D["w_qkp"][l][:, c0:c0 + 512], 8, 512)
                for h in range(4):
                    pa_, pb_ = (P[0], P[1]) if h % 2 == 0 else (P[2], P[3])
                    project(wa_, 8, h * 128, hn_rhs, pa_)
                    project(wp_, 8, h * 128, hn_rhs, pb_)
                    s1, s2 = scA(), scA()
                    TT("dve", s1, s1.t[:], pa_, pa_.t[:], ropec, ropec.t[:], ALU.mult)
                    TT("dve", s2, s2.t[:], pb_, pb_.t[:], ropes, ropes.t[:], ALU.mult)
                    if which == "q":
                        TT("dve", dst.k[h], dst.t[:, h, :], s1, s1.t[:], s2, s2.t[:], ALU.add)
                    else:
                        TT("dve", s1, s1.t[:], s1, s1.t[:], s2, s2.t[:], ALU.add)
                        DMA("sp", O["newk"][:, l, h, :], s1.t[:], R=[s1])
                        ACT(dst.k[h], dst.t[:, h, 0:1024], s1, s1.t[:], AF.Copy)
                        DMA("pool", dst.t[:, h, 1024:1536], D["kctx"][:, l, h, :], W=[dst.k[h]])
                        if h == 0:
                            tap("krot%d" % l, s1.t[:], [128, 1024], [s1])
            wv_ = load_w(D["w_in"][l][:, OFF["v_b"]:OFF["v_b"] + 512], 8, 512)
            wz_ = load_w(D["w_in"][l][:, OFF["z_b"]:OFF["z_b"] + 512], 8, 512)
            for h in range(4):
                project(wv_, 8, h * 128, hn_rhs, P[2])
                s1 = scA()
                ACT(s1, s1.t[:], P[2], P[2].t[:], AF.Copy)
                DMA("sp", O["newv"][:, l, h, :], s1.t[:], R=[s1])
                for tt in range(8):
                    OP("pe", "transpose", P[3].t[:, tt * 128:(tt + 1) * 128], s1.t[:, tt * 128:(tt + 1) * 128], ident.t[:],
                       R=[s1, ident], W=[P[3]])
                OP("act", "activation", Vt.t[:, 0:8, h, 0:128], P[3].t[:].rearrange("p (a e) -> p a e", a=8), AF.Copy,
                   R=[P[3]], W=Vt.k[0:8])
                DMA("pool", Vt.t[:, 8:12, h, 0:128], D["vctx"][:, l, :, h * 128:(h + 1) * 128], W=Vt.k[8:12])
                project(wz_, 8, h * 128, hn_rhs, P[h % 2])
                ACT(szb.k[h], szb.t[:, h, :], P[h % 2], P[h % 2].t[:], AF.Silu)
            PH = [TV(P[i // 2].t, "PH%d" % i) for i in range(4)]
            PT3 = TV(P[3].t, "PT3")
            for sub_, whole_ in [(PH[0], P[0]), (PH[1], P[0]), (PH[2], P[1]), (PH[3], P[1]), (PT3, P[3])]:
                sub_.b.writer = whole_.b.writer
                sub_.b.readers = dict(whole_.b.readers)
            blk_ = 0
            sci_ = 0
            exi = 0
            scale = 64 ** -0.5
            for h in range(4):
                for qb in range(2):
                    def acc(qt, m):
                        s = qt * 2 + m
                        bank, pos = s // 3, s % 3
                        pt = P[2] if bank < 2 else P[3]
                        off = (bank % 2) * 512 + pos * 160
                        return pt, pt.t[:, off:off + 129]
                    tiles = [(m, kt) for m in range(2) for kt in range(12)]

                    def emit_score(m, kt):
                        nonlocal sci_
                        ph = PH[sci_ % 4]
                        sc_ap = ph.t[:, (sci_ % 2) * 512:(sci_ % 2) * 512 + 512]
                        sci_ += 1
                        MM(ph, sc_ap, kT.k[h], kT.t[m * 64:(m + 1) * 64, h, kt * 128:(kt + 1) * 128],
                           qT.k[h], qT.t[m * 64:(m + 1) * 64, h, qb * 512:(qb + 1) * 512])
                        return ph, sc_ap

                    nxt = emit_score(*tiles[0])
                    for it_, (m, kt) in enumerate(tiles):
                        ph, sc_ap = nxt
                        if it_ + 1 < len(tiles):
                            nxt = emit_score(*tiles[it_ + 1])
                        ex = EX[exi % 3]
                        exi += 1
                        for sgl in range(2):
                            seg = qb * 2 + sgl
                            OP("act", "activation", ex.t[:, sgl * 256:(sgl + 1) * 256],
                               sc_ap[:, sgl * 256:(sgl + 1) * 256], AF.Exp,
                               bias=maskb.t[:, kt * 4 + seg:kt * 4 + seg + 1], scale=scale,
                               R=[ph, maskb], W=[ex])
                        for qt in range(4):
                            pt, aap = acc(qt, m)
                            first_in_bank = (qt * 2 + m) in (0, 4, 6, 1, 3, 7)
                            fw.op("pe", lambda e, aap=aap, ex=ex, qt=qt, kt=kt, fb=first_in_bank: e.matmul(
                                aap, lhsT=ex.t[:, qt * 128:(qt + 1) * 128], rhs=Vt.t[:, kt, h, :],
                                start=(kt == 0 and fb), stop=(kt == 11), skip_group_check=True),
                                reads=[ex.b, Vt.k[kt].b], writes=[pt.b])
                    accS = ACC[blk_ % 2]
                    o4 = OB4[blk_ % 2]
                    ss = SS4[blk_ % 2]
                    blk_ += 1
                    OP("dve", "tensor_copy", accS.t[:, 0:3, :], P[2].t[:, 0:480].rearrange("p (a e) -> p a e", a=3),
                       R=[P[2]], W=[accS])
                    OP("dve", "tensor_copy", accS.t[:, 3:6, :], P[2].t[:, 512:992].rearrange("p (a e) -> p a e", a=3),
                       R=[P[2]], W=[accS])
                    OP("dve", "tensor_copy", accS.t[:, 6:8, :], P[3].t[:, 0:320].rearrange("p (a e) -> p a e", a=2),
                       R=[P[3]], W=[accS])
                    OP("dve", "reciprocal", ss.t[:, 0:8], accS.t[:, :, 128], R=[accS], W=[ss])
                    TS("dve", ss, ss.t[:, 1:8:2], ss, ss.t[:, 1:8:2], lamv.t[:, 2:3], None, ALU.mult, RS=[lamv])
                    TT("dve", o4, o4.t[:], accS, accS.t[:, 0:8:2, 0:128], ss, col3(ss.t[:, 0:8:2], 128), ALU.mult)
                    TT("dve", lt4, lt4.t[:], accS, accS.t[:, 1:8:2, 0:128], ss, col3(ss.t[:, 1:8:2], 128), ALU.mult)
                    TT("dve", o4, o4.t[:], o4, o4.t[:], lt4, lt4.t[:], ALU.add)
                    TT("dve", lt4, lt4.t[:], o4, o4.t[:], o4, o4.t[:], ALU.mult)
                    OP("dve", "tensor_reduce", ss.t[:, 8:12], lt4.t[:], AX, ALU.add, R=[lt4], W=[ss])
                    TS("dve", ss, ss.t[:, 8:12], ss, ss.t[:, 8:12], 1.0 / 128, EPS, ALU.mult, ALU.add)
                    ACT(ss, ss.t[:, 8:12], ss, ss.t[:, 8:12], AF.Ln)
                    ACT(ss, ss.t[:, 12:16], ss, ss.t[:, 8:12], AF.Exp, RS=[lnc], scale=-0.5, bias=lnc.t[:, 0:1])
                    TT("dve", o4, o4.t[:], o4, o4.t[:], ss, col3(ss.t[:, 12:16], 128), ALU.mult)
                    TT("dve", o4, o4.t[:], o4, o4.t[:], gn,
                       gn.t[:].rearrange("p (o e) -> p o e", o=1).broadcast_to([128, 4, 128]), ALU.mult)
                    for qt in range(4):
                        OP("pe", "transpose", P[3].t[:, 512 + qt * 128:512 + (qt + 1) * 128], o4.t[:, qt, :], ident.t[:],
                           R=[o4, ident], W=[PT3])
                    TT("dve", outb.k[h], outb.t[:, h, qb * 512:(qb + 1) * 512], PT3, P[3].t[:, 512:1024],
                       szb.k[h], szb.t[:, h, qb * 512:(qb + 1) * 512], ALU.mult)
            tap("outb%d" % l, outb.t[:, 0, :], [128, 1024], [outb.k[0]])
            for whole_, subs_ in [(P[0], PH[0:2]), (P[1], PH[2:4]), (P[3], [PT3])]:
                for sub_ in subs_:
                    for k_, v_ in sub_.b.readers.items():
                        if whole_.b.readers.get(k_, 0) < v_:
                            whole_.b.readers[k_] = v_
                    w_ = sub_.b.writer
                    if w_ is not None:
                        if whole_.b.writer is None or (whole_.b.writer[0] == w_[0] and whole_.b.writer[1] < w_[1]):
                            whole_.b.writer = w_
                        elif whole_.b.writer[0] != w_[0]:
                            if whole_.b.readers.get(w_[0], 0) < w_[1]:
                                whole_.b.readers[w_[0]] = w_[1]
            merge_branch(l, 1, outb)
            fw.barrier()

    def branch_c(l):
        with ExitStack() as sc_:
            outc = sbt(sc_, "outc", [128, 4, 1024], BF16, nsub=4)
            cw = sbt(sc_, "cw", [128, 60])
            beta_t = sbt(sc_, "beta_t", [128, 8, 8]); g_t = sbt(sc_, "g_t", [128, 8, 8])
            alog = sbt(sc_, "alog", [128, 8]); dtb = sbt(sc_, "dtb", [128, 8]); gnc = sbt(sc_, "gnc", [128, 128])
            xp = sbt(sc_, "xp", [128, 4, 260])
            qf = sbt(sc_, "qf", [128, 1024]); kf = sbt(sc_, "kf", [128, 1024])
            k_tok = sbt(sc_, "k_tok", [128, 1024]); v_tok = sbt(sc_, "v_tok", [128, 1024])
            o_acc = sbt(sc_, "o_acc", [128, 1024])
            PTm = sbt(sc_, "PTm", [128, 1024]); RT = sbt(sc_, "RTm", [128, 1024])
            Dm = [sbt(sc_, "Dm%d" % d, [128, 1024]) for d in range(2)]
            DT = [sbt(sc_, "DT%d" % d, [128, 1024]) for d in range(2)]
            U = [sbt(sc_, "U%d" % d, [128, 1024]) for d in range(2)]
            WT = [sbt(sc_, "WT%d" % d, [128, 1024]) for d in range(2)]
            sml = [{n_: sbt(sc_, "%s%d" % (n_, d), [128, 8]) for n_ in ("gcol", "egc", "nbeta", "bge", "kds")}
                   for d in range(2)]
            egl = [sbt(sc_, "egl%d" % d, [128, 16]) for d in range(2)]
            S = [[sbt(sc_, "S%d%d" % (d, i), [128, 128]) for i in range(2)] for d in range(2)]
            VN = [[sbt(sc_, "VN%d%d" % (d, i), [128, 128]) for i in range(1)] * 2 for d in range(2)]
            otmp = [[sbt(sc_, "otmp%d%d" % (d, i), [128, 128]) for i in range(1)] * 2 for d in range(2)]
            rs8 = sbt(sc_, "rs8", [128, 8])
            gcr, Pm = SCA[0], rstd
            X1, X2 = SCA[1], SCA[2]
            epsc = sbt(sc_, "epsc", [128, 3])
            OP("pool", "memset", epsc.t[:, 0:1], EPS, W=[epsc])
            OP("pool", "memset", epsc.t[:, 1:2], 0.0, W=[epsc])
            OP("pool", "memset", epsc.t[:, 2:3], float(np.log(128 ** -0.5)), W=[epsc])
            DMA("sp", cw.t[:], D["dnconv"][l].rearrange("p j k -> p (j k)"), W=[cw])
            DMA("sp", alog.t[:], D["dnalog"][l], W=[alog])
            DMA("sp", dtb.t[:], D["dndt"][l], W=[dtb])
            DMA("sp", gnc.t[:], D["dnng"][l], W=[gnc])

            def r3(ap):
                return ap.rearrange("p (a e) -> p a e", a=8)

            def bc_tt(ap128):
                return ap128.rearrange("p (o e) -> p o e", o=1).broadcast_to([128, 8, 128])

            wba = load_w(D["w_in"][l][:, OFF["beta"]:OFF["beta"] + 16], 8, 16)
            for tt in range(8):
                for kt in range(8):
                    MM(P[0], P[0].t[:, tt * 16:(tt + 1) * 16], hnT.k[kt], hnT.t[:, kt, tt * 128:(tt + 1) * 128],
                       wba, wba.t[:, kt, 0:16], start=(kt == 0), stop=(kt == 7))
            pb = P[0].t[:, 0:128].rearrange("p (a c) -> p a c", a=8)
            ACT(beta_t, beta_t.t[:], P[0], pb[:, :, 0:8], AF.Sigmoid)
            dtbB = dtb.t[:].rearrange("p (o c) -> p o c", o=1).broadcast_to([128, 8, 8])
            TT("dve", g_t, g_t.t[:], P[0], pb[:, :, 8:16], dtb, dtbB, ALU.add)
            ACT(g_t, g_t.t[:], g_t, g_t.t[:], AF.Exp)
            ACT(g_t, g_t.t[:], g_t, g_t.t[:], AF.Ln, bias=1.0)
            ACT(alog, alog.t[:], alog, alog.t[:], AF.Exp)
            TS("dve", alog, alog.t[:], alog, alog.t[:], -1.0, None, ALU.mult)
            alB = alog.t[:].rearrange("p (o c) -> p o c", o=1).broadcast_to([128, 8, 8])
            TT("dve", g_t, g_t.t[:], g_t, g_t.t[:], alog, alB, ALU.mult)
            tap("g_t%d" % l, g_t.t[:], [128, 8, 8], [g_t])
            tap("beta_t%d" % l, beta_t.t[:], [128, 8, 8], [beta_t])
            if stop_after == "c0":
                halt[0] = True
                return


            QF = [qf, sbt(sc_, "qf1", [128, 1024])]

            def conv_phase(h):
                th = []
                wh = wb_next()
                for i_, nm_ in enumerate(("q_c", "k_c", "v_c", "z_c")):
                    c0_ = OFF[nm_] + h * 128
                    load_into(wh, i_ * 128, D["w_in"][l][:, c0_:c0_ + 128], 8, 128)
                qdst = QF[h % 2]
                tmp, l2t = X1, gcr

                def conv_tile(m, j):
                    pp = P[2]
                    a_ = X2
                    th.append(lambda: project(wh, 8, m * 128, hn_rhs, pp))
                    th.append(lambda: ACT(xp, xp.t[:, :, 2:258], pp, v3(pp.t[:]), AF.Copy))

                    def halo():
                        OP("dve", "memset", xp.t[:, 0, 0:2], 0.0, W=[xp])
                        OP("dve", "memset", xp.t[:, 3, 258:260], 0.0, W=[xp])
                        OP("dve", "tensor_scalar", xp.t[:, 1:4, 0:2], xp.t[:, 0:3, 256:258], flag.t[:, 0:1], None,
                           ALU.mult, R=[xp, flag], W=[xp])
                        OP("dve", "tensor_scalar", xp.t[:, 0:3, 258:260], xp.t[:, 1:4, 2:4], flag.t[:, 0:1], None,
                           ALU.mult, R=[xp, flag], W=[xp])
                    th.append(halo)
                    th.append(lambda: TS("dve", a_, v3(a_.t[:]), xp, xp.t[:, :, 0:256], cw.t[:, j * 5:j * 5 + 1], None,
                                         ALU.mult, RS=[cw]))
                    for k in range(1, 5):
                        th.append(lambda k=k: STT("dve", a_, v3(a_.t[:]), xp, xp.t[:, :, k:k + 256],
                                                  cw.t[:, j * 5 + k:j * 5 + k + 1], a_, v3(a_.t[:]), ALU.mult, ALU.add,
                                                  RS=[cw]))
                    th.append(lambda: ACT(tmp, tmp.t[:], a_, a_.t[:], AF.Silu))

                def l2n(dstT, scl):
                    def f1():
                        sb_ = scB()
                        ACT(sb_, sb_.t[:], tmp, tmp.t[:], AF.Square)
                        for half in range(2):
                            MM(P[3], P[3].t[:, half * 512:(half + 1) * 512], onesb, onesb.t[:], sb_,
                               sb_.t[:, half * 512:(half + 1) * 512])
                    th.append(f1)
                    th.append(lambda: ACT(l2t, l2t.t[:], P[3], P[3].t[:], AF.Ln, RS=[epsc], bias=epsc.t[:, 0:1]))
                    th.append(lambda: ACT(l2t, l2t.t[:], l2t, l2t.t[:], AF.Exp, RS=[epsc], scale=-0.5,
                                          bias=epsc.t[:, 1:2] if scl == 1.0 else epsc.t[:, 2:3]))
                    th.append(lambda: TT("dve", dstT, dstT.t[:], tmp, tmp.t[:], l2t, l2t.t[:], ALU.mult))

                conv_tile(0, h)
                l2n(qdst, 128 ** -0.5)
                conv_tile(1, 4 + h)
                l2n(kf, 1.0)

                def ktr():
                    for tt in range(8):
                        OP("pe", "transpose", P[3].t[:, tt * 128:(tt + 1) * 128], kf.t[:, tt * 128:(tt + 1) * 128],
                           ident.t[:], R=[kf, ident], W=[P[3]])
                    ACT(k_tok, k_tok.t[:], P[3], P[3].t[:], AF.Copy)
                th.append(ktr)
                conv_tile(2, 8 + h)

                def vtr():
                    for tt in range(8):
                        OP("pe", "transpose", P[3].t[:, tt * 128:(tt + 1) * 128], tmp.t[:, tt * 128:(tt + 1) * 128],
                           ident.t[:], R=[tmp, ident], W=[P[3]])
                    ACT(v_tok, v_tok.t[:], P[3], P[3].t[:], AF.Copy)
                th.append(vtr)
                return th, wh

            pending, wh_next = conv_phase(0)
            for h in range(4):
                for f_ in pending:
                    f_()
                pending = []
                wh = wh_next
                qf = QF[h % 2]
                if h == 0:
                    tap("qf%d" % l, qf.t[:], [128, 1024], [qf])
                    tap("kf%d" % l, kf.t[:], [128, 1024], [kf])
                    tap("vtok%d" % l, v_tok.t[:], [128, 1024], [v_tok])

                for dr in range(2):
                    cb = dr * 4 + h
                    sm_ = sml[dr]
                    gcol, egc, nbeta, bge, kds = (sm_[n_] for n_ in ("gcol", "egc", "nbeta", "bge", "kds"))
                    TRI = cm.t[:, dr, :]
                    MASK = cm.t[:, 1 - dr, :]
                    for tt in range(8):
                        OP("act", "activation", X1.t[:, tt * 128:(tt + 1) * 128], cm.t[:, 2, :], AF.Copy,
                           scale=g_t.t[:, tt, cb:cb + 1], R=[cm, g_t], W=[X1])
                    for tt in range(8):
                        MM(P[0], P[0].t[:, tt * 128:(tt + 1) * 128], X1, X1.t[:, tt * 128:(tt + 1) * 128], cm, TRI)
                        MM(P[1], P[1].t[:, tt:tt + 1], cm, TRI, g_t, g_t.t[:, tt, cb:cb + 1])
                        MM(P[2], P[2].t[:, tt * 128:(tt + 1) * 128], kf, kf.t[:, tt * 128:(tt + 1) * 128],
                           kf, kf.t[:, tt * 128:(tt + 1) * 128])
                    ACT(gcr, gcr.t[:], P[0], P[0].t[:], AF.Copy)
                    OP("dve", "tensor_copy", gcol.t[:], P[1].t[:, 0:8], R=[P[1]], W=[gcol])
                    ACT(egc, egc.t[:], gcol, gcol.t[:], AF.Exp)
                    TS("dve", nbeta, nbeta.t[:], beta_t, beta_t.t[:, :, cb], -1.0, None, ALU.mult)
                    TT("dve", bge, bge.t[:], beta_t, beta_t.t[:, :, cb], egc, egc.t[:], ALU.mult)
                    if stop_after == "c2a0":
                        halt[0] = True
                        return
                    D_ = Dm[dr]
                    TT("dve", D_, r3(D_.t[:]), gcol, col3(gcol.t[:], 128), gcr, r3(gcr.t[:]), ALU.subtract)
                    TS("dve", D_, D_.t[:], D_, D_.t[:], 0.0, None, ALU.min)
                    ACT(D_, D_.t[:], D_, D_.t[:], AF.Exp)
                    TT("dve", D_, r3(D_.t[:]), D_, r3(D_.t[:]), cm, bc_tt(MASK), ALU.mult)
                    TT("dve", X2, r3(X2.t[:]), D_, r3(D_.t[:]), ident, bc_tt(ident.t[:]), ALU.subtract)
                    TT("dve", Pm, r3(Pm.t[:]), P[2], r3(P[2].t[:]), nbeta, col3(nbeta.t[:], 128), ALU.mult)
                    TT("dve", Pm, Pm.t[:], Pm, Pm.t[:], X2, X2.t[:], ALU.mult)
                    if stop_after == "c2a1":
                        halt[0] = True
                        return
                    for tt in range(8):
                        OP("pe", "transpose", P[3].t[:, tt * 128:(tt + 1) * 128], Pm.t[:, tt * 128:(tt + 1) * 128],
                           ident.t[:], R=[Pm, ident], W=[P[3]])
                    ACT(PTm, PTm.t[:], P[3], P[3].t[:], AF.Copy)
                    if stop_after == "c2a2":
                        halt[0] = True
                        return
                    TT("dve", RT, r3(RT.t[:]), PTm, r3(PTm.t[:]), ident, bc_tt(ident.t[:]), ALU.add)
                    if stop_after == "c2a3":
                        halt[0] = True
                        return
                    for tt in range(8):
                        OP("pe", "transpose", P[0].t[:, tt * 128:(tt + 1) * 128], D_.t[:, tt * 128:(tt + 1) * 128],
                           ident.t[:], R=[D_, ident], W=[P[0]])
                    ACT(DT[dr], DT[dr].t[:], P[0], P[0].t[:], AF.Copy)
                    if stop_after == "c2a":
                        halt[0] = True
                        return
                    for k in range(1, 6):
                        for tt in range(8):
                            sl = slice(tt * 128, (tt + 1) * 128)
                            MM(P[0], P[0].t[:, sl], PTm, PTm.t[:, sl], Pm, Pm.t[:, sl])
                            if k < 5:
                                MM(P[1], P[1].t[:, sl], Pm, Pm.t[:, sl], PTm, PTm.t[:, sl])
                        ACT(Pm, Pm.t[:], P[0], P[0].t[:], AF.Copy)
                        if k < 5:
                            OP("dve", "tensor_copy", PTm.t[:], P[1].t[:], R=[P[1]], W=[PTm])
                        for tt in range(8):
                            sl = slice(tt * 128, (tt + 1) * 128)
                            MM(P[2], P[2].t[:, sl], Pm, Pm.t[:, sl], RT, RT.t[:, sl])
                        TT("dve", RT, RT.t[:], RT, RT.t[:], P[2], P[2].t[:], ALU.add)
                    if stop_after == "c2b":
                        halt[0] = True
                        return
                    TT("dve", X1, r3(X1.t[:]), v_tok, r3(v_tok.t[:]), beta_t, col3(beta_t.t[:, :, cb], 128), ALU.mult)
                    TT("dve", X2, r3(X2.t[:]), k_tok, r3(k_tok.t[:]), bge, col3(bge.t[:], 128), ALU.mult)
                    for tt in range(8):
                        sl = slice(tt * 128, (tt + 1) * 128)
                        MM(P[0], P[0].t[:, sl], RT, RT.t[:, sl], X1, X1.t[:, sl])
                        MM(P[1], P[1].t[:, sl], X2, X2.t[:, sl], RT, RT.t[:, sl])
                        MM(P[2], P[2].t[:, sl], kf, kf.t[:, sl], qf, qf.t[:, sl])
                    ACT(U[dr], U[dr].t[:], P[0], P[0].t[:], AF.Copy)
                    ACT(WT[dr], WT[dr].t[:], P[1], P[1].t[:], AF.Copy)
                    TT("dve", DT[dr], DT[dr].t[:], P[2], P[2].t[:], DT[dr], DT[dr].t[:], ALU.mult)
                    g3 = r3(gcr.t[:])
                    for hp in (0, 64):
                        lc = hp + 63 if dr == 0 else hp
                        TT("dve", kds, kds.t[hp:hp + 64, :], gcr, g3[hp:hp + 64, :, lc], gcol, gcol.t[hp:hp + 64, :],
                           ALU.subtract)
                    ACT(kds, kds.t[:], kds, kds.t[:], AF.Exp)
                    lc0 = 63 if dr == 0 else 0
                    ACT(egl[dr], egl[dr].t[:].rearrange("p (a c) -> p a c", a=8), gcr, g3[:, :, lc0::64], AF.Exp)
                    TT("dve", D_, r3(D_.t[:]), k_tok, r3(k_tok.t[:]), kds, col3(kds.t[:], 128), ALU.mult)
                    if h == 0 and dr == 0:
                        tap("U%d" % l, U[0].t[:], [128, 1024], [U[0]])
                        tap("RT%d" % l, RT.t[:], [128, 1024], [RT])
                if stop_after == "c2":
                    halt[0] = True
                    return
                OP("pool", "memset", o_acc.t[:], 0.0, W=[o_acc])
                if h + 1 < 4:
                    pending, wh_next = conv_phase(h + 1)
                per_step = (len(pending) + 15) // 16
                cur = [0, 0]
                for dr in range(2):
                    DMA("sp", S[dr][0].t[:], D["dns0"][:, l, dr, h, :], W=[S[dr][0]])
                for step in range(16):
                    for dr in range(2):
                        c = step if dr == 0 else 15 - step
                        tt, hp, ci = c // 2, (c % 2) * 64, c % 2
                        sm_ = sml[dr]
                        egc = sm_["egc"]
                        Sc = S[dr][cur[dr]]
                        Sn = S[dr][1 - cur[dr]]
                        boundary = (dr == 0 and c % 4 == 0 and c > 0) or (dr == 1 and c % 4 == 3 and c < 15)
                        if boundary:
                            seq = c // 4 - 1 if dr == 0 else c // 4 + 1
                            DMA("sp", O["ndn"][:, l, dr, h, seq, :], Sc.t[:], R=[Sc])
                            TS("dve", Sn, Sn.t[:], Sc, Sc.t[:], flag.t[:, 0:1], None, ALU.mult, RS=[flag])
                            cur[dr] = 1 - cur[dr]
                            Sc, Sn = Sn, Sc
                        pd = P[dr]
                        pc = (step % 2) * 512
                        vn = VN[dr][step % 2]
                        ot = otmp[dr][step % 2]
                        MM(pd, pd.t[hp:hp + 64, pc:pc + 128], WT[dr], WT[dr].t[:, tt * 128 + hp:tt * 128 + hp + 64], Sc, Sc.t[:])
                        MM(pd, pd.t[hp:hp + 64, pc + 128:pc + 256], qf, qf.t[:, c * 64:(c + 1) * 64], Sc, Sc.t[:])
                        TT("dve", vn, vn.t[hp:hp + 64, :], U[dr], U[dr].t[hp:hp + 64, tt * 128:(tt + 1) * 128],
                           pd, pd.t[hp:hp + 64, pc:pc + 128], ALU.subtract)
                        MM(pd, pd.t[hp:hp + 64, pc + 256:pc + 384],
                           DT[dr], DT[dr].t[hp:hp + 64, tt * 128 + hp:tt * 128 + hp + 64], vn, vn.t[hp:hp + 64, :])
                        MM(pd, pd.t[:, pc + 384:pc + 512], Dm[dr], Dm[dr].t[hp:hp + 64, tt * 128:(tt + 1) * 128],
                           vn, vn.t[hp:hp + 64, :])
                        TS("dve", ot, ot.t[hp:hp + 64, :], pd, pd.t[hp:hp + 64, pc + 128:pc + 256],
                           egc.t[hp:hp + 64, tt:tt + 1], None, ALU.mult, RS=[egc])
                        TT("dve", ot, ot.t[hp:hp + 64, :], ot, ot.t[hp:hp + 64, :], pd, pd.t[hp:hp + 64, pc + 256:pc + 384],
                           ALU.add)
                        TT("pool", o_acc, o_acc.t[hp:hp + 64, tt * 128:(tt + 1) * 128],
                           o_acc, o_acc.t[hp:hp + 64, tt * 128:(tt + 1) * 128], ot, ot.t[hp:hp + 64, :], ALU.add)
                        STT("dve", Sn, Sn.t[:], Sc, Sc.t[:], egl[dr].t[:, tt * 2 + ci:tt * 2 + ci + 1],
                            pd, pd.t[:, pc + 384:pc + 512], ALU.mult, ALU.add, RS=[egl[dr]])
                        cur[dr] = 1 - cur[dr]
                    for _ in range(per_step):
                        if pending:
                            pending.pop(0)()
                for dr in range(2):
                    seq = 3 if dr == 0 else 0
                    Sc = S[dr][cur[dr]]
                    DMA("sp", O["ndn"][:, l, dr, h, seq, :], Sc.t[:], R=[Sc])
                if h == 0:
                    tap("oacc%d" % l, o_acc.t[:], [128, 1024], [o_acc])
                if stop_after == "c3":
                    halt[0] = True
                    return
                TT("dve", X1, X1.t[:], o_acc, o_acc.t[:], o_acc, o_acc.t[:], ALU.mult)
                OP("dve", "tensor_reduce", rs8.t[:], r3(X1.t[:]), AX, ALU.add, R=[X1], W=[rs8])
                TS("dve", rs8, rs8.t[:], rs8, rs8.t[:], 1.0 / 128, EPS, ALU.mult, ALU.add)
                OP("dve", "reciprocal", rs8.t[:], rs8.t[:], R=[rs8], W=[rs8])
                ACT(rs8, rs8.t[:], rs8, rs8.t[:], AF.Sqrt)
                TT("dve", X1, r3(X1.t[:]), o_acc, r3(o_acc.t[:]), rs8, col3(rs8.t[:], 128), ALU.mult)
                TT("dve", X1, r3(X1.t[:]), X1, r3(X1.t[:]), gnc, bc_tt(gnc.t[:]), ALU.mult)
                for tt in range(8):
                    OP("pe", "transpose", P[2].t[:, tt * 128:(tt + 1) * 128], X1.t[:, tt * 128:(tt + 1) * 128], ident.t[:],
                       R=[X1, ident], W=[P[2]])
                project(wh, 8, 3 * 128, hn_rhs, P[3])
                z_ = scB()
                ACT(z_, z_.t[:], P[3], P[3].t[:], AF.Silu)
                TT("dve", outc.k[h], outc.t[:, h, :], P[2], P[2].t[:], z_, z_.t[:], ALU.mult)
            tap("outc%d" % l, outc.t[:, 0, :], [128, 1024], [outc.k[0]])
            merge_branch(l, 2, outc)
            fw.barrier()

    for l in range(nlayers):
        norm_modulate(l)
        if stop_after == "norm":
            break
        if "a" not in skip:
            branch_a(l)
        if stop_after == "a":
            break
        if "b" not in skip:
            branch_b(l)
        if stop_after == "b":
            break
        branch_c(l)
        if stop_after == "c" or halt[0]:
            break
        out_proj_residual(l)
    rms_stats()
    fng = sbt(top, "fng", [128, 8])
    DMA("sp", fng.t[:], D["fng"], W=[fng])
    for kt in range(8):
        sa = scA()
        OP("dve", "scalar_tensor_tensor", sa.t[:], xT.t[:, kt, :], fng.t[:, kt:kt + 1], rstd.t[:],
           ALU.mult, ALU.mult, R=[xT.k[kt], fng, rstd], W=[sa])
        DMA("sp", O["yT"][:, kt, :], sa.t[:], R=[sa])
    fw.finish()
    n_inst = fw.n_inst
    top.close()
    return nc, list(DBG.keys()), n_inst


_CACHE = {}


def kernel(**inputs):
    maps = prep_inputs(inputs)
    if "nc" not in _CACHE:
        _CACHE["nc"] = build()[0]
    nc = _CACHE["nc"]
    res = run_bass_kernel_spmd(nc, maps, core_ids=list(range(8)))
    return assemble(res.results)
```
